# Optimizing a Trainium2 kernel written in Bass

```python
import jax
import jax.numpy as jnp
from jax import lax
import numpy as np

D_MODEL = 1024
BATCH = 4
SEQ = 4096
DEPTH = 4
DEC_BATCH = 128
DEC_SEQ = 4
PAST_LEN = 8192
PAGE_SIZE = 128

MIX_WIDTH = D_MODEL
HEAD_DIM = 64
ATTN_WIDTH = D_MODEL // 2
N_HEADS = ATTN_WIDTH // HEAD_DIM
N_KV_HEADS = 2
GQA_GROUP = N_HEADS // N_KV_HEADS
KV_WIDTH = N_KV_HEADS * HEAD_DIM
WINDOW = 128
BLOCK_Q = 128
RWKV_WIDTH = D_MODEL // 4
RWKV_HEAD = 64
RWKV_HEADS = RWKV_WIDTH // RWKV_HEAD
LORA_W = 64
LORA_A = 64
LORA_G = 128
RWKV_PROJ = 3 * RWKV_WIDTH + LORA_W + LORA_A + LORA_G
LRU_WIDTH = D_MODEL // 4
LRU_BLOCKS = 4
LRU_BLOCK = LRU_WIDTH // LRU_BLOCKS
CONV_W = 4
LRU_C = 8.0
IN_COLS = ATTN_WIDTH + 2 * KV_WIDTH + RWKV_PROJ + 2 * LRU_WIDTH
IN_SPLITS = (ATTN_WIDTH, ATTN_WIDTH + KV_WIDTH, ATTN_WIDTH + 2 * KV_WIDTH,
             ATTN_WIDTH + 2 * KV_WIDTH + RWKV_PROJ, ATTN_WIDTH + 2 * KV_WIDTH + RWKV_PROJ + LRU_WIDTH)
RWKV_SPLITS = (RWKV_WIDTH, 2 * RWKV_WIDTH, 3 * RWKV_WIDTH, 3 * RWKV_WIDTH + LORA_W,
               3 * RWKV_WIDTH + LORA_W + LORA_A)
D_FF = -(-(8 * D_MODEL) // (3 * 256)) * 256
PLE_DIM = 256
RMS_EPS = 1e-6
GN_EPS = 64e-5

kernel_name = 'hymba_swa_rwkv7_rglru_step'


def rmsnorm(x, g):
    xf = x.astype(jnp.float32)
    y = xf * lax.rsqrt(jnp.mean(xf * xf, -1, keepdims=True) + RMS_EPS)
    return (y * g.astype(jnp.float32)).astype(x.dtype)


def _sink_probs(s, sink, mask):
    s = jnp.where(mask, s, -jnp.inf)
    m = jnp.maximum(jnp.max(s, -1, keepdims=True), sink)
    e = jnp.exp(s - m)
    return e / (jnp.sum(e, -1, keepdims=True) + jnp.exp(sink - m))


def swa_prompt(q, k, v, sinks):
    b, t = q.shape[:2]
    nb = t // BLOCK_Q
    qb = q.reshape(b, nb, BLOCK_Q, N_KV_HEADS, GQA_GROUP, HEAD_DIM)
    kb = k.reshape(b, nb, BLOCK_Q, N_KV_HEADS, HEAD_DIM)
    vb = v.reshape(b, nb, BLOCK_Q, N_KV_HEADS, HEAD_DIM)
    pad = ((0, 0), (1, 0), (0, 0), (0, 0), (0, 0))
    k_ext = jnp.concatenate([jnp.pad(kb[:, :-1], pad), kb], axis=2)
    v_ext = jnp.concatenate([jnp.pad(vb[:, :-1], pad), vb], axis=2)
    s = jnp.einsum('bnqhgd,bnkhd->bhgnqk', qb, k_ext,
                   preferred_element_type=jnp.float32) * (HEAD_DIM ** -0.5)
    qi = jnp.arange(BLOCK_Q)[:, None] + BLOCK_Q
    kj = jnp.arange(2 * BLOCK_Q)[None, :]
    d = qi - kj
    blk = jnp.arange(nb)[:, None, None]
    mask = (d >= 0) & (d <= WINDOW) & ((blk > 0) | (kj >= BLOCK_Q))
    sink = sinks.astype(jnp.float32).reshape(N_KV_HEADS, GQA_GROUP)[None, :, :, None, None, None]
    p = _sink_probs(s, sink, mask)
    o = jnp.einsum('bhgnqk,bnkhd->bnqhgd', p.astype(v.dtype), v_ext)
    return o.reshape(b, t, ATTN_WIDTH)


def swa_sample(q, k, v, ck, cv, sinks):
    b, tn = q.shape[:2]
    k_all = jnp.concatenate([ck.astype(k.dtype), k], axis=1)
    v_all = jnp.concatenate([cv.astype(v.dtype), v], axis=1)
    qg = q.reshape(b, tn, N_KV_HEADS, GQA_GROUP, HEAD_DIM)
    s = jnp.einsum('bqhgd,bkhd->bhgqk', qg, k_all,
                   preferred_element_type=jnp.float32) * (HEAD_DIM ** -0.5)
    d = (jnp.arange(tn)[:, None] + WINDOW) - jnp.arange(WINDOW + tn)[None, :]
    mask = (d >= 0) & (d <= WINDOW)
    sink = sinks.astype(jnp.float32).reshape(N_KV_HEADS, GQA_GROUP)[None, :, :, None, None]
    p = _sink_probs(s, sink, mask)
    o = jnp.einsum('bhgqk,bkhd->bqhgd', p.astype(v.dtype), v_all)
    return o.reshape(b, tn, ATTN_WIDTH), k_all[:, tn:], v_all[:, tn:]


def rwkv7_mix(proj, shift0, wkv0, lw):
    b, t = proj.shape[:2]
    f32 = jnp.float32
    prev = jnp.concatenate([shift0[:, None].astype(proj.dtype), proj[:, :-1]], axis=1)
    xs = proj + (prev - proj) * lw['rwkv_mu']
    r, k, v, xw, xa, xg = jnp.split(xs, RWKV_SPLITS, axis=-1)
    w_log = -jax.nn.softplus(-(lw['rwkv_w0'] + jnp.tanh(xw) @ lw['rwkv_w_up']).astype(f32)) - 0.5
    decay = jnp.exp(-jnp.exp(w_log))
    a = jax.nn.sigmoid((lw['rwkv_a0'] + xa @ lw['rwkv_a_up']).astype(f32))
    g = (jax.nn.sigmoid(xg) @ lw['rwkv_g_up']).astype(f32)

    def heads(z):
        return z.astype(f32).reshape(b, t, RWKV_HEADS, RWKV_HEAD)

    kk = heads(k * lw['rwkv_k_k'])
    kk = kk * lax.rsqrt(jnp.maximum(jnp.sum(kk * kk, -1, keepdims=True), 1e-24))
    k2 = k.astype(f32) * (1.0 + (a - 1.0) * lw['rwkv_k_a'].astype(f32))
    r_h, k_h, v_h, w_h, a_h = heads(r), heads(k2), heads(v), heads(decay), heads(a)

    def step(S, inp):
        r_t, w_t, k_t, v_t, kk_t, b_t = inp
        S = (S * w_t[:, :, None, :]
             - jnp.einsum('bhvk,bhk->bhv', S, kk_t)[..., None] * b_t[:, :, None, :]
             + v_t[..., None] * k_t[:, :, None, :])
        return S, jnp.einsum('bhvk,bhk->bhv', S, r_t)

    seq = tuple(jnp.moveaxis(z, 1, 0) for z in (r_h, w_h, k_h, v_h, kk, kk * a_h))
    wkv, y = lax.scan(step, wkv0.astype(f32), seq)
    y = jnp.moveaxis(y, 0, 1)
    mean = jnp.mean(y, -1, keepdims=True)
    var = jnp.mean(jnp.square(y - mean), -1, keepdims=True)
    y = (y - mean) * lax.rsqrt(var + GN_EPS)
    y = y.reshape(b, t, RWKV_WIDTH) * lw['rwkv_ln_w'] + lw['rwkv_ln_b']
    bonus = jnp.sum(r_h * k_h * lw['rwkv_r_k'].astype(f32), -1, keepdims=True) * v_h
    y = y + bonus.reshape(b, t, RWKV_WIDTH)
    return (y * g).astype(proj.dtype), proj[:, -1], wkv


def rglru_mix(xb, gb, conv0, h0, lw):
    b, t = xb.shape[:2]
    f32 = jnp.float32
    ext = jnp.concatenate([conv0.astype(xb.dtype), xb], axis=1)
    xc = lw['lru_conv_b'] + sum(ext[:, j:j + t] * lw['lru_conv_w'][j] for j in range(CONV_W))
    xh = xc.reshape(b, t, LRU_BLOCKS, LRU_BLOCK)
    r = jax.nn.sigmoid((jnp.einsum('bthi,hij->bthj', xh, lw['lru_w_a']).reshape(b, t, LRU_WIDTH)
                        + lw['lru_b_a']).astype(f32))
    i = jax.nn.sigmoid((jnp.einsum('bthi,hij->bthj', xh, lw['lru_w_i']).reshape(b, t, LRU_WIDTH)
                        + lw['lru_b_i']).astype(f32))
    log_a = LRU_C * r * jax.nn.log_sigmoid(lw['lru_L'].astype(f32))
    a = jnp.exp(log_a)
    u = jnp.sqrt(-jnp.expm1(2.0 * log_a)) * (i * xc.astype(f32))
    u = u.at[:, 0].add(a[:, 0] * h0.astype(f32))

    def combine(c1, c2):
        a1, b1 = c1
        a2, b2 = c2
        return a1 * a2, a2 * b1 + b2

    _, h = lax.associative_scan(combine, (a, u), axis=1)
    out = (h * jax.nn.gelu(gb.astype(f32))).astype(xb.dtype)
    return out, ext[:, -(CONV_W - 1):], h[:, -1]


def _layer(x, p, st, lw, sample):
    b, t = x.shape[:2]
    ck, cv, sh0, wkv0, conv0, h0 = st
    h = rmsnorm(x, lw['norm_mix_pre'])
    proj = h @ lw['w_in'] + lw['b_in']
    q, k, v, pb, xb, gb = jnp.split(proj, IN_SPLITS, axis=-1)
    q = q.reshape(b, t, N_HEADS, HEAD_DIM)
    k = k.reshape(b, t, N_KV_HEADS, HEAD_DIM)
    v = v.reshape(b, t, N_KV_HEADS, HEAD_DIM)
    if sample:
        o_a, nk, nv = swa_sample(q, k, v, ck, cv, lw['attn_sinks'])
    else:
        o_a = swa_prompt(q, k, v, lw['attn_sinks'])
        nk, nv = k[:, -WINDOW:], v[:, -WINDOW:]
    o_b, nsh, nwkv = rwkv7_mix(pb, sh0, wkv0, lw)
    o_c, nconv, nh = rglru_mix(xb, gb, conv0, h0, lw)
    mix = jnp.concatenate([o_a, o_b, o_c], axis=-1) @ lw['w_out'] + lw['b_out']
    x = x + rmsnorm(mix, lw['norm_mix_post'])
    f = rmsnorm(x, lw['norm_ffn_pre'])
    f = (jax.nn.silu(f @ lw['ffn_w_gate']) * (f @ lw['ffn_w_up'])) @ lw['ffn_w_down']
    x = x + rmsnorm(f, lw['norm_ffn_post'])
    x = x + jax.nn.sigmoid(x @ lw['ple_gate_w']) * (p @ lw['ple_w'])
    return x, (nk, nv, nsh, nwkv, nconv, nh)


def setup_inputs(seed: int = 0) -> dict:
    key = jax.random.key(seed)
    ks = jax.random.split(key, 64)
    cnt = iter(range(64))

    def nrm(shape, scale):
        return jax.random.normal(ks[next(cnt)], shape, jnp.float32) * scale

    def unif(shape, lo, hi):
        return jax.random.uniform(ks[next(cnt)], shape, jnp.float32, lo, hi)

    def gain(shape):
        return 1.0 + nrm(shape, 0.05)

    L = DEPTH
    inp = {}
    inp['x_prompt'] = nrm((BATCH, SEQ, D_MODEL), 1.0)
    inp['x_sample'] = nrm((DEC_BATCH, DEC_SEQ, D_MODEL), 1.0)
    inp['cache_k'] = nrm((L, DEC_BATCH, WINDOW, N_KV_HEADS, HEAD_DIM), 1.0)
    inp['cache_v'] = nrm((L, DEC_BATCH, WINDOW, N_KV_HEADS, HEAD_DIM), 1.0)
    inp['state_shift'] = nrm((L, DEC_BATCH, RWKV_PROJ), 1.0)
    inp['state_wkv'] = nrm((L, DEC_BATCH, RWKV_HEADS, RWKV_HEAD, RWKV_HEAD), 0.3)
    inp['state_conv'] = nrm((L, DEC_BATCH, CONV_W - 1, LRU_WIDTH), 1.0)
    inp['state_lru'] = nrm((L, DEC_BATCH, LRU_WIDTH), 0.5)
    inp['p_prompt'] = nrm((L, BATCH, SEQ, PLE_DIM), 1.0)
    inp['p_sample'] = nrm((L, DEC_BATCH, DEC_SEQ, PLE_DIM), 1.0)
    inp['norm_mix_pre'] = gain((L, D_MODEL))
    inp['norm_mix_post'] = gain((L, D_MODEL))
    inp['norm_ffn_pre'] = gain((L, D_MODEL))
    inp['norm_ffn_post'] = gain((L, D_MODEL))
    inp['w_in'] = nrm((L, D_MODEL, IN_COLS), D_MODEL ** -0.5)
    inp['b_in'] = nrm((L, IN_COLS), 0.02)
    inp['attn_sinks'] = nrm((L, N_HEADS), 1.0)
    inp['rwkv_mu'] = unif((L, RWKV_PROJ), 0.0, 1.0)
    inp['rwkv_w0'] = nrm((L, RWKV_WIDTH), 1.0)
    inp['rwkv_w_up'] = nrm((L, LORA_W, RWKV_WIDTH), 0.1)
    inp['rwkv_a0'] = nrm((L, RWKV_WIDTH), 0.5)
    inp['rwkv_a_up'] = nrm((L, LORA_A, RWKV_WIDTH), 0.1)
    inp['rwkv_g_up'] = nrm((L, LORA_G, RWKV_WIDTH), LORA_G ** -0.5)
    inp['rwkv_k_k'] = 0.85 + nrm((L, RWKV_WIDTH), 0.05)
    inp['rwkv_k_a'] = gain((L, RWKV_WIDTH))
    inp['rwkv_r_k'] = nrm((L, RWKV_HEADS, RWKV_HEAD), 0.1)
    inp['rwkv_ln_w'] = gain((L, RWKV_WIDTH))
    inp['rwkv_ln_b'] = nrm((L, RWKV_WIDTH), 0.02)
    inp['lru_conv_w'] = nrm((L, CONV_W, LRU_WIDTH), CONV_W ** -0.5)
    inp['lru_conv_b'] = nrm((L, LRU_WIDTH), 0.02)
    inp['lru_w_a'] = nrm((L, LRU_BLOCKS, LRU_BLOCK, LRU_BLOCK), LRU_BLOCK ** -0.5)
    inp['lru_b_a'] = nrm((L, LRU_WIDTH), 0.02)
    inp['lru_w_i'] = nrm((L, LRU_BLOCKS, LRU_BLOCK, LRU_BLOCK), LRU_BLOCK ** -0.5)
    inp['lru_b_i'] = nrm((L, LRU_WIDTH), 0.02)
    inp['lru_L'] = unif((L, LRU_WIDTH), 4.3, 9.0)
    inp['w_out'] = nrm((L, MIX_WIDTH, D_MODEL), MIX_WIDTH ** -0.5)
    inp['b_out'] = nrm((L, D_MODEL), 0.02)
    inp['ffn_w_gate'] = nrm((L, D_MODEL, D_FF), D_MODEL ** -0.5)
    inp['ffn_w_up'] = nrm((L, D_MODEL, D_FF), D_MODEL ** -0.5)
    inp['ffn_w_down'] = nrm((L, D_FF, D_MODEL), D_FF ** -0.5)
    inp['ple_w'] = nrm((L, PLE_DIM, D_MODEL), PLE_DIM ** -0.5)
    inp['ple_gate_w'] = nrm((L, D_MODEL, D_MODEL), D_MODEL ** -0.5)
    return inp


def reference(x_prompt, x_sample, cache_k, cache_v, state_shift, state_wkv, state_conv, state_lru,
              p_prompt, p_sample, norm_mix_pre, norm_mix_post, norm_ffn_pre, norm_ffn_post,
              w_in, b_in, attn_sinks, rwkv_mu, rwkv_w0, rwkv_w_up, rwkv_a0, rwkv_a_up, rwkv_g_up,
              rwkv_k_k, rwkv_k_a, rwkv_r_k, rwkv_ln_w, rwkv_ln_b, lru_conv_w, lru_conv_b,
              lru_w_a, lru_b_a, lru_w_i, lru_b_i, lru_L, w_out, b_out, ffn_w_gate, ffn_w_up,
              ffn_w_down, ple_w, ple_gate_w):
    f32 = jnp.float32
    bp = x_prompt.shape[0]
    xp, xs = x_prompt, x_sample
    new_p, new_s = [], []
    for i in range(DEPTH):
        lw = dict(norm_mix_pre=norm_mix_pre[i], norm_mix_post=norm_mix_post[i],
                  norm_ffn_pre=norm_ffn_pre[i], norm_ffn_post=norm_ffn_post[i],
                  w_in=w_in[i], b_in=b_in[i], attn_sinks=attn_sinks[i],
                  rwkv_mu=rwkv_mu[i], rwkv_w0=rwkv_w0[i], rwkv_w_up=rwkv_w_up[i],
                  rwkv_a0=rwkv_a0[i], rwkv_a_up=rwkv_a_up[i], rwkv_g_up=rwkv_g_up[i],
                  rwkv_k_k=rwkv_k_k[i], rwkv_k_a=rwkv_k_a[i], rwkv_r_k=rwkv_r_k[i],
                  rwkv_ln_w=rwkv_ln_w[i], rwkv_ln_b=rwkv_ln_b[i],
                  lru_conv_w=lru_conv_w[i], lru_conv_b=lru_conv_b[i],
                  lru_w_a=lru_w_a[i], lru_b_a=lru_b_a[i], lru_w_i=lru_w_i[i], lru_b_i=lru_b_i[i],
                  lru_L=lru_L[i], w_out=w_out[i], b_out=b_out[i],
                  ffn_w_gate=ffn_w_gate[i], ffn_w_up=ffn_w_up[i], ffn_w_down=ffn_w_down[i],
                  ple_w=ple_w[i], ple_gate_w=ple_gate_w[i])
        st_p = (None, None,
                jnp.zeros((bp, RWKV_PROJ), x_prompt.dtype),
                jnp.zeros((bp, RWKV_HEADS, RWKV_HEAD, RWKV_HEAD), f32),
                jnp.zeros((bp, CONV_W - 1, LRU_WIDTH), x_prompt.dtype),
                jnp.zeros((bp, LRU_WIDTH), f32))
        xp, sp = _layer(xp, p_prompt[i], st_p, lw, False)
        st_s = (cache_k[i], cache_v[i], state_shift[i], state_wkv[i], state_conv[i], state_lru[i])
        xs, ss = _layer(xs, p_sample[i], st_s, lw, True)
        new_p.append(sp)
        new_s.append(ss)

    def stk(lst, j):
        return jnp.stack([s[j] for s in lst], axis=0)

    return (xp, xs,
            stk(new_p, 0), stk(new_p, 1), stk(new_p, 2), stk(new_p, 3), stk(new_p, 4), stk(new_p, 5),
            stk(new_s, 0), stk(new_s, 1), stk(new_s, 2), stk(new_s, 3), stk(new_s, 4), stk(new_s, 5))
```

```python
import math
import numpy as np
from contextlib import ExitStack
import concourse.bass as bass
import concourse.mybir as mybir
from concourse.bass_utils import run_bass_kernel_spmd


F32 = mybir.dt.float32
BF16 = mybir.dt.bfloat16
AF = mybir.ActivationFunctionType
ALU = mybir.AluOpType
AX = mybir.AxisListType

_DTSIZE = {F32: 4, BF16: 2, mybir.dt.int32: 4, mybir.dt.uint32: 4}

SAME_ENGINE_SYNC = True
SEM_ROLL = 30000


class View:
    __slots__ = ("tile", "ap", "p0", "p1", "b0", "b1")

    def __init__(self, tile, ap, p0, p1, b0, b1):
        self.tile = tile
        self.ap = ap
        self.p0, self.p1, self.b0, self.b1 = p0, p1, b0, b1

    def with_ap(self, ap):
        return View(self.tile, ap, self.p0, self.p1, self.b0, self.b1)

    def bitcast(self, dt):
        return View(self.tile, self.ap.bitcast(dt), self.p0, self.p1, self.b0, self.b1)

    def __getitem__(self, idx):
        return View(self.tile, self.ap[idx], self.p0, self.p1, self.b0, self.b1)


class Tile:
    _n = 0

    def __init__(self, handle, name, shape, dtype, space):
        self.h = handle
        self.name = name
        self.shape = list(shape)
        self.dtype = dtype
        self.space = space
        self.esz = _DTSIZE[dtype]
        self.id = Tile._n
        Tile._n += 1
        st = [1] * len(shape)
        for i in range(len(shape) - 2, 0, -1):
            st[i] = st[i + 1] * shape[i + 1]
        self.strides = st
        self.w = []
        self.r = []

    def __getitem__(self, idx):
        if not isinstance(idx, tuple):
            idx = (idx,)
        idx = list(idx) + [slice(None)] * (len(self.shape) - len(idx))
        lo = 0
        hi = 0
        p0, p1 = 0, self.shape[0]
        for d, (ix, n) in enumerate(zip(idx, self.shape)):
            if isinstance(ix, int):
                if ix < 0:
                    ix += n
                s, e, stp = ix, ix + 1, 1
            else:
                s, e, stp = ix.indices(n)
            assert 0 <= s < e <= n, (self.name, idx, self.shape)
            last = s + ((e - 1 - s) // stp) * stp
            if d == 0:
                p0, p1 = s, last + 1
            else:
                lo += s * self.strides[d]
                hi += last * self.strides[d]
        b0, b1 = lo * self.esz, (hi + 1) * self.esz
        if self.space == "psum":
            p0, p1 = 0, 128
            b0 = b0 // 2048 * 2048
            b1 = (b1 + 2047) // 2048 * 2048
        return View(self, self.h[tuple(idx)], p0, p1, b0, b1)

    def full(self):
        return self[tuple(slice(None) for _ in self.shape)]


class VTile:
    def __init__(self, arena, off_bytes, shape, dtype, name=""):
        self.arena = arena
        self.name = name
        self.shape = list(shape)
        self.dtype = dtype
        self.esz = _DTSIZE[dtype]
        self.off = off_bytes
        n = 1
        for d in shape[1:]:
            n *= d
        self.nbytes = n * self.esz
        assert off_bytes % 4 == 0
        n4 = (self.nbytes + 3) // 4
        base = arena.h[0:shape[0], off_bytes // 4: off_bytes // 4 + n4]
        if dtype != arena.dtype:
            base = base.bitcast(dtype)
        if self.nbytes % 4 != 0:
            base = base[:, 0:n]
        if len(shape) > 2:
            names = [f"d{i}" for i in range(1, len(shape))]
            kw = {nm: shape[i + 1] for i, nm in enumerate(names)}
            base = base.rearrange("p (" + " ".join(names) + ") -> p " + " ".join(names), **kw)
        self.base = base
        st = [1] * len(shape)
        for i in range(len(shape) - 2, 0, -1):
            st[i] = st[i + 1] * shape[i + 1]
        self.strides = st

    def __getitem__(self, idx):
        if not isinstance(idx, tuple):
            idx = (idx,)
        idx = list(idx) + [slice(None)] * (len(self.shape) - len(idx))
        lo = 0
        hi = 0
        p0, p1 = 0, self.shape[0]
        for d, (ix, n) in enumerate(zip(idx, self.shape)):
            if isinstance(ix, int):
                if ix < 0:
                    ix += n
                s, e, stp = ix, ix + 1, 1
            else:
                s, e, stp = ix.indices(n)
            assert 0 <= s < e <= n, (self.name, idx, self.shape)
            last = s + ((e - 1 - s) // stp) * stp
            if d == 0:
                p0, p1 = s, last + 1
            else:
                lo += s * self.strides[d]
                hi += last * self.strides[d]
        return View(self.arena, self.base[tuple(idx)], p0, p1, self.off + lo * self.esz, self.off + (hi + 1) * self.esz)

    def full(self):
        return self[tuple(slice(None) for _ in self.shape)]


class Arena:
    def __init__(self, tile):
        self.tile = tile
        self.top = 0
        self.cap = tile.shape[1] * tile.esz
        self.peak = 0

    def alloc(self, name, shape, dtype):
        n = 1
        for d in shape[1:]:
            n *= d
        nb = (n * _DTSIZE[dtype] + 31) // 32 * 32
        assert self.top + nb <= self.cap, f"arena overflow {name} {self.top}+{nb}>{self.cap}"
        v = VTile(self.tile, self.top, shape, dtype, name)
        self.top += nb
        self.peak = max(self.peak, self.top)
        return v

    def mark(self):
        return self.top

    def release(self, m):
        self.top = m


def _ov(a, b):
    return a[0] < b[1] and b[0] < a[1] and a[2] < b[3] and b[2] < a[3]


def _cov(a, b):
    return a[0] <= b[0] and a[1] >= b[1] and a[2] <= b[2] and a[3] >= b[3]


class Op:
    __slots__ = ("eng", "fn", "deps", "signal", "chan", "idx", "cnt", "semk", "name", "isdma")

    def __init__(self, eng, fn, chan, name):
        self.eng = eng
        self.fn = fn
        self.deps = set()
        self.signal = False
        self.chan = chan
        self.cnt = None
        self.semk = None
        self.name = name
        self.isdma = chan is not None


class Prog:
    ENGS = ("pe", "act", "dve", "pool", "sp")

    def __init__(self, nc):
        self.nc = nc
        self.ops = []
        self.stack = ExitStack()
        self.tiles = []
        self.chan_count = {}

    def sbuf(self, name, shape, dtype):
        h = self.stack.enter_context(self.nc.sbuf_tensor(name, list(shape), dtype))
        t = Tile(h, name, shape, dtype, "sbuf")
        self.tiles.append(t)
        return t

    def psum(self, name, shape, dtype):
        h = self.stack.enter_context(self.nc.psum_tensor(name, list(shape), dtype))
        t = Tile(h, name, shape, dtype, "psum")
        self.tiles.append(t)
        return t

    def add(self, eng, fn, reads=(), writes=(), chan=None, name=""):
        op = Op(eng, fn, chan, name)
        op.idx = len(self.ops)
        self.ops.append(op)
        isdma = chan is not None
        for v in reads:
            if v is None:
                continue
            t = v.tile
            reg = (v.p0, v.p1, v.b0, v.b1)
            for w in t.w:
                if _ov(w, reg):
                    op.deps.add(w[4])
            if not isdma:
                t.r = [r for r in t.r if not (r[5] == eng and _cov(reg, r))]
            t.r.append((v.p0, v.p1, v.b0, v.b1, op.idx, None if isdma else eng))
        for v in writes:
            if v is None:
                continue
            t = v.tile
            reg = (v.p0, v.p1, v.b0, v.b1)
            for w in t.w:
                if _ov(w, reg):
                    op.deps.add(w[4])
            for r in t.r:
                if _ov(r, reg) and r[4] != op.idx:
                    op.deps.add(r[4])
            t.w = [w for w in t.w if not _cov(reg, w)]
            t.r = [r for r in t.r if not _cov(reg, r) or r[4] == op.idx]
            t.w.append((v.p0, v.p1, v.b0, v.b1, op.idx))
        op.deps.discard(op.idx)
        return op

    def dma(self, q, out, in_, chan, reads=(), writes=(), **kw):
        oa = out.ap if isinstance(out, View) else out
        ia = in_.ap if isinstance(in_, View) else in_
        rd = list(reads) + ([in_] if isinstance(in_, View) else [])
        wr = list(writes) + ([out] if isinstance(out, View) else [])
        return self.add(q, lambda e: e.dma_start(out=oa, in_=ia, **kw), rd, wr, chan=chan, name="dma")

    def mm(self, out, lhsT, rhs, start=True, stop=True, **kw):
        return self.add(
            "pe",
            lambda e: e.matmul(out.ap, lhsT.ap, rhs.ap, start=start, stop=stop, **kw),
            [lhsT, rhs] + ([] if start else [out]),
            [out],
            name="mm",
        )

    def transpose(self, out, in_, ident):
        return self.add("pe", lambda e: e.transpose(out.ap, in_.ap, ident.ap), [in_, ident], [out], name="tr")

    def act(self, out, in_, func, bias=None, scale=None, accum=None, eng="act"):
        kw = {}
        rd = [in_]
        if bias is not None:
            if isinstance(bias, View):
                kw["bias"] = bias.ap
                rd.append(bias)
            else:
                kw["bias"] = bias
        if scale is not None:
            if isinstance(scale, View):
                kw["scale"] = scale.ap
                rd.append(scale)
            else:
                kw["scale"] = scale
        wr = [out]
        if accum is not None:
            kw["accum_out"] = accum.ap
            wr.append(accum)
        return self.add(eng, lambda e: e.activation(out.ap, in_.ap, func, **kw), rd, wr, name="act")

    def tt(self, out, in0, in1, op, eng="dve"):
        return self.add(eng, lambda e: e.tensor_tensor(out.ap, in0.ap, in1.ap, op), [in0, in1], [out], name="tt")

    def ts(self, out, in0, s1, op0, s2=None, op1=None, eng="dve", accum=None):
        rd = [in0]
        a1 = s1
        if isinstance(s1, View):
            a1 = s1.ap
            rd.append(s1)
        a2 = s2
        if isinstance(s2, View):
            a2 = s2.ap
            rd.append(s2)
        kw = {}
        wr = [out]
        if accum is not None:
            kw["accum_out"] = accum.ap
            wr.append(accum)
        if op1 is None:
            return self.add(eng, lambda e: e.tensor_scalar(out.ap, in0.ap, a1, None, op0, **kw), rd, wr, name="ts")
        return self.add(eng, lambda e: e.tensor_scalar(out.ap, in0.ap, a1, a2, op0, op1, **kw), rd, wr, name="ts")

    def stt(self, out, in0, scalar, in1, op0, op1, eng="dve"):
        rd = [in0, in1]
        a = scalar
        if isinstance(scalar, View):
            a = scalar.ap
            rd.append(scalar)
        return self.add(eng, lambda e: e.scalar_tensor_tensor(out.ap, in0.ap, a, in1.ap, op0, op1), rd, [out], name="stt")

    def copy(self, out, in_, eng="dve"):
        if eng == "act":
            return self.act(out, in_, AF.Copy)
        return self.add(eng, lambda e: e.tensor_copy(out.ap, in_.ap), [in_], [out], name="copy")

    def memset(self, out, val, eng="dve"):
        return self.add(eng, lambda e: e.memset(out.ap, val), [], [out], name="memset")

    DMA_POOL = {"sp": 24, "pool": 12, "act": 4}

    def emit(self, final_chans=()):
        nc = self.nc
        ops = self.ops
        for op in ops:
            for d in op.deps:
                ops[d].signal = True
        per = {e: [] for e in self.ENGS}
        for op in ops:
            per[op.eng].append(op)
        nsem = {}
        ndma = {}
        for e in self.ENGS:
            c = 0
            nd = 0
            K = self.DMA_POOL.get(e, 4)
            for op in per[e]:
                if op.isdma:
                    op.semk = ("dma", e, nd % K)
                    op.cnt = 16 * (nd // K + 1)
                    nd += 1
                elif op.signal:
                    k = c // SEM_ROLL
                    op.semk = (e, k)
                    op.cnt = c % SEM_ROLL + 1
                    c += 1
            nsem[e] = (c + SEM_ROLL - 1) // SEM_ROLL
            ndma[e] = nd
        sems = {}
        for e in self.ENGS:
            for k in range(max(nsem[e], 1)):
                sems[(e, k)] = self.stack.enter_context(nc.semaphore(f"s_{e}_{k}"))
            K = self.DMA_POOL.get(e, 4)
            for k in range(min(K, ndma[e])):
                sems[("dma", e, k)] = self.stack.enter_context(nc.semaphore(f"d_{e}_{k}"))
        self.nwaits = 0
        block = self.stack.enter_context(nc.Block())

        def section(ename):
            def body(eng):
                waited = {}
                K = self.DMA_POOL.get(ename, 4)
                nd = 0
                for op in per[ename]:
                    need = {}
                    for d in op.deps:
                        y = ops[d]
                        if (not y.isdma) and y.eng == ename and (ename == "pe" or not SAME_ENGINE_SYNC):
                            continue
                        k = y.semk
                        if y.cnt > need.get(k, 0):
                            need[k] = y.cnt
                    if op.isdma and nd >= K:
                        k = ("dma", ename, nd % K)
                        c = 16 * (nd // K)
                        if c > need.get(k, 0):
                            need[k] = c
                    for k, c in need.items():
                        if waited.get(k, 0) >= c:
                            continue
                        if k[0] != "dma":
                            later = any(kk[0] == k[0] and kk[1] > k[1] for kk in waited if kk[0] != "dma")
                            if later:
                                continue
                        eng.wait_ge(sems[k], c)
                        self.nwaits += 1
                        waited[k] = c
                    ins = op.fn(eng)
                    if op.isdma:
                        ins.then_inc(sems[op.semk], 16)
                        nd += 1
                    elif op.signal:
                        ins.then_inc(sems[op.semk], 1)
                for k in range(min(K, nd)):
                    cnt = 16 * ((nd - 1 - k) // K + 1)
                    if waited.get(("dma", ename, k), 0) < cnt:
                        eng.wait_ge(sems[("dma", ename, k)], cnt)
            return body

        block.tensor(section("pe"))
        block.scalar(section("act"))
        block.vector(section("dve"))
        block.gpsimd(section("pool"))
        block.sync(section("sp"))
        self.stack.close()


D = 1024
NIN = 2304
DFF = 2816
PLE = 256
RMS_EPS = 1e-6
GN_EPS = 64e-5
NEG = -30000.0


class Cfg:
    def __init__(self, L=4, SEQ=4096, NSEG=4, DECB=128):
        self.L = L
        self.SEQ = SEQ
        self.NSEG = NSEG
        self.TP = SEQ // NSEG
        self.SBC = DECB // 8
        self.SB = self.SBC // NSEG
        self.NS = self.SB * 4
        self.T = self.TP + self.NS
        self.NB = self.TP // 128
        self.NCH = self.TP // 64
        assert self.TP % 128 == 0 and self.SBC % NSEG == 0
        nblk = -(-self.T // 512)
        w = -(-self.T // nblk)
        w = -(-w // 4) * 4
        blks = []
        c = 0
        while c < self.T:
            n = min(w, self.T - c)
            blks.append((c, n))
            c += n
        self.blks = blks


PC = {}
_c = 0
for _n, _w in [("gpre", 8), ("gpost", 8), ("gfpre", 8), ("gfpost", 8), ("bq", 4), ("bkd", 2), ("brw", 8), ("blx", 2),
               ("blg", 2), ("mu", 8), ("w0", 2), ("a0", 2), ("kk", 2), ("ka", 2), ("rk", 2), ("lnw", 2), ("lnb", 2),
               ("cw", 8), ("cb", 2), ("ba", 2), ("bi", 2), ("Lp", 2), ("bout", 8), ("c8", 2), ("ka1", 2)]:
    PC[_n] = (_c, _w)
    _c += _w
NPRM = _c


class KernBase:
    def __init__(self, P, cfg, dr):
        self.P = P
        self.cfg = cfg
        self.dr = dr
        self.nc = P.nc
        self._bank = 0
        self.wq = []
        self.wq_issued = 0
        self.wq_slots = {}

    def bank(self, n=1):
        if self._bank + n > 8:
            self._bank = 0
        b = self._bank
        self._bank = (self._bank + n) % 8
        return b * 512

    def ps(self, p, ncols, nb=1, c0=None):
        if c0 is None:
            c0 = self.bank(nb)
        return self.PS[0:p, c0:c0 + ncols]

    def prm(self, l, name, i=0, n=1, p0=0, p1=128):
        c, w = PC[name]
        return self.PRM[p0:p1, l, c + i:c + i + n]

    def setup(self):
        P, cfg, dr = self.P, self.cfg, self.dr
        L, T = cfg.L, cfg.T
        self.PS = P.psum("PS", [128, 4096], F32)
        self.ident = P.sbuf("ident", [128, 128], BF16)
        self.identf = P.sbuf("identf", [128, 128], F32)
        self.ones = P.sbuf("ones", [128, 128], BF16)
        self.onesbd = P.sbuf("onesbd", [128, 128], BF16)
        self.maskb = P.sbuf("maskb", [128, 256], F32)
        self.maskf = P.sbuf("maskf", [128, 256], F32)
        self.trim = P.sbuf("trim", [64, 2, 64], F32)
        self.trilo = P.sbuf("trilo", [64, 64], F32)
        self.PRM = P.sbuf("PRM", [128, L, NPRM], F32)
        self.bkv = P.sbuf("bkv", [128, L, 256], F32)
        self.sinkb = P.sbuf("sinkb", [128, L, 8], F32)
        self.wup = P.sbuf("wup", [128, L, 256], BF16)
        self.aup = P.sbuf("aup", [128, L, 256], BF16)
        self.gup = P.sbuf("gup", [128, L, 256], BF16)
        self.wabd = P.sbuf("wabd", [128, L, 2, 128], BF16)
        self.wibd = P.sbuf("wibd", [128, L, 2, 128], BF16)
        self.epsc = P.sbuf("epsc", [128, 2], F32)
        self.kcar = P.sbuf("kcar", [128, L, 2, 128], BF16)
        self.vcar = P.sbuf("vcar", [128, L, 512], BF16)
        self.shcar = P.sbuf("shcar", [128, L, 8], F32)
        self.Hst = P.sbuf("Hst", [128, L, 2, 2, 128], F32)
        self.cvcar = P.sbuf("cvcar", [128, L, 2, 3], F32)
        self.hcar = P.sbuf("hcar", [128, L, 2], F32)
        self.xT = P.sbuf("xT", [128, 8, T], F32)
        self.hT = P.sbuf("hT", [128, 8, T], BF16)
        self.mixT = P.sbuf("mixT", [128, 8, T], BF16)
        self.rstd = P.sbuf("rstd", [128, T], F32)
        self.lntmp = P.sbuf("lntmp", [128, 512], F32)
        self.stage = [P.sbuf(f"stage{i}", [128, 1024], F32) for i in range(2)]
        self.NSLOT = 4
        self.wslot = [P.sbuf(f"wslot{i}", [128, 4096], BF16) for i in range(self.NSLOT)]
        self.AR = P.sbuf("arena", [128, self.ARENA_BYTES // 4], F32)
        self.arena = Arena(self.AR)

        P.memset(self.identf.full(), 1.0)
        idf = self.identf.full()
        P.add("pool", lambda e: e.affine_select(idf.ap, idf.ap, [[-1, 128]], ALU.is_equal, 0.0, base=0, channel_multiplier=1),
              [idf], [idf], name="ident")
        P.copy(self.ident.full(), idf)
        P.memset(self.ones.full(), 1.0)
        P.memset(self.onesbd.full(), 0.0)
        P.memset(self.onesbd[0:64, 0:64], 1.0)
        P.memset(self.onesbd[64:128, 64:128], 1.0)
        P.memset(self.epsc[:, 0:1], RMS_EPS)
        P.memset(self.epsc[:, 1:2], GN_EPS)

        def asel(view, pattern, op, fill, base, cm):
            P.add("pool", lambda e: e.affine_select(view.ap, view.ap, pattern, op, fill, base=base, channel_multiplier=cm),
                  [view], [view], name="asel")
        for m in (self.maskb, self.maskf):
            P.memset(m.full(), 0.0)
            asel(m.full(), [[1, 256]], ALU.is_ge, NEG, 0, -1)
            asel(m.full(), [[-1, 256]], ALU.is_ge, NEG, 128, 1)
        asel(self.maskf.full(), [[1, 256]], ALU.is_ge, NEG, -128, 0)
        P.memset(self.trim.full(), 1.0)
        asel(self.trim[:, 0, :], [[1, 64]], ALU.is_gt, 0.0, 0, -1)
        asel(self.trim[:, 1, :], [[1, 64]], ALU.is_ge, 0.0, 0, -1)
        P.memset(self.trilo.full(), 1.0)
        asel(self.trilo.full(), [[-1, 64]], ALU.is_gt, 0.0, 0, 1)

        for t_ in (self.kcar, self.vcar, self.shcar, self.Hst, self.cvcar, self.hcar, self.wabd, self.wibd):
            P.memset(t_.full(), 0.0, eng="pool")

        def ld(name, src, i=0):
            c, w = PC[name]
            n = src.shape[1] // 128
            for l_ in range(L):
                P.dma("sp", self.PRM[:, l_, c + i:c + i + n], src[l_].rearrange("(c p) -> p c", p=128), chan="setup",
                      allow_slow_non_contiguous=True)
        ld("gpre", dr["norm_mix_pre"]); ld("gpost", dr["norm_mix_post"])
        ld("gfpre", dr["norm_ffn_pre"]); ld("gfpost", dr["norm_ffn_post"])
        ld("bq", dr["b_in"][:, 0:512])
        ld("brw", dr["b_in"][:, 768:1792]); ld("blx", dr["b_in"][:, 1792:2048]); ld("blg", dr["b_in"][:, 2048:2304])
        ld("mu", dr["rwkv_mu"]); ld("w0", dr["rwkv_w0"]); ld("a0", dr["rwkv_a0"]); ld("kk", dr["rwkv_k_k"])
        ld("ka", dr["rwkv_k_a"]); ld("rk", dr["rwkv_r_k"].rearrange("l h d -> l (h d)"))
        ld("lnw", dr["rwkv_ln_w"]); ld("lnb", dr["rwkv_ln_b"])
        for j in range(4):
            ld("cw", dr["lru_conv_w"][:, j, :], i=2 * j)
        ld("cb", dr["lru_conv_b"]); ld("ba", dr["lru_b_a"]); ld("bi", dr["lru_b_i"]); ld("Lp", dr["lru_L"])
        ld("bout", dr["b_out"])
        c, w = PC["bkd"]
        for g in range(2):
            for hp in range(2):
                for l_ in range(L):
                    P.dma("sp", self.PRM[hp * 64:(hp + 1) * 64, l_, c + g:c + g + 1],
                          dr["b_in"][l_, 512 + 64 * g:512 + 64 * g + 64].rearrange("(c p) -> p c", p=64), chan="setup",
                          allow_slow_non_contiguous=True)
        P.dma("sp", self.bkv.full(), dr["b_in"][:, 512:768].partition_broadcast(128), chan="setup")
        P.dma("sp", self.sinkb.full(), dr["attn_sinks"].partition_broadcast(128), chan="setup")
        P.dma("pool", self.wup[0:64, :, :], dr["rwkv_w_up"].rearrange("l i j -> i l j"), chan="setup2")
        P.dma("pool", self.aup[64:128, :, :], dr["rwkv_a_up"].rearrange("l i j -> i l j"), chan="setup2")
        P.dma("pool", self.gup.full(), dr["rwkv_g_up"].rearrange("l i j -> i l j"), chan="setup2")
        for par in range(2):
            for (dst, src) in ((self.wabd, dr["lru_w_a"]), (self.wibd, dr["lru_w_i"])):
                for cc in range(2):
                    P.dma("pool", dst[par * 64:(par + 1) * 64, :, cc, par * 64:(par + 1) * 64],
                          src[:, 2 * cc + par, :, :].rearrange("l i j -> i l j"), chan="setup2")
        for l in range(L):
            P.act(self.prm(l, "c8", 0, 2), self.prm(l, "Lp", 0, 2), AF.Sigmoid)
            P.act(self.prm(l, "c8", 0, 2), self.prm(l, "c8", 0, 2), AF.Ln)
            P.ts(self.prm(l, "c8", 0, 2), self.prm(l, "c8", 0, 2), 8.0, ALU.mult)
            P.ts(self.prm(l, "ka1", 0, 2), self.prm(l, "ka", 0, 2), -1.0, ALU.mult, 1.0, ALU.add)

    def wq_add(self, loader):
        self.wq.append(loader)
        return len(self.wq) - 1

    def wget(self, bid):
        lim = min(len(self.wq), bid + self.NSLOT)
        while self.wq_issued < lim:
            i = self.wq_issued
            self.wq[i](self.wslot[i % self.NSLOT], f"w{i % self.NSLOT}")
            self.wq_issued += 1
        return self.wslot[bid % self.NSLOT]

    def wload(self, slot, chan, src, nk, W, pieces):
        for (s0, n, d0) in pieces:
            dst_ap = slot.h[:, 0:nk * W].rearrange("p (k w) -> p k w", w=W)[:, :, d0:d0 + n]
            v = View(slot, dst_ap, 0, 128, 0, nk * W * 2)
            self.P.dma("pool", v, src[:, s0:s0 + n].rearrange("(k p) n -> p k n", p=128), chan=chan)

    def wview(self, slot, nk, W, k, c0, n, p0=0, p1=128):
        ap = slot.h[p0:p1, 0:nk * W].rearrange("p (k w) -> p k w", w=W)[:, k, c0:c0 + n]
        return View(slot, ap, p0, p1, (k * W + c0) * 2, (k * W + c0 + n) * 2)

    def load_segment(self, s):
        P, cfg, dr = self.P, self.cfg, self.dr
        TP, NB, NS = cfg.TP, cfg.NB, cfg.NS
        k = 0
        for b in range(NB + 1):
            st = self.stage[k % 2]
            k += 1
            if b < NB:
                n = 128
                src = dr["xp"][s * TP + b * 128: s * TP + (b + 1) * 128, :]
                c0 = b * 128
            else:
                n = NS
                src = dr["xs"][s * NS:(s + 1) * NS, :]
                c0 = TP
            P.dma("sp", st[0:n, :], src, chan=f"xin{(k - 1) % 2}")
            for half in range(2):
                pv = self.ps(128, 4 * n)
                for j in range(4):
                    c = half * 4 + j
                    P.transpose(pv[:, j * n:(j + 1) * n], st[0:n, c * 128:(c + 1) * 128], self.identf[0:n, 0:n])
                dst = self.xT[:, half * 4:half * 4 + 4, c0:c0 + n]
                src_v = pv.with_ap(pv.ap.rearrange("p (j n) -> p j n", j=4))
                if half == 0:
                    P.copy(dst, src_v, eng="act")
                else:
                    P.copy(dst, src_v, eng="dve")

    def store_segment(self, s):
        P, cfg, dr = self.P, self.cfg, self.dr
        TP, NB, NS = cfg.TP, cfg.NB, cfg.NS
        k = 0
        for b in range(NB + 1):
            st = self.stage[k % 2]
            k += 1
            if b < NB:
                n = 128
                dst = dr["yp"][s * TP + b * 128: s * TP + (b + 1) * 128, :]
                c0 = b * 128
            else:
                n = NS
                dst = dr["ys"][s * NS:(s + 1) * NS, :]
                c0 = TP
            for half in range(2):
                pv = self.ps(n, 512)
                for j in range(4):
                    c = half * 4 + j
                    P.transpose(pv[:, j * 128:(j + 1) * 128], self.xT[:, c, c0:c0 + n], self.identf.full())
                if half == 0:
                    P.copy(st[0:n, 0:512], pv, eng="act")
                else:
                    P.copy(st[0:n, 512:1024], pv, eng="dve")
            P.dma("sp", dst, st[0:n, :], chan=f"out{(k - 1) % 2}")

    def sumsq_rstd(self, src_fn, sq):
        P, cfg = self.P, self.cfg
        for c in range(8):
            P.act(sq[:, c, :], src_fn(c), AF.Square)
        for (c0, n) in cfg.blks:
            pv = self.ps(128, n)
            for c in range(8):
                P.mm(pv, self.ones.full(), sq[:, c, c0:c0 + n], start=(c == 0), stop=(c == 7))
            P.act(self.lntmp[:, 0:n], pv, AF.Ln, bias=self.epsc[:, 0:1], scale=1.0 / D)
            P.act(self.rstd[:, c0:c0 + n], self.lntmp[:, 0:n], AF.Exp, scale=-0.5)

    def dense(self, M, nk, lhsT_fn, rhs_fn, evac_fn, blks=None):
        P = self.P
        for (c0, n) in (blks or self.cfg.blks):
            pv = self.ps(M, n)
            for k in range(nk):
                P.mm(pv, lhsT_fn(k), rhs_fn(k, c0, n), start=(k == 0), stop=(k == nk - 1))
            evac_fn(pv, c0, n)


NBLK_PER_LAYER = 30


def rr_gen(gens):
    gens = list(gens)
    while gens:
        for g_ in list(gens):
            try:
                next(g_)
                yield
            except StopIteration:
                gens.remove(g_)


def run_rr(gens):
    gens = list(gens)
    while gens:
        for g_ in list(gens):
            try:
                next(g_)
            except StopIteration:
                gens.remove(g_)


class Kern(KernBase):
    ARENA_BYTES = 92 * 1024

    def register_layer(self, l):
        dr = self.dr
        w_in = dr["w_in"][l]
        ids = {}

        def reg(name, fn):
            ids[name] = self.wq_add(fn)
        reg("q", lambda sl, ch: self.wload(sl, ch, w_in, 8, 512, [(0, 512, 0)]))
        reg("kv", lambda sl, ch: self.wload(sl, ch, w_in, 8, 512, [(512, 64, 0), (512, 64, 64), (576, 64, 128), (576, 64, 192), (512, 256, 256)]))
        reg("lru", lambda sl, ch: self.wload(sl, ch, w_in, 8, 512, [(1792, 512, 0)]))
        reg("lora", lambda sl, ch: self.wload(sl, ch, w_in, 8, 256, [(768 + 768, 256, 0)]))
        for p in range(2):
            reg(f"pair{p}", lambda sl, ch, p=p: self.wload(sl, ch, w_in, 8, 384, [(768 + 128 * p, 128, 0), (768 + 256 + 128 * p, 128, 128), (768 + 512 + 128 * p, 128, 256)]))
        for j in range(2):
            reg(f"wout{j}", lambda sl, ch, j=j: self.wload(sl, ch, dr["w_out"][l], 8, 512, [(512 * j, 512, 0)]))
        for j in range(11):
            def f(sl, ch, j=j):
                self.wload(sl, ch, dr["ffn_w_gate"][l], 8, 512, [(256 * j, 256, 0)])
                self.wload(sl, ch, dr["ffn_w_up"][l], 8, 512, [(256 * j, 256, 256)])
            reg(f"gu{j}", f)
        for m in range(8):
            reg(f"down{m}", lambda sl, ch, m=m: self.wload(sl, ch, dr["ffn_w_down"][l], 22, 128, [(128 * m, 128, 0)]))
        reg("pw", lambda sl, ch: self.wload(sl, ch, dr["ple_w"][l], 2, 1024, [(0, 1024, 0)]))
        for j in range(2):
            reg(f"pg{j}", lambda sl, ch, j=j: self.wload(sl, ch, dr["ple_gate_w"][l], 8, 512, [(512 * j, 512, 0)]))
        return ids

    def layer(self, l, s, ids):
        P, cfg = self.P, self.cfg
        T, TP = cfg.T, cfg.TP
        A = self.arena
        m0 = A.mark()
        import os as _os
        STOP = int(_os.environ.get("KS_STOP", "99"))
        if STOP < 1:
            return
        sq = self.mixT
        self.sumsq_rstd(lambda c: self.xT[:, c, :], sq)
        for c in range(8):
            P.stt(self.hT[:, c, :], self.xT[:, c, :], self.prm(l, "gpre", c), self.rstd.full(), ALU.mult, ALU.mult)
        if STOP < 2:
            return
        self.attention(l, s, ids, side=self.lru_gen(l, s, ids))
        A.release(m0)
        self.rwkv(l, s, ids)
        A.release(m0)
        ybuf = A.alloc("ybuf", [128, 8, T], F32)
        for j in range(2):
            sl = self.wget(ids[f"wout{j}"])
            for mm_ in range(4):
                m = j * 4 + mm_
                self.dense(128, 8, lambda k: self.wview(sl, 8, 512, k, mm_ * 128, 128),
                           lambda k, c0, n: self.mixT[:, k, c0:c0 + n],
                           lambda pv, c0, n, m=m: P.act(ybuf[:, m, c0:c0 + n], pv, AF.Identity, bias=self.prm(l, "bout", m)))
        self.post_norm_add(l, ybuf, "gpost")
        if STOP < 6:
            A.release(m0)
            return
        self.sumsq_rstd(lambda c: self.xT[:, c, :], self.mixT)
        for c in range(8):
            P.stt(self.hT[:, c, :], self.xT[:, c, :], self.prm(l, "gfpre", c), self.rstd.full(), ALU.mult, ALU.mult)
        act = A.alloc("act", [128, 22, T], BF16)
        sg = A.alloc("sg", [128, 2, 512], F32)
        ei = 0
        for j in range(11):
            sl = self.wget(ids[f"gu{j}"])
            for jj in range(2):
                fc = 2 * j + jj
                for (c0, n) in cfg.blks:
                    pg = self.ps(128, n)
                    for k in range(8):
                        P.mm(pg, self.wview(sl, 8, 512, k, jj * 128, 128), self.hT[:, k, c0:c0 + n], start=(k == 0), stop=(k == 7))
                    pu = self.ps(128, n)
                    for k in range(8):
                        P.mm(pu, self.wview(sl, 8, 512, k, 256 + jj * 128, 128), self.hT[:, k, c0:c0 + n], start=(k == 0), stop=(k == 7))
                    sgt = sg[:, ei % 2, 0:n]
                    ei += 1
                    P.act(sgt, pg, AF.Silu)
                    P.tt(act[:, fc, c0:c0 + n], sgt, pu, ALU.mult)
        for m in range(8):
            sl = self.wget(ids[f"down{m}"])
            self.dense(128, 22, lambda k: self.wview(sl, 22, 128, k, 0, 128),
                       lambda k, c0, n: act[:, k, c0:c0 + n],
                       lambda pv, c0, n, m=m: P.copy(ybuf[:, m, c0:c0 + n], pv, eng="act"))
        self.post_norm_add(l, ybuf, "gfpost")
        A.release(m0)
        if STOP < 7:
            return
        for c in range(8):
            P.copy(self.hT[:, c, :], self.xT[:, c, :], eng=("act" if c % 2 else "dve"))
        pT = A.alloc("pT", [128, 2, T], BF16)
        self.load_ple(l, s, pT)
        pwb = A.alloc("pwb", [128, 8, T], F32)
        sgp = A.alloc("sgp", [128, 2, 512], F32)
        slw = self.wget(ids["pw"])
        for m in range(8):
            self.dense(128, 2, lambda k: self.wview(slw, 2, 1024, k, m * 128, 128), lambda k, c0, n: pT[:, k, c0:c0 + n],
                       lambda pv, c0, n, m=m: P.copy(pwb[:, m, c0:c0 + n], pv, eng="act"))
        ei = 0
        for j in range(2):
            sl = self.wget(ids[f"pg{j}"])
            for mm_ in range(4):
                m = j * 4 + mm_
                for (c0, n) in cfg.blks:
                    pg = self.ps(128, n)
                    for k in range(8):
                        P.mm(pg, self.wview(sl, 8, 512, k, mm_ * 128, 128), self.hT[:, k, c0:c0 + n], start=(k == 0), stop=(k == 7))
                    sgt = sgp[:, ei % 2, 0:n]
                    ei += 1
                    P.act(sgt, pg, AF.Sigmoid)
                    P.tt(sgt, sgt, pwb[:, m, c0:c0 + n], ALU.mult)
                    P.tt(self.xT[:, m, c0:c0 + n], self.xT[:, m, c0:c0 + n], sgt, ALU.add)
        A.release(m0)

    def post_norm_add(self, l, ybuf, gname):
        P, cfg = self.P, self.cfg
        self.sumsq_rstd(lambda c: ybuf[:, c, :], self.mixT)
        for c in range(8):
            P.stt(ybuf[:, c, :], ybuf[:, c, :], self.prm(l, gname, c), self.rstd.full(), ALU.mult, ALU.mult)
            P.tt(self.xT[:, c, :], self.xT[:, c, :], ybuf[:, c, :], ALU.add)

    def load_ple(self, l, s, pT):
        P, cfg, dr = self.P, self.cfg, self.dr
        TP, NB, NS = cfg.TP, cfg.NB, cfg.NS
        for b in range(NB + 1):
            st = self.stage[b % 2]
            if b < NB:
                n = 128
                src = dr["pp"][l, s * TP + b * 128: s * TP + (b + 1) * 128, :]
                c0 = b * 128
            else:
                n = NS
                src = dr["psm"][l, s * NS:(s + 1) * NS, :]
                c0 = TP
            P.dma("sp", st[0:n, 0:256], src, chan=f"xin{b % 2}")
            pv = self.ps(128, 2 * n)
            for j in range(2):
                P.transpose(pv[:, j * n:(j + 1) * n], st[0:n, j * 128:(j + 1) * 128], self.identf[0:n, 0:n])
            P.copy(pT[:, :, c0:c0 + n], pv.with_ap(pv.ap.rearrange("p (j n) -> p j n", j=2)), eng="act")

    def attention(self, l, s, ids, side=None):
        P, cfg, dr = self.P, self.cfg, self.dr
        T, TP, NB, SB, NS = cfg.T, cfg.TP, cfg.NB, cfg.SB, cfg.NS
        A = self.arena
        qT = A.alloc("qT", [128, 4, T], BF16)
        kdT = A.alloc("kdT", [128, 2, 128 + T], BF16)
        vpad = A.alloc("vpad", [128, NB + 1, 512], BF16)
        vpad_c = A.alloc("vpad_c", [128, SB, 512], BF16)
        vpad_s = A.alloc("vpad_s", [4, SB, 512], BF16)
        kcT = A.alloc("kcT", [128, SB, 2, 128], BF16)
        kvtok = A.alloc("kvtok", [128, 2, 256], F32)
        cst = A.alloc("cst", [128, 2, 128], F32)
        cdup = A.alloc("cdup", [128, 2, 128], BF16)
        sc = A.alloc("sc", [128, 8, 256], F32)
        ee = A.alloc("ee", [128, 8, 256], F32)
        pps = [A.alloc(f"pp{i_}", [128, 8, 256], BF16) for i_ in range(2)]
        pTs = A.alloc("pTs", [128, 16, 128], BF16)
        sm = A.alloc("sm", [128, 6, 8], F32)
        P.memset(vpad.full(), 0.0, eng="pool")
        P.memset(vpad_c.full(), 0.0, eng="pool")
        P.memset(vpad_s.full(), 0.0, eng="pool")
        P.copy(kdT[:, :, 0:128], self.kcar[:, l, :, :], eng="pool")
        P.copy(vpad[:, 0, :], self.vcar[:, l, :], eng="pool")
        sl = self.wget(ids["q"])
        for m in range(4):
            self.dense(128, 8, lambda k: self.wview(sl, 8, 512, k, m * 128, 128),
                       lambda k, c0, n: self.hT[:, k, c0:c0 + n],
                       lambda pv, c0, n, m=m: P.act(qT[:, m, c0:c0 + n], pv, AF.Identity, bias=self.prm(l, "bq", m)))
        sl = self.wget(ids["kv"])
        for g in range(2):
            self.dense(128, 8, lambda k: self.wview(sl, 8, 512, k, g * 128, 128),
                       lambda k, c0, n: self.hT[:, k, c0:c0 + n],
                       lambda pv, c0, n, g=g: P.act(kdT[:, g, 128 + c0:128 + c0 + n], pv, AF.Identity, bias=self.prm(l, "bkd", g)))

        def vscatter(dst_fn, src, n):
            for g in range(2):
                for var in range(2):
                    P.copy(dst_fn(g, var), src[0:n, 128 + g * 64:128 + (g + 1) * 64], eng=("dve" if var else "pool"))

        last_seg = (s == cfg.NSEG - 1)
        for b in range(NB):
            pv = self.ps(128, 256)
            for k in range(8):
                P.mm(pv, self.hT[:, k, b * 128:(b + 1) * 128], self.wview(sl, 8, 512, k, 256, 256), start=(k == 0), stop=(k == 7))
            kt = kvtok[:, b % 2, :]
            P.tt(kt, pv, self.bkv[:, l, :], ALU.add)
            vscatter(lambda g, var: vpad[:, b + 1, g * 256 + var * 192: g * 256 + var * 192 + 64], kt, 128)
            if last_seg and b == NB - 1:
                P.dma("sp", dr["kp"][l], kt[:, 0:128], chan="outs")
                P.dma("sp", dr["vp"][l], kt[:, 128:256], chan="outs")
        for b in range(SB):
            gb = s * SB + b
            pv = self.ps(4, 256)
            for k in range(8):
                P.mm(pv, self.hT[:, k, TP + 4 * b:TP + 4 * b + 4], self.wview(sl, 8, 512, k, 256, 256), start=(k == 0), stop=(k == 7))
            kt = kvtok[0:4, b % 2, :]
            P.tt(kt, pv, self.bkv[0:4, l, :], ALU.add)
            vscatter(lambda g, var: vpad_s[0:4, b, g * 256 + var * 192: g * 256 + var * 192 + 64], kt, 4)
            P.dma("sp", dr["ks"][l, gb, 124:128, :], kt[:, 0:128], chan="outs")
            P.dma("sp", dr["vs"][l, gb, 124:128, :], kt[:, 128:256], chan="outs")
            P.dma("sp", cst[:, 0, :], dr["ck"][l, gb], chan="cache0")
            P.dma("sp", cst[:, 1, :], dr["cv"][l, gb], chan="cache1")
            for g in range(2):
                for var in range(2):
                    P.copy(vpad_c[:, b, g * 256 + var * 192: g * 256 + var * 192 + 64], cst[:, 1, g * 64:(g + 1) * 64], eng=("dve" if var else "pool"))
            for g in range(2):
                for hp in range(2):
                    P.copy(cdup[:, g, hp * 64:(hp + 1) * 64], cst[:, 0, g * 64:(g + 1) * 64], eng=("dve" if hp else "pool"))
            pb = self.ps(128, 128).bitcast(BF16)
            for g in range(2):
                P.transpose(pb[:, g * 128:(g + 1) * 128], cdup[:, g, :], self.ident.full())
            P.copy(kcT[:, b, :, :], pb.with_ap(pb.ap.rearrange("p (g n) -> p g n", g=2)), eng="act")
        if s == 0:
            P.dma("sp", dr["ks"][l, :, 0:124, :], dr["ck"][l, :, 4:128, :], chan="outs")
            P.dma("sp", dr["vs"][l, :, 0:124, :], dr["cv"][l, :, 4:128, :], chan="outs")

        def attn_block(M, qc0, kviews, vviews, mask, wk, par):
            c_s = self.bank(4)
            pp = pps[par]

            def hcol(h):
                hp_, cc = h % 2, h // 2
                return (hp_ * 2 + cc // 2) * 512 + (cc % 2) * 256
            for h in range(8):
                c, hp, g = h // 2, h % 2, h // 4
                off = 0
                for (kf, nk) in kviews:
                    P.mm(self.PS[0:M, c_s + hcol(h) + off: c_s + hcol(h) + off + nk],
                         qT[hp * 64:(hp + 1) * 64, c, qc0:qc0 + M], kf(g, hp))
                    off += nk
            scv = sc[0:M, :, 0:wk]
            mk = mask[0:M, 0:wk]
            for h in range(8):
                P.stt(sc[0:M, h, 0:wk], self.PS[0:M, c_s + hcol(h): c_s + hcol(h) + wk], 0.125, mk, ALU.mult, ALU.add)
            mx = sm[0:M, 0, :]
            P.add("dve", lambda e: e.tensor_reduce(mx.ap, scv.ap, AX.X, ALU.max), [scv], [mx])
            P.tt(mx, mx, self.sinkb[0:M, l, :], ALU.max)
            nm = sm[0:M, 1, :]
            P.ts(nm, mx, -1.0, ALU.mult)
            rs = sm[0:M, 2, :]
            for h in range(8):
                P.act(ee[0:M, h, 0:wk], sc[0:M, h, 0:wk], AF.Exp, bias=sm[0:M, 1, h:h + 1], accum=sm[0:M, 2, h:h + 1])
            es = sm[0:M, 3, :]
            P.tt(es, self.sinkb[0:M, l, :], nm, ALU.add)
            P.act(es, es, AF.Exp)
            P.tt(es, es, rs, ALU.add)
            rd = sm[0:M, 4, :]
            P.add("dve", lambda e: e.reciprocal(rd.ap, es.ap), [es], [rd])
            for h in range(8):
                if h % 2:
                    P.ts(pp[0:M, h, 0:wk], ee[0:M, h, 0:wk], sm[0:M, 4, h:h + 1], ALU.mult)
                else:
                    P.act(pp[0:M, h, 0:wk], ee[0:M, h, 0:wk], AF.Copy, scale=sm[0:M, 4, h:h + 1])

            def phase_b():
                c_t = self.bank(2)
                pTp = self.PS[:, c_t:c_t + 1024].bitcast(BF16)
                nkb = len(kviews)
                for h in range(8):
                    off = 0
                    for kb, (kf, nk) in enumerate(kviews):
                        P.transpose(pTp[0:nk, (h * 2 + kb) * 128:(h * 2 + kb) * 128 + M], pp[0:M, h, off:off + nk], self.ident[0:M, 0:M])
                        off += nk
                for kb, (kf, nk) in enumerate(kviews):
                    for hb in range(2):
                        src = pTp[0:nk, hb * 1024:(hb + 1) * 1024]
                        src = src.with_ap(src.ap.rearrange("p (h kb m) -> p h kb m", h=4, kb=2)[:, :, kb, 0:M])
                        d0 = pTs[0:nk, hb * 8:(hb + 1) * 8, :]
                        dst = d0.with_ap(d0.ap.rearrange("p (h kb) m -> p h kb m", kb=2)[:, :, kb, 0:M])
                        P.copy(dst, src, eng=("act" if hb == 0 else "dve"))
                c_o = self.bank(1)
                for c in range(4):
                    po = self.PS[:, c_o + c * M: c_o + (c + 1) * M]
                    first = True
                    for hh in range(2):
                        h = 2 * c + hh
                        g = h // 4
                        for kb, (kf, nk) in enumerate(kviews):
                            lastmm = (hh == 1 and kb == nkb - 1)
                            P.mm(po, vviews[kb](g, hh, nk), pTs[0:nk, h * 2 + kb, 0:M], start=first, stop=lastmm)
                            first = False
                pov = self.PS[:, c_o:c_o + 4 * M]
                P.copy(self.mixT[:, 0:4, qc0:qc0 + M], pov.with_ap(pov.ap.rearrange("p (c m) -> p c m", c=4)), eng="act")
            return phase_b

        jobs = []
        for i in range(NB):
            mask = self.maskf if (s == 0 and i == 0) else self.maskb
            jobs.append((128, i * 128,
                         [(lambda g, hp, i=i: kdT[hp * 64:(hp + 1) * 64, g, i * 128:i * 128 + 128], 128),
                          (lambda g, hp, i=i: kdT[hp * 64:(hp + 1) * 64, g, (i + 1) * 128:(i + 1) * 128 + 128], 128)],
                         [lambda g, hh, nk, i=i: vpad[0:nk, i, g * 256 + hh * 128: g * 256 + hh * 128 + 128],
                          lambda g, hh, nk, i=i: vpad[0:nk, i + 1, g * 256 + hh * 128: g * 256 + hh * 128 + 128]],
                         mask, 256))
        for b in range(SB):
            jobs.append((4, TP + 4 * b,
                         [(lambda g, hp, b=b: kcT[hp * 64:(hp + 1) * 64, b, g, :], 128),
                          (lambda g, hp, b=b: kdT[hp * 64:(hp + 1) * 64, g, 128 + TP + 4 * b:128 + TP + 4 * b + 4], 4)],
                         [lambda g, hh, nk, b=b: vpad_c[0:nk, b, g * 256 + hh * 128: g * 256 + hh * 128 + 128],
                          lambda g, hh, nk, b=b: vpad_s[0:nk, b, g * 256 + hh * 128: g * 256 + hh * 128 + 128]],
                         self.maskb, 132))
        def side_steps(k):
            if side is not None:
                for _ in range(k):
                    if next(side, "done") == "done":
                        break
        pend = None
        for ji, job in enumerate(jobs):
            fin = attn_block(*job, ji % 2)
            side_steps(7)
            if pend is not None:
                pend()
                side_steps(7)
            pend = fin
        pend()
        side_steps(100000)
        P.copy(self.kcar[:, l, :, :], kdT[:, :, TP:TP + 128], eng="pool")
        P.copy(self.vcar[:, l, :], vpad[:, NB, :], eng="pool")

    def rwkv(self, l, s, ids):
        P, cfg = self.P, self.cfg
        self.wget(ids["lora"]); self.wget(ids["pair0"]); self.wget(ids["pair1"])
        P.memset(self.mixT[:, 4:6, :], 0.0, eng="pool")

    def lru_gen(self, l, s, ids):
        P, cfg, dr = self.P, self.cfg, self.dr
        T, TP, SB, NS = cfg.T, cfg.TP, cfg.SB, cfg.NS
        A = self.arena
        sl = self.wget(ids["lru"])
        xe_p = A.alloc("xe_p", [128, 2, 3 + TP], F32)
        xe_s = A.alloc("xe_s", [128, 2, SB, 7], F32)
        gb = A.alloc("gb", [128, 2, T], F32)
        xc = A.alloc("xc", [128, 2, T], F32)
        xcb = A.alloc("xcb", [128, 2, T], BF16)
        rg = A.alloc("rg", [128, 2, T], F32)
        ig = A.alloc("ig", [128, 2, T], F32)
        aa = A.alloc("aa", [128, 2, T], F32)
        uu = A.alloc("uu", [128, 2, T], F32)
        hh_ = A.alloc("hh", [128, 2, T], F32)
        gl = A.alloc("gl", [128, 2, T], F32)
        h0s = A.alloc("h0s", [128, 2, SB], F32)
        cvo = A.alloc("cvo", [128, 2, SB, 3], F32)
        for c in range(2):
            P.copy(xe_p[:, c, 0:3], self.cvcar[:, l, c, :], eng="pool")
        for b in range(SB):
            gbi = s * SB + b
            for c in range(2):
                P.dma("sp", xe_s[:, c, b, 0:3], dr["st_conv"][l, gbi, :, c * 128:(c + 1) * 128].rearrange("j p -> p j"),
                      chan="st", allow_slow_non_contiguous=True)
        for c in range(2):
            P.dma("sp", h0s[:, c, :], dr["st_lru"][l, s * SB:(s + 1) * SB, c * 128:(c + 1) * 128].rearrange("b p -> p b"), chan="st",
                  allow_slow_non_contiguous=True)
        for c in range(2):
            def ev_x(pv, c0, n, c=c):
                npp = max(0, min(c0 + n, TP) - c0)
                if npp > 0:
                    P.act(xe_p[:, c, 3 + c0:3 + c0 + npp], pv[:, 0:npp], AF.Identity, bias=self.prm(l, "blx", c))
                if npp < n:
                    lo = c0 + npp - TP
                    assert lo % 4 == 0 and (n - npp) % 4 == 0
                    ps_ = pv[:, npp:n]
                    P.act(xe_s[:, c, lo // 4:(lo + n - npp) // 4, 3:7], ps_.with_ap(ps_.ap.rearrange("p (b t) -> p b t", t=4)), AF.Identity,
                          bias=self.prm(l, "blx", c))
            self.dense(128, 8, lambda k: self.wview(sl, 8, 512, k, c * 128, 128), lambda k, c0, n: self.hT[:, k, c0:c0 + n], ev_x)
            yield
            self.dense(128, 8, lambda k: self.wview(sl, 8, 512, k, 256 + c * 128, 128), lambda k, c0, n: self.hT[:, k, c0:c0 + n],
                       lambda pv, c0, n, c=c: P.act(gb[:, c, c0:c0 + n], pv, AF.Identity, bias=self.prm(l, "blg", c)))
            yield
        def conv_chain(c, dst, ext):
            P.ts(dst, ext(0), self.prm(l, "cw", 0 + c), ALU.mult, self.prm(l, "cb", c), ALU.add)
            yield
            for j in range(1, 4):
                P.stt(dst, ext(j), self.prm(l, "cw", 2 * j + c), dst, ALU.mult, ALU.add)
                yield
        gens = []
        for c in range(2):
            gens.append(conv_chain(c, xc[:, c, 0:TP], lambda j, c=c: xe_p[:, c, j:j + TP]))
            xs_ = xc[:, c, TP:T]
            gens.append(conv_chain(c, xs_.with_ap(xs_.ap.rearrange("p (b t) -> p b t", t=4)), lambda j, c=c: xe_s[:, c, :, j:j + 4]))
        yield from rr_gen(gens)
        for c in range(2):
            P.copy(self.cvcar[:, l, c, :], xe_p[:, c, TP:TP + 3], eng="pool")
            P.copy(cvo[:, c, :, :], xe_s[:, c, :, 4:7], eng="pool")
            P.copy(xcb[:, c, :], xc[:, c, :], eng="act")
        for c in range(2):
            self.dense(128, 1, lambda k: self.wabd[:, l, c, :], lambda k, c0, n: xcb[:, c, c0:c0 + n],
                       lambda pv, c0, n, c=c: P.act(rg[:, c, c0:c0 + n], pv, AF.Sigmoid, bias=self.prm(l, "ba", c)))
            yield
            self.dense(128, 1, lambda k: self.wibd[:, l, c, :], lambda k, c0, n: xcb[:, c, c0:c0 + n],
                       lambda pv, c0, n, c=c: P.act(ig[:, c, c0:c0 + n], pv, AF.Sigmoid, bias=self.prm(l, "bi", c)))
            yield
        def main_chain(c):
            a_ = aa[:, c, :]
            u_ = uu[:, c, :]
            P.act(a_, rg[:, c, :], AF.Exp, scale=self.prm(l, "c8", c)); yield
            P.tt(u_, a_, a_, ALU.mult); yield
            P.ts(u_, u_, -1.0, ALU.mult, 1.0, ALU.add); yield
            P.act(u_, u_, AF.Sqrt); yield
            P.tt(ig[:, c, :], ig[:, c, :], xc[:, c, :], ALU.mult); yield
            P.tt(u_, u_, ig[:, c, :], ALU.mult); yield
            us = uu[:, c, TP:T]
            us3 = us.with_ap(us.ap.rearrange("p (b t) -> p b t", t=4)[:, :, 0])
            as_ = aa[:, c, TP:T]
            as3 = as_.with_ap(as_.ap.rearrange("p (b t) -> p b t", t=4)[:, :, 0])
            tmp = h0s[:, c, :]
            P.tt(tmp, tmp, as3, ALU.mult); yield
            P.tt(us3, us3, tmp, ALU.add); yield
            P.memset(as3, 0.0); yield
            hv = hh_[:, c, :]
            ini = self.hcar[:, l, c:c + 1]
            P.add("dve", lambda e, hv=hv, a_=a_, u_=u_, ini=ini: e.tensor_tensor_scan(hv.ap, a_.ap, u_.ap, ini.ap, ALU.mult, ALU.add),
                  [a_, u_, ini], [hv]); yield
            P.copy(self.hcar[:, l, c:c + 1], hh_[:, c, TP - 1:TP], eng="pool"); yield

        def gelu_chain(c):
            g_ = gb[:, c, :]
            t1 = gl[:, c, :]
            P.act(t1, g_, AF.Square); yield
            P.ts(t1, t1, 0.044715, ALU.mult, 1.0, ALU.add); yield
            P.tt(t1, t1, g_, ALU.mult); yield
            P.act(t1, t1, AF.Sigmoid, scale=1.5957691216057308); yield
            P.tt(t1, t1, g_, ALU.mult); yield
        yield from rr_gen([main_chain(0), gelu_chain(0), main_chain(1), gelu_chain(1)])
        for c in range(2):
            P.tt(self.mixT[:, 6 + c, :], gl[:, c, :], hh_[:, c, :], ALU.mult)
        for b in range(SB):
            gbi = s * SB + b
            for c in range(2):
                P.dma("sp", dr["convs"][l, gbi, :, c * 128:(c + 1) * 128].rearrange("j p -> p j"), cvo[:, c, b, :], chan="outs",
                      allow_slow_non_contiguous=True)
        hs = hh_[:, :, TP:T]
        hs_last = hs.with_ap(hs.ap.rearrange("p c (b t) -> p c b t", t=4)[:, :, :, 3])
        P.copy(h0s.full(), hs_last, eng="pool")
        for c in range(2):
            P.dma("sp", dr["lrus"][l, s * SB:(s + 1) * SB, c * 128:(c + 1) * 128].rearrange("b p -> p b"), h0s[:, c, :], chan="outs",
                  allow_slow_non_contiguous=True)
        if s == cfg.NSEG - 1:
            for c in range(2):
                P.dma("sp", dr["convp"][l, :, c * 128:(c + 1) * 128].rearrange("j p -> p j"), self.cvcar[:, l, c, :], chan="outs",
                      allow_slow_non_contiguous=True)
            P.dma("sp", dr["lrup"][l].rearrange("(c p) -> p c", p=128), self.hcar[:, l, :], chan="outs", allow_slow_non_contiguous=True)


NG = 5
CW = 64
DEC_C = -math.exp(-0.5)


class Kern(Kern):
    def setup(self):
        super().setup()
        P = self.P
        A = self.arena
        self.trimx = A.alloc("trimx", [64, 2, 2, 2, 64], F32)
        self.trilox = A.alloc("trilox", [64, NG, 64], F32)
        self.identx = A.alloc("identx", [64, NG, 2, 64], BF16)
        for a in range(2):
            for w in range(2):
                P.copy(self.trimx[:, a, w, :, :], self.trim.full(), eng="pool")
        for i in range(NG):
            P.copy(self.trilox[:, i, :], self.trilo.full(), eng="pool")
            for hh in range(2):
                P.copy(self.identx[:, i, hh, :], self.ident[0:64, 0:64], eng="pool")

    def rwkv(self, l, s, ids):
        P, cfg, dr = self.P, self.cfg, self.dr
        T, TP, SB, NS, NCH = cfg.T, cfg.TP, cfg.SB, cfg.NS, cfg.NCH
        assert NCH % 2 == 0
        NU = NCH + SB
        TW = NU * CW
        A = self.arena
        last_seg = (s == cfg.NSEG - 1)
        tblks = []
        c_ = 0
        while c_ < TW:
            n_ = min(512, TW - c_)
            tblks.append((c_, n_))
            c_ += n_

        def v3(view, inner):
            return view.with_ap(view.ap.rearrange("p (u c) -> p u c", c=inner))

        pE_ = [A.alloc(f"pE{i_}", [128, 1 + TP], F32) for i_ in range(2)]
        pEs_ = [A.alloc(f"pEs{i_}", [128, SB, 5], F32) for i_ in range(2)]
        pcnt = [0]
        shs = A.alloc("shs", [128, 8, SB], F32)
        sho = A.alloc("sho", [128, 8, SB], F32)
        dtmp_ = [A.alloc(f"dtmp{i_}", [128, TP], F32) for i_ in range(2)]
        dtmps_ = [A.alloc(f"dtmps{i_}", [128, SB, 4], F32) for i_ in range(2)]
        rmask = A.alloc("rmask", [128, TW], F32)
        tw = A.alloc("tw", [128, TW], BF16)
        sgb = A.alloc("sgb", [128, TW], BF16)
        P.memset(rmask.full(), 1.0, eng="pool")
        P.memset(v3(rmask.full(), CW)[:, :, 0:1], 0.0, eng="pool")
        for c in range(8):
            P.dma("sp", shs[:, c, :], dr["st_shift"][l, s * SB:(s + 1) * SB, c * 128:(c + 1) * 128].rearrange("b p -> p b"),
                  chan="st", allow_slow_non_contiguous=True)

        def proj_chunk(c, lhs_fn, xs):
            pb_ = pcnt[0] % 2
            pcnt[0] += 1
            pE, pEs, dtmp, dtmps = pE_[pb_], pEs_[pb_], dtmp_[pb_], dtmps_[pb_]
            P.memset(xs[:, TP:TW], 0.0, eng="pool")
            P.copy(pE[:, 0:1], self.shcar[:, l, c:c + 1], eng="pool")
            P.copy(pEs[:, :, 0], shs[:, c, :], eng="pool")

            def ev(pv, c0, n):
                npp = max(0, min(c0 + n, TP) - c0)
                if npp > 0:
                    P.act(pE[:, 1 + c0:1 + c0 + npp], pv[:, 0:npp], AF.Identity, bias=self.prm(l, "brw", c))
                if npp < n:
                    lo = c0 + npp - TP
                    assert lo % 4 == 0 and (n - npp) % 4 == 0
                    ps_ = pv[:, npp:n]
                    P.act(pEs[:, lo // 4:(lo + n - npp) // 4, 1:5], ps_.with_ap(ps_.ap.rearrange("p (b t) -> p b t", t=4)), AF.Identity,
                          bias=self.prm(l, "brw", c))
            self.dense(128, 8, lhs_fn, lambda k, c0, n: self.hT[:, k, c0:c0 + n], ev)
            P.copy(self.shcar[:, l, c:c + 1], pE[:, TP:TP + 1], eng="pool")
            P.copy(sho[:, c, :], pEs[:, :, 4], eng="pool")
            mu = self.prm(l, "mu", c)
            P.tt(dtmp.full(), pE[:, 0:TP], pE[:, 1:TP + 1], ALU.subtract, eng="pool")
            P.stt(xs[:, 0:TP], dtmp.full(), mu, pE[:, 1:TP + 1], ALU.mult, ALU.add)
            P.tt(dtmps.full(), pEs[:, :, 0:4], pEs[:, :, 1:5], ALU.subtract, eng="pool")
            xs_s = v3(xs[:, TP:TW], CW)[:, :, 0:4]
            P.stt(xs_s, dtmps.full(), mu, pEs[:, :, 1:5], ALU.mult, ALU.add)

        m1 = A.mark()
        xs6 = A.alloc("xs6", [128, TW], F32)
        xs7 = A.alloc("xs7", [128, TW], F32)
        sl = self.wget(ids["lora"])
        proj_chunk(6, lambda k: self.wview(sl, 8, 256, k, 0, 128), xs6)
        proj_chunk(7, lambda k: self.wview(sl, 8, 256, k, 128, 128), xs7)
        P.act(tw[0:64, :], xs6[0:64, :], AF.Tanh)
        P.copy(tw[64:128, :], xs6[64:128, :])
        P.act(sgb.full(), xs7.full(), AF.Sigmoid)
        A.release(m1)

        for p in range(2):
            m1 = A.mark()
            sl = self.wget(ids[f"pair{p}"])
            f32t = lambda nm: A.alloc(nm, [128, TW], F32)
            b16t = lambda nm: A.alloc(nm, [128, TW], BF16)
            xr, xk, xv = f32t("xr"), f32t("xk"), f32t("xv")
            lw, aa, gg = f32t("lw"), f32t("aa"), f32t("gg")
            kk, k2, bn, cs, S1, S2 = f32t("kk"), f32t("k2"), f32t("bn"), f32t("cs"), f32t("S1"), f32t("S2")
            KR = A.alloc("KR", [128, NU, 2, CW], BF16)
            kt, bt, kG, bG, vb, tb = b16t("kt"), b16t("bt"), b16t("kG"), b16t("bG"), b16t("vb"), b16t("tb")
            gc = A.alloc("gc", [128, NU], F32)
            TOK = A.alloc("TOK", [64, NG, 3, 2, 128], BF16)
            MT = A.alloc("MT", [64, NG, 2, 2, 2, CW], BF16)
            ML = A.alloc("ML", [64, NG, 2, CW], BF16)
            AV = A.alloc("AV", [64, NG, 128], F32)
            NM = [A.alloc(f"NM{j}", [64, NG, 2, 2, CW], BF16) for j in range(2)]
            XX = [A.alloc(f"XX{j}", [64, NG, 2, CW], BF16) for j in range(2)]
            Wsb = A.alloc("Wsb", [64, 2, 128], BF16)
            Upad = A.alloc("Upad", [64, 2, 2, 128], BF16)
            Hs = A.alloc("Hs", [128, 2, 128], F32)
            HT = A.alloc("HT", [128, 128], F32)
            ST = A.alloc("ST", [64, 2, 128], F32)
            P.memset(TOK.full(), 0.0, eng="pool")
            P.memset(Upad.full(), 0.0, eng="pool")
            P.memset(ST.full(), 0.0, eng="pool")

            proj_chunk(p, lambda k: self.wview(sl, 8, 384, k, 0, 128), xr)
            proj_chunk(2 + p, lambda k: self.wview(sl, 8, 384, k, 128, 128), xk)
            proj_chunk(4 + p, lambda k: self.wview(sl, 8, 384, k, 256, 128), xv)
            def prep(ua, ub):
                ca, cb = ua * CW, ub * CW
                C = slice(ca, cb)
                n = cb - ca
                assert n <= 512
                pv = self.ps(128, n)
                P.mm(pv, self.wup[0:64, l, p * 128:(p + 1) * 128], tw[0:64, C])
                P.act(lw[:, C], pv, AF.Sigmoid, bias=self.prm(l, "w0", p)); yield
                pv = self.ps(128, n)
                P.mm(pv, self.aup[64:128, l, p * 128:(p + 1) * 128], tw[64:128, C])
                P.act(aa[:, C], pv, AF.Sigmoid, bias=self.prm(l, "a0", p)); yield
                pv = self.ps(128, n)
                P.mm(pv, self.gup[:, l, p * 128:(p + 1) * 128], sgb[:, C])
                P.copy(gg[:, C], pv, eng="act"); yield
                P.ts(lw[:, C], lw[:, C], DEC_C, ALU.mult); yield
                if ub > NCH:
                    cs0 = max(ua, NCH) * CW
                    P.memset(v3(lw[:, cs0:cb], CW)[:, :, 4:CW], 0.0, eng="pool"); yield
                P.ts(kk[:, C], xk[:, C], self.prm(l, "kk", p), ALU.mult); yield
                P.act(tb[:, C], kk[:, C], AF.Square); yield
                pv = self.ps(128, n)
                P.mm(pv, self.onesbd.full(), tb[:, C])
                P.ts(S1[:, C], pv, 1e-18, ALU.max); yield
                P.act(S1[:, C], S1[:, C], AF.Ln); yield
                P.act(S1[:, C], S1[:, C], AF.Exp, scale=-0.5); yield
                P.tt(kk[:, C], kk[:, C], S1[:, C], ALU.mult); yield
                P.ts(S2[:, C], aa[:, C], self.prm(l, "ka", p), ALU.mult, self.prm(l, "ka1", p), ALU.add); yield
                P.tt(k2[:, C], xk[:, C], S2[:, C], ALU.mult); yield
                P.stt(bn[:, C], kk[:, C], -1.0, aa[:, C], ALU.mult, ALU.mult); yield
                P.stt(tb[:, C], xr[:, C], self.prm(l, "rk", p), k2[:, C], ALU.mult, ALU.mult); yield
                pv = self.ps(128, n)
                P.mm(pv, self.onesbd.full(), tb[:, C])
                P.tt(xk[:, C], pv, xv[:, C], ALU.mult); yield
                csv, lwv, rmv = cs[:, C], lw[:, C], rmask[:, C]
                P.add("dve", lambda e, csv=csv, lwv=lwv, rmv=rmv: e.tensor_tensor_scan(csv.ap, rmv.ap, lwv.ap, 0.0, ALU.mult, ALU.add),
                      [rmv, lwv], [csv]); yield
                P.act(S1[:, C], cs[:, C], AF.Exp); yield
                P.copy(gc[:, ua:ub], v3(S1[:, C], CW)[:, :, CW - 1], eng="pool"); yield
                P.tt(xr[:, C], xr[:, C], S1[:, C], ALU.mult); yield
                P.copy(KR[:, ua:ub, 1, :], v3(xr[:, C], CW), eng="act"); yield
                P.tt(S2[:, C], cs[:, C], lw[:, C], ALU.subtract); yield
                P.act(S2[:, C], S2[:, C], AF.Exp); yield
                P.tt(kk[:, C], kk[:, C], S2[:, C], ALU.mult); yield
                P.copy(KR[:, ua:ub, 0, :], v3(kk[:, C], CW), eng="act"); yield
                P.act(S1[:, C], cs[:, C], AF.Exp, scale=-1.0); yield
                P.tt(kt[:, C], k2[:, C], S1[:, C], ALU.mult); yield
                P.tt(bt[:, C], bn[:, C], S1[:, C], ALU.mult); yield
                for u in range(ua, ub):
                    P.act(S2[:, u * CW:(u + 1) * CW], cs[:, u * CW:(u + 1) * CW], AF.Exp, bias=cs[:, u * CW + CW - 1:u * CW + CW], scale=-1.0)
                yield
                P.tt(kG[:, C], k2[:, C], S2[:, C], ALU.mult); yield
                P.tt(bG[:, C], bn[:, C], S2[:, C], ALU.mult); yield
                P.copy(vb[:, C], xv[:, C], eng="act"); yield
            bonus = xk
            nstr = 2 if NU >= 4 else 1
            cuts = [round(i_ * NU / nstr) for i_ in range(nstr + 1)]
            run_rr([prep(cuts[i_], cuts[i_ + 1]) for i_ in range(nstr)])
            yT = lw

            for g0 in range(0, NU, NG):
                ng = min(NG, NU - g0)
                for i in range(ng):
                    u = g0 + i
                    cu = slice(u * CW, (u + 1) * CW)
                    pb = self.ps(64, 192).bitcast(BF16)
                    for w_, src in enumerate((vb, kG, bG)):
                        P.transpose(pb[:, w_ * 128:(w_ + 1) * 128], src[:, cu], self.ident.full())
                    tv = TOK[:, i, :, :, :]
                    flat = TOK.base[0:64, i, :, :, :]
                    dst_ap = flat.rearrange("p w h x -> p (w h x)").rearrange("p (w a b) -> p w a b", w=3, a=4, b=64)[:, :, 0:4:3, :]
                    P.copy(tv.with_ap(dst_ap), pb.with_ap(pb.ap.rearrange("p (w h b) -> p w h b", w=3, h=2)), eng="act")
                for i0 in range(0, ng, 2):
                    n2 = min(2, ng - i0)
                    pbank = [self.ps(64, 512), self.ps(64, 512)]
                    for ii in range(n2):
                        i = i0 + ii
                        u = g0 + i
                        cu = slice(u * CW, (u + 1) * CW)
                        for hh in range(2):
                            hs = slice(hh * 64, (hh + 1) * 64)
                            P.mm(pbank[hh][:, ii * 256:ii * 256 + 128], bt[hs, cu], KR[hs, u, :, :])
                            P.mm(pbank[hh][:, ii * 256 + 128:ii * 256 + 256], kt[hs, cu], KR[hs, u, :, :])
                    for hh in range(2):
                        dst = MT[:, i0:i0 + n2, hh, :, :, :]
                        P.tt(dst, pbank[hh][:, 0:n2 * 256].with_ap(pbank[hh][:, 0:n2 * 256].ap.rearrange("p (i w a t) -> p i w a t", i=n2, w=2, a=2)),
                             self.trimx[:, 0:n2, :, :, :], ALU.mult, eng=("dve" if hh == 0 else "dve"))
                pm = [self.ps(64, ng * 64), self.ps(64, ng * 64)]
                for i in range(ng):
                    u = g0 + i
                    cu = slice(u * CW, (u + 1) * CW)
                    for hh in range(2):
                        hs = slice(hh * 64, (hh + 1) * 64)
                        P.mm(pm[hh][:, i * 64:(i + 1) * 64], KR[hs, u, 0, :], bt[hs, cu])
                for hh in range(2):
                    P.tt(ML[:, 0:ng, hh, :], pm[hh].with_ap(pm[hh].ap.rearrange("p (i s) -> p i s", i=ng)), self.trilox[:, 0:ng, :], ALU.mult)
                for i0 in range(0, ng, 4):
                    n4 = min(4, ng - i0)
                    pv_ = self.ps(64, n4 * 128)
                    for ii in range(n4):
                        i = i0 + ii
                        for hh in range(2):
                            P.mm(pv_[:, ii * 128 + hh * 64:ii * 128 + hh * 64 + 64], MT[:, i, hh, 1, 0, :], TOK[:, i, 0, hh, hh * 64:(hh + 1) * 64])
                    P.copy(AV[:, i0:i0 + n4, :], pv_.with_ap(pv_.ap.rearrange("p (i x) -> p i x", i=n4)), eng="act")
                Nv = lambda j, i, hh: (MT[:, i, hh, 0, 0, :] if j == 0 else NM[j % 2][:, i, hh, 0, :])
                Mv = lambda j, i, hh: (ML[:, i, hh, :] if j == 0 else NM[j % 2][:, i, hh, 1, :])
                P.tt(XX[0][:, 0:ng, :, :], MT[:, 0:ng, :, 0, 0, :], self.identx[:, 0:ng, :, :], ALU.add)
                for j in range(1, 6):
                    for i0 in range(0, ng, 2):
                        n2 = min(2, ng - i0)
                        pn = self.ps(64, n2 * 256)
                        for ii in range(n2):
                            i = i0 + ii
                            for hh in range(2):
                                o = ii * 256 + hh * 128
                                if j <= 4:
                                    P.mm(pn[:, o:o + 64], Mv(j - 1, i, hh), Nv(j - 1, i, hh))
                                P.mm(pn[:, o + 64:o + 128], Nv(j - 1, i, hh), Mv(j - 1, i, hh))
                        src = pn.with_ap(pn.ap.rearrange("p (i h a t) -> p i h a t", i=n2, h=2, a=2))
                        if j <= 4:
                            P.copy(NM[j % 2][:, i0:i0 + n2, :, :, :], src, eng="act")
                        else:
                            P.copy(NM[j % 2][:, i0:i0 + n2, :, 1, :], src[:, :, :, 1, :], eng="act")
                    for i0 in range(0, ng, 4):
                        n4 = min(4, ng - i0)
                        px = self.ps(64, n4 * 128)
                        for ii in range(n4):
                            i = i0 + ii
                            for hh in range(2):
                                P.mm(px[:, ii * 128 + hh * 64:ii * 128 + hh * 64 + 64], Mv(j, i, hh), XX[(j - 1) % 2][:, i, hh, :])
                        P.tt(XX[j % 2][:, i0:i0 + n4, :, :], px.with_ap(px.ap.rearrange("p (i h t) -> p i h t", i=n4, h=2)),
                             XX[(j - 1) % 2][:, i0:i0 + n4, :, :], ALU.add)
                TT = XX[5 % 2]
                for i in range(ng):
                    u = g0 + i
                    cu = slice(u * CW, (u + 1) * CW)
                    is_s = u >= NCH
                    if not is_s:
                        Hcur = self.Hst[:, l, p, u % 2, :]
                        Hnext = self.Hst[:, l, p, (u + 1) % 2, :]
                    else:
                        b = u - NCH
                        gb_ = s * SB + b
                        for hh in range(2):
                            P.dma("sp", ST[:, hh, hh * 64:(hh + 1) * 64], dr["st_wkv"][l, gb_, 2 * p + hh], chan="st")
                        ph = self.ps(128, 128)
                        for hh in range(2):
                            P.transpose(ph[:, hh * 64:(hh + 1) * 64], ST[:, hh, :], self.identf[0:64, 0:64])
                        P.copy(Hs[:, 0, :], ph)
                        Hcur = Hs[:, 0, :]
                        Hnext = Hs[:, 1, :]
                    pp_ = u % 2
                    pw = self.ps(64, 128)
                    P.mm(pw, kk[:, cu], Hcur)
                    P.tt(Wsb[:, pp_, :], pw, AV[:, i, :], ALU.add)
                    pu = self.ps(64, 128)
                    for hh in range(2):
                        P.mm(pu[:, hh * 64:(hh + 1) * 64], TT[:, i, hh, :], Wsb[:, pp_, hh * 64:(hh + 1) * 64])
                    ud = Upad[:, pp_, :, :]
                    ud_ap = Upad.base[0:64, pp_, :, :].rearrange("p h x -> p (h x)").rearrange("p (a b) -> p a b", a=4, b=64)[:, 0:4:3, :]
                    P.copy(ud.with_ap(ud_ap), pu.with_ap(pu.ap.rearrange("p (h b) -> p h b", h=2)), eng="act")
                    py = self.ps(128, 64)
                    P.mm(py, Hcur, xr[:, cu], start=True, stop=False)
                    for hh in range(2):
                        P.mm(py, TOK[:, i, 0, hh, :], MT[:, i, hh, 1, 1, :], start=False, stop=False)
                    for hh in range(2):
                        P.mm(py, Upad[:, pp_, hh, :], MT[:, i, hh, 0, 1, :], start=False, stop=(hh == 1))
                    P.copy(yT[:, cu], py, eng="act")
                    ph2 = self.ps(128, 128)
                    for hh in range(2):
                        P.mm(ph2, TOK[:, i, 1, hh, :], TOK[:, i, 0, hh, :], start=(hh == 0), stop=False)
                    for hh in range(2):
                        P.mm(ph2, TOK[:, i, 2, hh, :], Upad[:, pp_, hh, :], start=False, stop=(hh == 1))
                    P.stt(Hnext, Hcur, gc[:, u:u + 1], ph2, ALU.mult, ALU.add)
                    if is_s:
                        self.wkv_out(Hnext, HT, dr["wkvs"][l, gb_], p)
            if last_seg:
                self.wkv_out(self.Hst[:, l, p, 0, :], HT, dr["wkvp"][l], p)

            def gnorm(ua, ub):
                ca, cb = ua * CW, ub * CW
                C = slice(ca, cb)
                n = cb - ca
                P.copy(tb[:, C], yT[:, C], eng="act"); yield
                P.act(kt[:, C], yT[:, C], AF.Square); yield
                p1 = self.ps(128, n)
                P.mm(p1, self.onesbd.full(), tb[:, C])
                p2 = self.ps(128, n)
                P.mm(p2, self.onesbd.full(), kt[:, C])
                P.ts(S1[:, C], p1, 1.0 / 64, ALU.mult); yield
                P.tt(S2[:, C], S1[:, C], S1[:, C], ALU.mult); yield
                P.stt(S2[:, C], p2, 1.0 / 64, S2[:, C], ALU.mult, ALU.subtract); yield
                P.ts(S2[:, C], S2[:, C], 0.0, ALU.max); yield
                P.act(S2[:, C], S2[:, C], AF.Ln, bias=self.epsc[:, 1:2], scale=1.0); yield
                P.act(S2[:, C], S2[:, C], AF.Exp, scale=-0.5); yield
                P.tt(yT[:, C], yT[:, C], S1[:, C], ALU.subtract); yield
                P.tt(yT[:, C], yT[:, C], S2[:, C], ALU.mult); yield
                P.ts(yT[:, C], yT[:, C], self.prm(l, "lnw", p), ALU.mult, self.prm(l, "lnb", p), ALU.add); yield
                P.tt(yT[:, C], yT[:, C], bonus[:, C], ALU.add); yield
            run_rr([gnorm(cuts[i_], cuts[i_ + 1]) for i_ in range(nstr)])
            P.tt(self.mixT[:, 4 + p, 0:TP], yT[:, 0:TP], gg[:, 0:TP], ALU.mult)
            ms = self.mixT[:, 4 + p, TP:T]
            P.tt(ms.with_ap(ms.ap.rearrange("p (b t) -> p b t", t=4)), v3(yT[:, TP:TW], CW)[:, :, 0:4], v3(gg[:, TP:TW], CW)[:, :, 0:4], ALU.mult)
            A.release(m1)

        for c in range(8):
            P.dma("sp", dr["shs"][l, s * SB:(s + 1) * SB, c * 128:(c + 1) * 128].rearrange("b p -> p b"), sho[:, c, :], chan="outs",
                  allow_slow_non_contiguous=True)
        if last_seg:
            P.dma("sp", dr["shp"][l].rearrange("(c p) -> p c", p=128), self.shcar[:, l, :], chan="outs", allow_slow_non_contiguous=True)

    def wkv_out(self, H, HT, dst, p):
        P = self.P
        pt = self.ps(128, 128)
        P.transpose(pt, H, self.identf.full())
        P.copy(HT.full(), pt)
        for hh in range(2):
            P.dma("sp", dst[2 * p + hh], HT[hh * 64:(hh + 1) * 64, hh * 64:(hh + 1) * 64], chan="outs")

WNAMES = [("norm_mix_pre", [D]), ("norm_mix_post", [D]), ("norm_ffn_pre", [D]), ("norm_ffn_post", [D]),
          ("w_in", [D, NIN]), ("b_in", [NIN]), ("attn_sinks", [8]), ("rwkv_mu", [1024]), ("rwkv_w0", [256]),
          ("rwkv_w_up", [64, 256]), ("rwkv_a0", [256]), ("rwkv_a_up", [64, 256]), ("rwkv_g_up", [128, 256]),
          ("rwkv_k_k", [256]), ("rwkv_k_a", [256]), ("rwkv_r_k", [4, 64]), ("rwkv_ln_w", [256]), ("rwkv_ln_b", [256]),
          ("lru_conv_w", [4, 256]), ("lru_conv_b", [256]), ("lru_w_a", [4, 64, 64]), ("lru_b_a", [256]),
          ("lru_w_i", [4, 64, 64]), ("lru_b_i", [256]), ("lru_L", [256]), ("w_out", [D, D]), ("b_out", [D]),
          ("ffn_w_gate", [D, DFF]), ("ffn_w_up", [D, DFF]), ("ffn_w_down", [DFF, D]), ("ple_w", [PLE, D]),
          ("ple_gate_w", [D, D])]


def build(cfg):
    nc = bass.Bass("TRN2", target_bir_lowering=False)
    dr = {}
    L, SEQ, SBC = cfg.L, cfg.SEQ, cfg.SBC

    def inp(name, shape):
        dr[name] = nc.dram_tensor(name, list(shape), F32, kind="ExternalInput").ap()

    def outp(name, shape):
        dr[name] = nc.dram_tensor(name, list(shape), F32, kind="ExternalOutput").ap()
    inp("xp", [SEQ, D]); inp("xs", [SBC * 4, D]); inp("ck", [L, SBC, 128, 128]); inp("cv", [L, SBC, 128, 128])
    inp("st_shift", [L, SBC, 1024]); inp("st_wkv", [L, SBC, 4, 64, 64]); inp("st_conv", [L, SBC, 3, 256])
    inp("st_lru", [L, SBC, 256]); inp("pp", [L, SEQ, PLE]); inp("psm", [L, SBC * 4, PLE])
    for n, sh in WNAMES:
        inp(n, [L] + sh)
    outp("yp", [SEQ, D]); outp("ys", [SBC * 4, D]); outp("kp", [L, 128, 128]); outp("vp", [L, 128, 128])
    outp("shp", [L, 1024]); outp("wkvp", [L, 4, 64, 64]); outp("convp", [L, 3, 256]); outp("lrup", [L, 256])
    outp("ks", [L, SBC, 128, 128]); outp("vs", [L, SBC, 128, 128]); outp("shs", [L, SBC, 1024])
    outp("wkvs", [L, SBC, 4, 64, 64]); outp("convs", [L, SBC, 3, 256]); outp("lrus", [L, SBC, 256])
    P = Prog(nc)
    K = Kern(P, cfg, dr)
    K.setup()
    ids = [[K.register_layer(l) for l in range(L)] for s in range(cfg.NSEG)]
    for s in range(cfg.NSEG):
        K.load_segment(s)
        for l in range(L):
            K.layer(l, s, ids[s][l])
        K.store_segment(s)
    P.emit()
    return nc, P, K


_CACHE = {}


def run(cfg, inputs):
    key = (cfg.L, cfg.SEQ, cfg.NSEG, cfg.SBC)
    if key not in _CACHE:
        _CACHE[key] = build(cfg)[0]
    nc = _CACHE[key]
    L, SBC = cfg.L, cfg.SBC
    f = lambda a: np.ascontiguousarray(np.asarray(a, dtype=np.float32))
    in_maps = []
    for i in range(8):
        b = i % 4
        sl = slice(i * SBC, (i + 1) * SBC)
        m = {"xp": f(inputs["x_prompt"][b]), "xs": f(inputs["x_sample"][sl]).reshape(SBC * 4, D),
             "ck": f(inputs["cache_k"][:, sl]).reshape(L, SBC, 128, 128), "cv": f(inputs["cache_v"][:, sl]).reshape(L, SBC, 128, 128),
             "st_shift": f(inputs["state_shift"][:, sl]), "st_wkv": f(inputs["state_wkv"][:, sl]),
             "st_conv": f(inputs["state_conv"][:, sl]), "st_lru": f(inputs["state_lru"][:, sl]),
             "pp": f(inputs["p_prompt"][:, b]), "psm": f(inputs["p_sample"][:, sl]).reshape(L, SBC * 4, PLE)}
        for n, sh in WNAMES:
            m[n] = f(inputs[n])
        in_maps.append(m)
    res = run_bass_kernel_spmd(nc, in_maps, core_ids=list(range(8)))
    R = res.results
    DECB = 8 * SBC
    cat_p = lambda k: np.stack([R[i][k] for i in range(4)], axis=0)
    cat_pl = lambda k: np.stack([R[i][k] for i in range(4)], axis=1)
    cat_s = lambda k: np.concatenate([R[i][k] for i in range(8)], axis=1)
    yp = cat_p("yp")
    ys = np.concatenate([R[i]["ys"] for i in range(8)], axis=0).reshape(DECB, 4, D)
    outs = (yp, ys, cat_pl("kp").reshape(L, 4, 128, 2, 64), cat_pl("vp").reshape(L, 4, 128, 2, 64), cat_pl("shp"), cat_pl("wkvp"),
            cat_pl("convp"), cat_pl("lrup"), cat_s("ks").reshape(L, DECB, 128, 2, 64), cat_s("vs").reshape(L, DECB, 128, 2, 64),
            cat_s("shs"), cat_s("wkvs"), cat_s("convs"), cat_s("lrus"))
    return tuple(np.ascontiguousarray(o, dtype=np.float32) for o in outs)


def kernel(**inputs):
    cfg = Cfg(L=4, SEQ=4096, NSEG=8, DECB=128)
    return run(cfg, inputs)
```

```python
import math
import numpy as np
from contextlib import ExitStack
import concourse.bass as bass
import concourse.mybir as mybir
from concourse.bass_utils import run_bass_kernel_spmd


F32 = mybir.dt.float32
BF16 = mybir.dt.bfloat16
AF = mybir.ActivationFunctionType
ALU = mybir.AluOpType
AX = mybir.AxisListType

_DTSIZE = {F32: 4, BF16: 2, mybir.dt.int32: 4, mybir.dt.uint32: 4}

SAME_ENGINE_SYNC = True
SEM_ROLL = 30000


class View:
    __slots__ = ("tile", "ap", "p0", "p1", "b0", "b1")

    def __init__(self, tile, ap, p0, p1, b0, b1):
        self.tile = tile
        self.ap = ap
        self.p0, self.p1, self.b0, self.b1 = p0, p1, b0, b1

    def with_ap(self, ap):
        return View(self.tile, ap, self.p0, self.p1, self.b0, self.b1)

    def bitcast(self, dt):
        return View(self.tile, self.ap.bitcast(dt), self.p0, self.p1, self.b0, self.b1)

    def __getitem__(self, idx):
        return View(self.tile, self.ap[idx], self.p0, self.p1, self.b0, self.b1)


class Tile:
    _n = 0

    def __init__(self, handle, name, shape, dtype, space):
        self.h = handle
        self.name = name
        self.shape = list(shape)
        self.dtype = dtype
        self.space = space
        self.esz = _DTSIZE[dtype]
        self.id = Tile._n
        Tile._n += 1
        st = [1] * len(shape)
        for i in range(len(shape) - 2, 0, -1):
            st[i] = st[i + 1] * shape[i + 1]
        self.strides = st
        self.w = []
        self.r = []

    def __getitem__(self, idx):
        if not isinstance(idx, tuple):
            idx = (idx,)
        idx = list(idx) + [slice(None)] * (len(self.shape) - len(idx))
        lo = 0
        hi = 0
        p0, p1 = 0, self.shape[0]
        for d, (ix, n) in enumerate(zip(idx, self.shape)):
            if isinstance(ix, int):
                if ix < 0:
                    ix += n
                s, e, stp = ix, ix + 1, 1
            else:
                s, e, stp = ix.indices(n)
            assert 0 <= s < e <= n, (self.name, idx, self.shape)
            last = s + ((e - 1 - s) // stp) * stp
            if d == 0:
                p0, p1 = s, last + 1
            else:
                lo += s * self.strides[d]
                hi += last * self.strides[d]
        b0, b1 = lo * self.esz, (hi + 1) * self.esz
        if self.space == "psum":
            p0, p1 = 0, 128
            b0 = b0 // 2048 * 2048
            b1 = (b1 + 2047) // 2048 * 2048
        return View(self, self.h[tuple(idx)], p0, p1, b0, b1)

    def full(self):
        return self[tuple(slice(None) for _ in self.shape)]


class VTile:
    def __init__(self, arena, off_bytes, shape, dtype, name=""):
        self.arena = arena
        self.name = name
        self.shape = list(shape)
        self.dtype = dtype
        self.esz = _DTSIZE[dtype]
        self.off = off_bytes
        n = 1
        for d in shape[1:]:
            n *= d
        self.nbytes = n * self.esz
        assert off_bytes % 4 == 0
        n4 = (self.nbytes + 3) // 4
        base = arena.h[0:shape[0], off_bytes // 4: off_bytes // 4 + n4]
        if dtype != arena.dtype:
            base = base.bitcast(dtype)
        if self.nbytes % 4 != 0:
            base = base[:, 0:n]
        if len(shape) > 2:
            names = [f"d{i}" for i in range(1, len(shape))]
            kw = {nm: shape[i + 1] for i, nm in enumerate(names)}
            base = base.rearrange("p (" + " ".join(names) + ") -> p " + " ".join(names), **kw)
        self.base = base
        st = [1] * len(shape)
        for i in range(len(shape) - 2, 0, -1):
            st[i] = st[i + 1] * shape[i + 1]
        self.strides = st

    def __getitem__(self, idx):
        if not isinstance(idx, tuple):
            idx = (idx,)
        idx = list(idx) + [slice(None)] * (len(self.shape) - len(idx))
        lo = 0
        hi = 0
        p0, p1 = 0, self.shape[0]
        for d, (ix, n) in enumerate(zip(idx, self.shape)):
            if isinstance(ix, int):
                if ix < 0:
                    ix += n
                s, e, stp = ix, ix + 1, 1
            else:
                s, e, stp = ix.indices(n)
            assert 0 <= s < e <= n, (self.name, idx, self.shape)
            last = s + ((e - 1 - s) // stp) * stp
            if d == 0:
                p0, p1 = s, last + 1
            else:
                lo += s * self.strides[d]
                hi += last * self.strides[d]
        return View(self.arena, self.base[tuple(idx)], p0, p1, self.off + lo * self.esz, self.off + (hi + 1) * self.esz)

    def full(self):
        return self[tuple(slice(None) for _ in self.shape)]


class Arena:
    def __init__(self, tile):
        self.tile = tile
        self.top = 0
        self.cap = tile.shape[1] * tile.esz
        self.peak = 0

    def alloc(self, name, shape, dtype):
        n = 1
        for d in shape[1:]:
            n *= d
        nb = (n * _DTSIZE[dtype] + 31) // 32 * 32
        assert self.top + nb <= self.cap, f"arena overflow {name} {self.top}+{nb}>{self.cap}"
        v = VTile(self.tile, self.top, shape, dtype, name)
        self.top += nb
        self.peak = max(self.peak, self.top)
        return v

    def mark(self):
        return self.top

    def release(self, m):
        self.top = m


def _ov(a, b):
    return a[0] < b[1] and b[0] < a[1] and a[2] < b[3] and b[2] < a[3]


def _cov(a, b):
    return a[0] <= b[0] and a[1] >= b[1] and a[2] <= b[2] and a[3] >= b[3]


class Op:
    __slots__ = ("eng", "fn", "deps", "signal", "chan", "idx", "cnt", "semk", "name", "isdma")

    def __init__(self, eng, fn, chan, name):
        self.eng = eng
        self.fn = fn
        self.deps = set()
        self.signal = False
        self.chan = chan
        self.cnt = None
        self.semk = None
        self.name = name
        self.isdma = chan is not None


class Prog:
    ENGS = ("pe", "act", "dve", "pool", "sp")

    def __init__(self, nc):
        self.nc = nc
        self.ops = []
        self.stack = ExitStack()
        self.tiles = []
        self.chan_count = {}

    def sbuf(self, name, shape, dtype):
        h = self.stack.enter_context(self.nc.sbuf_tensor(name, list(shape), dtype))
        t = Tile(h, name, shape, dtype, "sbuf")
        self.tiles.append(t)
        return t

    def psum(self, name, shape, dtype):
        h = self.stack.enter_context(self.nc.psum_tensor(name, list(shape), dtype))
        t = Tile(h, name, shape, dtype, "psum")
        self.tiles.append(t)
        return t

    def add(self, eng, fn, reads=(), writes=(), chan=None, name=""):
        op = Op(eng, fn, chan, name)
        op.idx = len(self.ops)
        self.ops.append(op)
        isdma = chan is not None
        for v in reads:
            if v is None:
                continue
            t = v.tile
            reg = (v.p0, v.p1, v.b0, v.b1)
            for w in t.w:
                if _ov(w, reg):
                    op.deps.add(w[4])
            if not isdma:
                t.r = [r for r in t.r if not (r[5] == eng and _cov(reg, r))]
            t.r.append((v.p0, v.p1, v.b0, v.b1, op.idx, None if isdma else eng))
        for v in writes:
            if v is None:
                continue
            t = v.tile
            reg = (v.p0, v.p1, v.b0, v.b1)
            for w in t.w:
                if _ov(w, reg):
                    op.deps.add(w[4])
            for r in t.r:
                if _ov(r, reg) and r[4] != op.idx:
                    op.deps.add(r[4])
            t.w = [w for w in t.w if not _cov(reg, w)]
            t.r = [r for r in t.r if not _cov(reg, r) or r[4] == op.idx]
            t.w.append((v.p0, v.p1, v.b0, v.b1, op.idx))
        op.deps.discard(op.idx)
        return op

    def dma(self, q, out, in_, chan, reads=(), writes=(), **kw):
        oa = out.ap if isinstance(out, View) else out
        ia = in_.ap if isinstance(in_, View) else in_
        rd = list(reads) + ([in_] if isinstance(in_, View) else [])
        wr = list(writes) + ([out] if isinstance(out, View) else [])
        return self.add(q, lambda e: e.dma_start(out=oa, in_=ia, **kw), rd, wr, chan=chan, name="dma")

    def mm(self, out, lhsT, rhs, start=True, stop=True, **kw):
        return self.add(
            "pe",
            lambda e: e.matmul(out.ap, lhsT.ap, rhs.ap, start=start, stop=stop, **kw),
            [lhsT, rhs] + ([] if start else [out]),
            [out],
            name="mm",
        )

    def transpose(self, out, in_, ident):
        return self.add("pe", lambda e: e.transpose(out.ap, in_.ap, ident.ap), [in_, ident], [out], name="tr")

    def act(self, out, in_, func, bias=None, scale=None, accum=None, eng="act"):
        kw = {}
        rd = [in_]
        if bias is not None:
            if isinstance(bias, View):
                kw["bias"] = bias.ap
                rd.append(bias)
            else:
                kw["bias"] = bias
        if scale is not None:
            if isinstance(scale, View):
                kw["scale"] = scale.ap
                rd.append(scale)
            else:
                kw["scale"] = scale
        wr = [out]
        if accum is not None:
            kw["accum_out"] = accum.ap
            wr.append(accum)
        return self.add(eng, lambda e: e.activation(out.ap, in_.ap, func, **kw), rd, wr, name="act")

    def tt(self, out, in0, in1, op, eng="dve"):
        return self.add(eng, lambda e: e.tensor_tensor(out.ap, in0.ap, in1.ap, op), [in0, in1], [out], name="tt")

    def ts(self, out, in0, s1, op0, s2=None, op1=None, eng="dve", accum=None):
        rd = [in0]
        a1 = s1
        if isinstance(s1, View):
            a1 = s1.ap
            rd.append(s1)
        a2 = s2
        if isinstance(s2, View):
            a2 = s2.ap
            rd.append(s2)
        kw = {}
        wr = [out]
        if accum is not None:
            kw["accum_out"] = accum.ap
            wr.append(accum)
        if op1 is None:
            return self.add(eng, lambda e: e.tensor_scalar(out.ap, in0.ap, a1, None, op0, **kw), rd, wr, name="ts")
        return self.add(eng, lambda e: e.tensor_scalar(out.ap, in0.ap, a1, a2, op0, op1, **kw), rd, wr, name="ts")

    def stt(self, out, in0, scalar, in1, op0, op1, eng="dve"):
        rd = [in0, in1]
        a = scalar
        if isinstance(scalar, View):
            a = scalar.ap
            rd.append(scalar)
        return self.add(eng, lambda e: e.scalar_tensor_tensor(out.ap, in0.ap, a, in1.ap, op0, op1), rd, [out], name="stt")

    def copy(self, out, in_, eng="dve"):
        if eng == "act":
            return self.act(out, in_, AF.Copy)
        return self.add(eng, lambda e: e.tensor_copy(out.ap, in_.ap), [in_], [out], name="copy")

    def memset(self, out, val, eng="dve"):
        return self.add(eng, lambda e: e.memset(out.ap, val), [], [out], name="memset")

    DMA_POOL = {"sp": 24, "pool": 12, "act": 4}

    def emit(self, final_chans=()):
        nc = self.nc
        ops = self.ops
        for op in ops:
            for d in op.deps:
                ops[d].signal = True
        per = {e: [] for e in self.ENGS}
        for op in ops:
            per[op.eng].append(op)
        nsem = {}
        ndma = {}
        for e in self.ENGS:
            c = 0
            nd = 0
            K = self.DMA_POOL.get(e, 4)
            for op in per[e]:
                if op.isdma:
                    op.semk = ("dma", e, nd % K)
                    op.cnt = 16 * (nd // K + 1)
                    nd += 1
                elif op.signal:
                    k = c // SEM_ROLL
                    op.semk = (e, k)
                    op.cnt = c % SEM_ROLL + 1
                    c += 1
            nsem[e] = (c + SEM_ROLL - 1) // SEM_ROLL
            ndma[e] = nd
        sems = {}
        for e in self.ENGS:
            for k in range(max(nsem[e], 1)):
                sems[(e, k)] = self.stack.enter_context(nc.semaphore(f"s_{e}_{k}"))
            K = self.DMA_POOL.get(e, 4)
            for k in range(min(K, ndma[e])):
                sems[("dma", e, k)] = self.stack.enter_context(nc.semaphore(f"d_{e}_{k}"))
        self.nwaits = 0
        block = self.stack.enter_context(nc.Block())

        def section(ename):
            def body(eng):
                waited = {}
                K = self.DMA_POOL.get(ename, 4)
                nd = 0
                for op in per[ename]:
                    need = {}
                    for d in op.deps:
                        y = ops[d]
                        if (not y.isdma) and y.eng == ename and (ename == "pe" or not SAME_ENGINE_SYNC):
                            continue
                        k = y.semk
                        if y.cnt > need.get(k, 0):
                            need[k] = y.cnt
                    if op.isdma and nd >= K:
                        k = ("dma", ename, nd % K)
                        c = 16 * (nd // K)
                        if c > need.get(k, 0):
                            need[k] = c
                    for k, c in need.items():
                        if waited.get(k, 0) >= c:
                            continue
                        if k[0] != "dma":
                            later = any(kk[0] == k[0] and kk[1] > k[1] for kk in waited if kk[0] != "dma")
                            if later:
                                continue
                        eng.wait_ge(sems[k], c)
                        self.nwaits += 1
                        waited[k] = c
                    ins = op.fn(eng)
                    if op.isdma:
                        ins.then_inc(sems[op.semk], 16)
                        nd += 1
                    elif op.signal:
                        ins.then_inc(sems[op.semk], 1)
                for k in range(min(K, nd)):
                    cnt = 16 * ((nd - 1 - k) // K + 1)
                    if waited.get(("dma", ename, k), 0) < cnt:
                        eng.wait_ge(sems[("dma", ename, k)], cnt)
            return body

        block.tensor(section("pe"))
        block.scalar(section("act"))
        block.vector(section("dve"))
        block.gpsimd(section("pool"))
        block.sync(section("sp"))
        self.stack.close()


D = 1024
NIN = 2304
DFF = 2816
PLE = 256
RMS_EPS = 1e-6
GN_EPS = 64e-5
NEG = -30000.0


class Cfg:
    def __init__(self, L=4, SEQ=4096, NSEG=4, DECB=128):
        self.L = L
        self.SEQ = SEQ
        self.NSEG = NSEG
        self.TP = SEQ // NSEG
        self.SBC = DECB // 8
        self.SB = self.SBC // NSEG
        self.NS = self.SB * 4
        self.T = self.TP + self.NS
        self.NB = self.TP // 128
        self.NCH = self.TP // 64
        assert self.TP % 128 == 0 and self.SBC % NSEG == 0
        nblk = -(-self.T // 512)
        w = -(-self.T // nblk)
        w = -(-w // 4) * 4
        blks = []
        c = 0
        while c < self.T:
            n = min(w, self.T - c)
            blks.append((c, n))
            c += n
        self.blks = blks


PC = {}
_c = 0
for _n, _w in [("gpre", 8), ("gpost", 8), ("gfpre", 8), ("gfpost", 8), ("bq", 4), ("bkd", 2), ("brw", 8), ("blx", 2),
               ("blg", 2), ("mu", 8), ("w0", 2), ("a0", 2), ("kk", 2), ("ka", 2), ("rk", 2), ("lnw", 2), ("lnb", 2),
               ("cw", 8), ("cb", 2), ("ba", 2), ("bi", 2), ("Lp", 2), ("bout", 8), ("c8", 2), ("ka1", 2)]:
    PC[_n] = (_c, _w)
    _c += _w
NPRM = _c


class KernBase:
    def __init__(self, P, cfg, dr):
        self.P = P
        self.cfg = cfg
        self.dr = dr
        self.nc = P.nc
        self._bank = 0
        self.wq = []
        self.wq_issued = 0
        self.wq_slots = {}

    def bank(self, n=1):
        if self._bank + n > 8:
            self._bank = 0
        b = self._bank
        self._bank = (self._bank + n) % 8
        return b * 512

    def ps(self, p, ncols, nb=1, c0=None):
        if c0 is None:
            c0 = self.bank(nb)
        return self.PS[0:p, c0:c0 + ncols]

    def prm(self, l, name, i=0, n=1, p0=0, p1=128):
        c, w = PC[name]
        return self.PRM[p0:p1, l, c + i:c + i + n]

    def setup(self):
        P, cfg, dr = self.P, self.cfg, self.dr
        L, T = cfg.L, cfg.T
        self.PS = P.psum("PS", [128, 4096], F32)
        self.ident = P.sbuf("ident", [128, 128], BF16)
        self.identf = P.sbuf("identf", [128, 128], F32)
        self.ones = P.sbuf("ones", [128, 128], BF16)
        self.onesbd = P.sbuf("onesbd", [128, 128], BF16)
        self.maskb = P.sbuf("maskb", [128, 256], F32)
        self.maskf = P.sbuf("maskf", [128, 256], F32)
        self.trim = P.sbuf("trim", [64, 2, 64], F32)
        self.trilo = P.sbuf("trilo", [64, 64], F32)
        self.PRM = P.sbuf("PRM", [128, L, NPRM], F32)
        self.bkv = P.sbuf("bkv", [128, L, 256], F32)
        self.sinkb = P.sbuf("sinkb", [128, L, 8], F32)
        self.wup = P.sbuf("wup", [128, L, 256], BF16)
        self.aup = P.sbuf("aup", [128, L, 256], BF16)
        self.gup = P.sbuf("gup", [128, L, 256], BF16)
        self.wabd = P.sbuf("wabd", [128, L, 2, 128], BF16)
        self.wibd = P.sbuf("wibd", [128, L, 2, 128], BF16)
        self.epsc = P.sbuf("epsc", [128, 2], F32)
        self.kcar = P.sbuf("kcar", [128, L, 2, 128], BF16)
        self.vcar = P.sbuf("vcar", [128, L, 512], BF16)
        self.shcar = P.sbuf("shcar", [128, L, 8], F32)
        self.Hst = P.sbuf("Hst", [128, L, 2, 2, 128], F32)
        self.cvcar = P.sbuf("cvcar", [128, L, 2, 3], F32)
        self.hcar = P.sbuf("hcar", [128, L, 2], F32)
        self.xT = P.sbuf("xT", [128, 8, T], F32)
        self.hT = P.sbuf("hT", [128, 8, T], BF16)
        self.mixT = P.sbuf("mixT", [128, 8, T], BF16)
        self.rstd = P.sbuf("rstd", [128, T], F32)
        self.lntmp = P.sbuf("lntmp", [128, 512], F32)
        self.stage = [P.sbuf(f"stage{i}", [128, 1024], F32) for i in range(2)]
        self.NSLOT = 4
        self.wslot = [P.sbuf(f"wslot{i}", [128, 4096], BF16) for i in range(self.NSLOT)]
        self.AR = P.sbuf("arena", [128, self.ARENA_BYTES // 4], F32)
        self.arena = Arena(self.AR)

        P.memset(self.identf.full(), 1.0)
        idf = self.identf.full()
        P.add("pool", lambda e: e.affine_select(idf.ap, idf.ap, [[-1, 128]], ALU.is_equal, 0.0, base=0, channel_multiplier=1),
              [idf], [idf], name="ident")
        P.copy(self.ident.full(), idf)
        P.memset(self.ones.full(), 1.0)
        P.memset(self.onesbd.full(), 0.0)
        P.memset(self.onesbd[0:64, 0:64], 1.0)
        P.memset(self.onesbd[64:128, 64:128], 1.0)
        P.memset(self.epsc[:, 0:1], RMS_EPS)
        P.memset(self.epsc[:, 1:2], GN_EPS)

        def asel(view, pattern, op, fill, base, cm):
            P.add("pool", lambda e: e.affine_select(view.ap, view.ap, pattern, op, fill, base=base, channel_multiplier=cm),
                  [view], [view], name="asel")
        for m in (self.maskb, self.maskf):
            P.memset(m.full(), 0.0)
            asel(m.full(), [[1, 256]], ALU.is_ge, NEG, 0, -1)
            asel(m.full(), [[-1, 256]], ALU.is_ge, NEG, 128, 1)
        asel(self.maskf.full(), [[1, 256]], ALU.is_ge, NEG, -128, 0)
        P.memset(self.trim.full(), 1.0)
        asel(self.trim[:, 0, :], [[1, 64]], ALU.is_gt, 0.0, 0, -1)
        asel(self.trim[:, 1, :], [[1, 64]], ALU.is_ge, 0.0, 0, -1)
        P.memset(self.trilo.full(), 1.0)
        asel(self.trilo.full(), [[-1, 64]], ALU.is_gt, 0.0, 0, 1)

        for t_ in (self.kcar, self.vcar, self.shcar, self.Hst, self.cvcar, self.hcar, self.wabd, self.wibd):
            P.memset(t_.full(), 0.0, eng="pool")

        def ld(name, src, i=0):
            c, w = PC[name]
            n = src.shape[1] // 128
            for l_ in range(L):
                P.dma("sp", self.PRM[:, l_, c + i:c + i + n], src[l_].rearrange("(c p) -> p c", p=128), chan="setup",
                      allow_slow_non_contiguous=True)
        ld("gpre", dr["norm_mix_pre"]); ld("gpost", dr["norm_mix_post"])
        ld("gfpre", dr["norm_ffn_pre"]); ld("gfpost", dr["norm_ffn_post"])
        ld("bq", dr["b_in"][:, 0:512])
        ld("brw", dr["b_in"][:, 768:1792]); ld("blx", dr["b_in"][:, 1792:2048]); ld("blg", dr["b_in"][:, 2048:2304])
        ld("mu", dr["rwkv_mu"]); ld("w0", dr["rwkv_w0"]); ld("a0", dr["rwkv_a0"]); ld("kk", dr["rwkv_k_k"])
        ld("ka", dr["rwkv_k_a"]); ld("rk", dr["rwkv_r_k"].rearrange("l h d -> l (h d)"))
        ld("lnw", dr["rwkv_ln_w"]); ld("lnb", dr["rwkv_ln_b"])
        for j in range(4):
            ld("cw", dr["lru_conv_w"][:, j, :], i=2 * j)
        ld("cb", dr["lru_conv_b"]); ld("ba", dr["lru_b_a"]); ld("bi", dr["lru_b_i"]); ld("Lp", dr["lru_L"])
        ld("bout", dr["b_out"])
        c, w = PC["bkd"]
        for g in range(2):
            for hp in range(2):
                for l_ in range(L):
                    P.dma("sp", self.PRM[hp * 64:(hp + 1) * 64, l_, c + g:c + g + 1],
                          dr["b_in"][l_, 512 + 64 * g:512 + 64 * g + 64].rearrange("(c p) -> p c", p=64), chan="setup",
                          allow_slow_non_contiguous=True)
        P.dma("sp", self.bkv.full(), dr["b_in"][:, 512:768].partition_broadcast(128), chan="setup")
        P.dma("sp", self.sinkb.full(), dr["attn_sinks"].partition_broadcast(128), chan="setup")
        P.dma("pool", self.wup[0:64, :, :], dr["rwkv_w_up"].rearrange("l i j -> i l j"), chan="setup2")
        P.dma("pool", self.aup[64:128, :, :], dr["rwkv_a_up"].rearrange("l i j -> i l j"), chan="setup2")
        P.dma("pool", self.gup.full(), dr["rwkv_g_up"].rearrange("l i j -> i l j"), chan="setup2")
        for par in range(2):
            for (dst, src) in ((self.wabd, dr["lru_w_a"]), (self.wibd, dr["lru_w_i"])):
                for cc in range(2):
                    P.dma("pool", dst[par * 64:(par + 1) * 64, :, cc, par * 64:(par + 1) * 64],
                          src[:, 2 * cc + par, :, :].rearrange("l i j -> i l j"), chan="setup2")
        for l in range(L):
            P.act(self.prm(l, "c8", 0, 2), self.prm(l, "Lp", 0, 2), AF.Sigmoid)
            P.act(self.prm(l, "c8", 0, 2), self.prm(l, "c8", 0, 2), AF.Ln)
            P.ts(self.prm(l, "c8", 0, 2), self.prm(l, "c8", 0, 2), 8.0, ALU.mult)
            P.ts(self.prm(l, "ka1", 0, 2), self.prm(l, "ka", 0, 2), -1.0, ALU.mult, 1.0, ALU.add)

    def wq_add(self, loader):
        self.wq.append(loader)
        return len(self.wq) - 1

    def wget(self, bid):
        lim = min(len(self.wq), bid + self.NSLOT)
        while self.wq_issued < lim:
            i = self.wq_issued
            self.wq[i](self.wslot[i % self.NSLOT], f"w{i % self.NSLOT}")
            self.wq_issued += 1
        return self.wslot[bid % self.NSLOT]

    def wload(self, slot, chan, src, nk, W, pieces):
        for (s0, n, d0) in pieces:
            dst_ap = slot.h[:, 0:nk * W].rearrange("p (k w) -> p k w", w=W)[:, :, d0:d0 + n]
            v = View(slot, dst_ap, 0, 128, 0, nk * W * 2)
            self.P.dma("pool", v, src[:, s0:s0 + n].rearrange("(k p) n -> p k n", p=128), chan=chan)

    def wview(self, slot, nk, W, k, c0, n, p0=0, p1=128):
        ap = slot.h[p0:p1, 0:nk * W].rearrange("p (k w) -> p k w", w=W)[:, k, c0:c0 + n]
        return View(slot, ap, p0, p1, (k * W + c0) * 2, (k * W + c0 + n) * 2)

    def load_segment(self, s):
        P, cfg, dr = self.P, self.cfg, self.dr
        TP, NB, NS = cfg.TP, cfg.NB, cfg.NS
        k = 0
        for b in range(NB + 1):
            st = self.stage[k % 2]
            k += 1
            if b < NB:
                n = 128
                src = dr["xp"][s * TP + b * 128: s * TP + (b + 1) * 128, :]
                c0 = b * 128
            else:
                n = NS
                src = dr["xs"][s * NS:(s + 1) * NS, :]
                c0 = TP
            P.dma("sp", st[0:n, :], src, chan=f"xin{(k - 1) % 2}")
            for half in range(2):
                pv = self.ps(128, 4 * n)
                for j in range(4):
                    c = half * 4 + j
                    P.transpose(pv[:, j * n:(j + 1) * n], st[0:n, c * 128:(c + 1) * 128], self.identf[0:n, 0:n])
                dst = self.xT[:, half * 4:half * 4 + 4, c0:c0 + n]
                src_v = pv.with_ap(pv.ap.rearrange("p (j n) -> p j n", j=4))
                if half == 0:
                    P.copy(dst, src_v, eng="act")
                else:
                    P.copy(dst, src_v, eng="dve")

    def store_segment(self, s):
        P, cfg, dr = self.P, self.cfg, self.dr
        TP, NB, NS = cfg.TP, cfg.NB, cfg.NS
        k = 0
        for b in range(NB + 1):
            st = self.stage[k % 2]
            k += 1
            if b < NB:
                n = 128
                dst = dr["yp"][s * TP + b * 128: s * TP + (b + 1) * 128, :]
                c0 = b * 128
            else:
                n = NS
                dst = dr["ys"][s * NS:(s + 1) * NS, :]
                c0 = TP
            for half in range(2):
                pv = self.ps(n, 512)
                for j in range(4):
                    c = half * 4 + j
                    P.transpose(pv[:, j * 128:(j + 1) * 128], self.xT[:, c, c0:c0 + n], self.identf.full())
                if half == 0:
                    P.copy(st[0:n, 0:512], pv, eng="act")
                else:
                    P.copy(st[0:n, 512:1024], pv, eng="dve")
            P.dma("sp", dst, st[0:n, :], chan=f"out{(k - 1) % 2}")

    def sumsq_rstd(self, src_fn, sq):
        P, cfg = self.P, self.cfg
        for c in range(8):
            P.act(sq[:, c, :], src_fn(c), AF.Square)
        for (c0, n) in cfg.blks:
            pv = self.ps(128, n)
            for c in range(8):
                P.mm(pv, self.ones.full(), sq[:, c, c0:c0 + n], start=(c == 0), stop=(c == 7))
            P.act(self.lntmp[:, 0:n], pv, AF.Ln, bias=self.epsc[:, 0:1], scale=1.0 / D)
            P.act(self.rstd[:, c0:c0 + n], self.lntmp[:, 0:n], AF.Exp, scale=-0.5)

    def dense(self, M, nk, lhsT_fn, rhs_fn, evac_fn, blks=None):
        P = self.P
        for (c0, n) in (blks or self.cfg.blks):
            pv = self.ps(M, n)
            for k in range(nk):
                P.mm(pv, lhsT_fn(k), rhs_fn(k, c0, n), start=(k == 0), stop=(k == nk - 1))
            evac_fn(pv, c0, n)


NBLK_PER_LAYER = 30


def rr_gen(gens):
    gens = list(gens)
    while gens:
        for g_ in list(gens):
            try:
                next(g_)
                yield
            except StopIteration:
                gens.remove(g_)


def run_rr(gens):
    gens = list(gens)
    while gens:
        for g_ in list(gens):
            try:
                next(g_)
            except StopIteration:
                gens.remove(g_)


class Kern(KernBase):
    ARENA_BYTES = 92 * 1024

    def register_layer(self, l):
        dr = self.dr
        w_in = dr["w_in"][l]
        ids = {}

        def reg(name, fn):
            ids[name] = self.wq_add(fn)
        reg("q", lambda sl, ch: self.wload(sl, ch, w_in, 8, 512, [(0, 512, 0)]))
        reg("kv", lambda sl, ch: self.wload(sl, ch, w_in, 8, 512, [(512, 64, 0), (512, 64, 64), (576, 64, 128), (576, 64, 192), (512, 256, 256)]))
        reg("lru", lambda sl, ch: self.wload(sl, ch, w_in, 8, 512, [(1792, 512, 0)]))
        reg("lora", lambda sl, ch: self.wload(sl, ch, w_in, 8, 256, [(768 + 768, 256, 0)]))
        for p in range(2):
            reg(f"pair{p}", lambda sl, ch, p=p: self.wload(sl, ch, w_in, 8, 384, [(768 + 128 * p, 128, 0), (768 + 256 + 128 * p, 128, 128), (768 + 512 + 128 * p, 128, 256)]))
        for j in range(2):
            reg(f"wout{j}", lambda sl, ch, j=j: self.wload(sl, ch, dr["w_out"][l], 8, 512, [(512 * j, 512, 0)]))
        for j in range(11):
            def f(sl, ch, j=j):
                self.wload(sl, ch, dr["ffn_w_gate"][l], 8, 512, [(256 * j, 256, 0)])
                self.wload(sl, ch, dr["ffn_w_up"][l], 8, 512, [(256 * j, 256, 256)])
            reg(f"gu{j}", f)
        for m in range(8):
            reg(f"down{m}", lambda sl, ch, m=m: self.wload(sl, ch, dr["ffn_w_down"][l], 22, 128, [(128 * m, 128, 0)]))
        reg("pw", lambda sl, ch: self.wload(sl, ch, dr["ple_w"][l], 2, 1024, [(0, 1024, 0)]))
        for j in range(2):
            reg(f"pg{j}", lambda sl, ch, j=j: self.wload(sl, ch, dr["ple_gate_w"][l], 8, 512, [(512 * j, 512, 0)]))
        return ids

    def layer(self, l, s, ids):
        P, cfg = self.P, self.cfg
        T, TP = cfg.T, cfg.TP
        A = self.arena
        m0 = A.mark()
        import os as _os
        STOP = int(_os.environ.get("KS_STOP", "99"))
        if STOP < 1:
            return
        sq = self.mixT
        self.sumsq_rstd(lambda c: self.xT[:, c, :], sq)
        for c in range(8):
            P.stt(self.hT[:, c, :], self.xT[:, c, :], self.prm(l, "gpre", c), self.rstd.full(), ALU.mult, ALU.mult)
        if STOP < 2:
            return
        self.attention(l, s, ids, side=self.lru_gen(l, s, ids))
        A.release(m0)
        self.rwkv(l, s, ids)
        A.release(m0)
        ybuf = A.alloc("ybuf", [128, 8, T], F32)
        for j in range(2):
            sl = self.wget(ids[f"wout{j}"])
            for mm_ in range(4):
                m = j * 4 + mm_
                self.dense(128, 8, lambda k: self.wview(sl, 8, 512, k, mm_ * 128, 128),
                           lambda k, c0, n: self.mixT[:, k, c0:c0 + n],
                           lambda pv, c0, n, m=m: P.act(ybuf[:, m, c0:c0 + n], pv, AF.Identity, bias=self.prm(l, "bout", m)))
        self.post_norm_add(l, ybuf, "gpost")
        if STOP < 6:
            A.release(m0)
            return
        self.sumsq_rstd(lambda c: self.xT[:, c, :], self.mixT)
        for c in range(8):
            P.stt(self.hT[:, c, :], self.xT[:, c, :], self.prm(l, "gfpre", c), self.rstd.full(), ALU.mult, ALU.mult)
        act = A.alloc("act", [128, 22, T], BF16)
        sg = A.alloc("sg", [128, 2, 512], F32)
        ei = 0
        for j in range(11):
            sl = self.wget(ids[f"gu{j}"])
            for jj in range(2):
                fc = 2 * j + jj
                for (c0, n) in cfg.blks:
                    pg = self.ps(128, n)
                    for k in range(8):
                        P.mm(pg, self.wview(sl, 8, 512, k, jj * 128, 128), self.hT[:, k, c0:c0 + n], start=(k == 0), stop=(k == 7))
                    pu = self.ps(128, n)
                    for k in range(8):
                        P.mm(pu, self.wview(sl, 8, 512, k, 256 + jj * 128, 128), self.hT[:, k, c0:c0 + n], start=(k == 0), stop=(k == 7))
                    sgt = sg[:, ei % 2, 0:n]
                    ei += 1
                    P.act(sgt, pg, AF.Silu)
                    P.tt(act[:, fc, c0:c0 + n], sgt, pu, ALU.mult)
        for m in range(8):
            sl = self.wget(ids[f"down{m}"])
            self.dense(128, 22, lambda k: self.wview(sl, 22, 128, k, 0, 128),
                       lambda k, c0, n: act[:, k, c0:c0 + n],
                       lambda pv, c0, n, m=m: P.copy(ybuf[:, m, c0:c0 + n], pv, eng="act"))
        self.post_norm_add(l, ybuf, "gfpost")
        A.release(m0)
        if STOP < 7:
            return
        for c in range(8):
            P.copy(self.hT[:, c, :], self.xT[:, c, :], eng=("act" if c % 2 else "dve"))
        pT = A.alloc("pT", [128, 2, T], BF16)
        self.load_ple(l, s, pT)
        pwb = A.alloc("pwb", [128, 8, T], F32)
        sgp = A.alloc("sgp", [128, 2, 512], F32)
        slw = self.wget(ids["pw"])
        for m in range(8):
            self.dense(128, 2, lambda k: self.wview(slw, 2, 1024, k, m * 128, 128), lambda k, c0, n: pT[:, k, c0:c0 + n],
                       lambda pv, c0, n, m=m: P.copy(pwb[:, m, c0:c0 + n], pv, eng="act"))
        ei = 0
        for j in range(2):
            sl = self.wget(ids[f"pg{j}"])
            for mm_ in range(4):
                m = j * 4 + mm_
                for (c0, n) in cfg.blks:
                    pg = self.ps(128, n)
                    for k in range(8):
                        P.mm(pg, self.wview(sl, 8, 512, k, mm_ * 128, 128), self.hT[:, k, c0:c0 + n], start=(k == 0), stop=(k == 7))
                    sgt = sgp[:, ei % 2, 0:n]
                    ei += 1
                    P.act(sgt, pg, AF.Sigmoid)
                    P.tt(sgt, sgt, pwb[:, m, c0:c0 + n], ALU.mult)
                    P.tt(self.xT[:, m, c0:c0 + n], self.xT[:, m, c0:c0 + n], sgt, ALU.add)
        A.release(m0)

    def post_norm_add(self, l, ybuf, gname):
        P, cfg = self.P, self.cfg
        self.sumsq_rstd(lambda c: ybuf[:, c, :], self.mixT)
        for c in range(8):
            P.stt(ybuf[:, c, :], ybuf[:, c, :], self.prm(l, gname, c), self.rstd.full(), ALU.mult, ALU.mult)
            P.tt(self.xT[:, c, :], self.xT[:, c, :], ybuf[:, c, :], ALU.add)

    def load_ple(self, l, s, pT):
        P, cfg, dr = self.P, self.cfg, self.dr
        TP, NB, NS = cfg.TP, cfg.NB, cfg.NS
        for b in range(NB + 1):
            st = self.stage[b % 2]
            if b < NB:
                n = 128
                src = dr["pp"][l, s * TP + b * 128: s * TP + (b + 1) * 128, :]
                c0 = b * 128
            else:
                n = NS
                src = dr["psm"][l, s * NS:(s + 1) * NS, :]
                c0 = TP
            P.dma("sp", st[0:n, 0:256], src, chan=f"xin{b % 2}")
            pv = self.ps(128, 2 * n)
            for j in range(2):
                P.transpose(pv[:, j * n:(j + 1) * n], st[0:n, j * 128:(j + 1) * 128], self.identf[0:n, 0:n])
            P.copy(pT[:, :, c0:c0 + n], pv.with_ap(pv.ap.rearrange("p (j n) -> p j n", j=2)), eng="act")

    def attention(self, l, s, ids, side=None):
        P, cfg, dr = self.P, self.cfg, self.dr
        T, TP, NB, SB, NS = cfg.T, cfg.TP, cfg.NB, cfg.SB, cfg.NS
        A = self.arena
        qT = A.alloc("qT", [128, 4, T], BF16)
        kdT = A.alloc("kdT", [128, 2, 128 + T], BF16)
        vpad = A.alloc("vpad", [128, NB + 1, 512], BF16)
        vpad_c = A.alloc("vpad_c", [128, SB, 512], BF16)
        vpad_s = A.alloc("vpad_s", [4, SB, 512], BF16)
        kcT = A.alloc("kcT", [128, SB, 2, 128], BF16)
        kvtok = A.alloc("kvtok", [128, 2, 256], F32)
        cst = A.alloc("cst", [128, SB, 2, 128], F32)
        cdup = A.alloc("cdup", [128, 2, 128], BF16)
        sc = A.alloc("sc", [128, 8, 256], F32)
        ee = A.alloc("ee", [128, 8, 256], F32)
        pps = [A.alloc(f"pp{i_}", [128, 8, 256], BF16) for i_ in range(2)]
        pTs = A.alloc("pTs", [128, 16, 128], BF16)
        sm = A.alloc("sm", [128, 6, 8], F32)
        P.memset(vpad.full(), 0.0, eng="pool")
        P.memset(vpad_c.full(), 0.0, eng="pool")
        P.memset(vpad_s.full(), 0.0, eng="pool")
        P.copy(kdT[:, :, 0:128], self.kcar[:, l, :, :], eng="pool")
        P.copy(vpad[:, 0, :], self.vcar[:, l, :], eng="pool")
        for b in range(SB):
            gb = s * SB + b
            P.dma("sp", cst[:, b, 0, :], dr["ck"][l, gb], chan="cache0")
            P.dma("sp", cst[:, b, 1, :], dr["cv"][l, gb], chan="cache1")
        for b in range(SB):
            for g in range(2):
                for var in range(2):
                    P.copy(vpad_c[:, b, g * 256 + var * 192: g * 256 + var * 192 + 64], cst[:, b, 1, g * 64:(g + 1) * 64], eng=("dve" if var else "pool"))
            for g in range(2):
                for hp in range(2):
                    P.copy(cdup[:, g, hp * 64:(hp + 1) * 64], cst[:, b, 0, g * 64:(g + 1) * 64], eng=("dve" if hp else "pool"))
            pb = self.ps(128, 128).bitcast(BF16)
            for g in range(2):
                P.transpose(pb[:, g * 128:(g + 1) * 128], cdup[:, g, :], self.ident.full())
            P.copy(kcT[:, b, :, :], pb.with_ap(pb.ap.rearrange("p (g n) -> p g n", g=2)), eng="act")
        sl = self.wget(ids["q"])
        for m in range(4):
            self.dense(128, 8, lambda k: self.wview(sl, 8, 512, k, m * 128, 128),
                       lambda k, c0, n: self.hT[:, k, c0:c0 + n],
                       lambda pv, c0, n, m=m: P.act(qT[:, m, c0:c0 + n], pv, AF.Identity, bias=self.prm(l, "bq", m)))
        sl = self.wget(ids["kv"])
        for g in range(2):
            self.dense(128, 8, lambda k: self.wview(sl, 8, 512, k, g * 128, 128),
                       lambda k, c0, n: self.hT[:, k, c0:c0 + n],
                       lambda pv, c0, n, g=g: P.act(kdT[:, g, 128 + c0:128 + c0 + n], pv, AF.Identity, bias=self.prm(l, "bkd", g)))

        def vscatter(dst_fn, src, n):
            for g in range(2):
                for var in range(2):
                    P.copy(dst_fn(g, var), src[0:n, 128 + g * 64:128 + (g + 1) * 64], eng=("dve" if var else "pool"))

        last_seg = (s == cfg.NSEG - 1)
        for b in range(NB):
            pv = self.ps(128, 256)
            for k in range(8):
                P.mm(pv, self.hT[:, k, b * 128:(b + 1) * 128], self.wview(sl, 8, 512, k, 256, 256), start=(k == 0), stop=(k == 7))
            kt = kvtok[:, b % 2, :]
            P.tt(kt, pv, self.bkv[:, l, :], ALU.add)
            vscatter(lambda g, var: vpad[:, b + 1, g * 256 + var * 192: g * 256 + var * 192 + 64], kt, 128)
            if last_seg and b == NB - 1:
                P.dma("sp", dr["kp"][l], kt[:, 0:128], chan="outs")
                P.dma("sp", dr["vp"][l], kt[:, 128:256], chan="outs")
        for b in range(SB):
            gb = s * SB + b
            pv = self.ps(4, 256)
            for k in range(8):
                P.mm(pv, self.hT[:, k, TP + 4 * b:TP + 4 * b + 4], self.wview(sl, 8, 512, k, 256, 256), start=(k == 0), stop=(k == 7))
            kt = kvtok[0:4, b % 2, :]
            P.tt(kt, pv, self.bkv[0:4, l, :], ALU.add)
            vscatter(lambda g, var: vpad_s[0:4, b, g * 256 + var * 192: g * 256 + var * 192 + 64], kt, 4)
            P.dma("sp", dr["ks"][l, gb, 124:128, :], kt[:, 0:128], chan="outs")
            P.dma("sp", dr["vs"][l, gb, 124:128, :], kt[:, 128:256], chan="outs")
        if s == 0:
            P.dma("sp", dr["ks"][l, :, 0:124, :], dr["ck"][l, :, 4:128, :], chan="outs")
            P.dma("sp", dr["vs"][l, :, 0:124, :], dr["cv"][l, :, 4:128, :], chan="outs")

        def attn_block(M, qc0, kviews, vviews, mask, wk, par):
            c_s = self.bank(4)
            pp = pps[par]

            def hcol(h):
                hp_, cc = h % 2, h // 2
                return (hp_ * 2 + cc // 2) * 512 + (cc % 2) * 256
            for h in range(8):
                c, hp, g = h // 2, h % 2, h // 4
                off = 0
                for (kf, nk) in kviews:
                    P.mm(self.PS[0:M, c_s + hcol(h) + off: c_s + hcol(h) + off + nk],
                         qT[hp * 64:(hp + 1) * 64, c, qc0:qc0 + M], kf(g, hp))
                    off += nk
            scv = sc[0:M, :, 0:wk]
            mk = mask[0:M, 0:wk]
            for h in range(8):
                P.stt(sc[0:M, h, 0:wk], self.PS[0:M, c_s + hcol(h): c_s + hcol(h) + wk], 0.125, mk, ALU.mult, ALU.add)
            mx = sm[0:M, 0, :]
            P.add("dve", lambda e: e.tensor_reduce(mx.ap, scv.ap, AX.X, ALU.max), [scv], [mx])
            P.tt(mx, mx, self.sinkb[0:M, l, :], ALU.max)
            nm = sm[0:M, 1, :]
            P.ts(nm, mx, -1.0, ALU.mult)
            rs = sm[0:M, 2, :]
            for h in range(8):
                P.act(ee[0:M, h, 0:wk], sc[0:M, h, 0:wk], AF.Exp, bias=sm[0:M, 1, h:h + 1], accum=sm[0:M, 2, h:h + 1])
            es = sm[0:M, 3, :]
            P.tt(es, self.sinkb[0:M, l, :], nm, ALU.add)
            P.act(es, es, AF.Exp)
            P.tt(es, es, rs, ALU.add)
            rd = sm[0:M, 4, :]
            P.add("dve", lambda e: e.reciprocal(rd.ap, es.ap), [es], [rd])
            for h in range(8):
                if h % 2:
                    P.ts(pp[0:M, h, 0:wk], ee[0:M, h, 0:wk], sm[0:M, 4, h:h + 1], ALU.mult)
                else:
                    P.act(pp[0:M, h, 0:wk], ee[0:M, h, 0:wk], AF.Copy, scale=sm[0:M, 4, h:h + 1])

            def phase_b():
                c_t = self.bank(2)
                pTp = self.PS[:, c_t:c_t + 1024].bitcast(BF16)
                nkb = len(kviews)
                for h in range(8):
                    off = 0
                    for kb, (kf, nk) in enumerate(kviews):
                        P.transpose(pTp[0:nk, (h * 2 + kb) * 128:(h * 2 + kb) * 128 + M], pp[0:M, h, off:off + nk], self.ident[0:M, 0:M])
                        off += nk
                for kb, (kf, nk) in enumerate(kviews):
                    for hb in range(2):
                        src = pTp[0:nk, hb * 1024:(hb + 1) * 1024]
                        src = src.with_ap(src.ap.rearrange("p (h kb m) -> p h kb m", h=4, kb=2)[:, :, kb, 0:M])
                        d0 = pTs[0:nk, hb * 8:(hb + 1) * 8, :]
                        dst = d0.with_ap(d0.ap.rearrange("p (h kb) m -> p h kb m", kb=2)[:, :, kb, 0:M])
                        P.copy(dst, src, eng=("act" if hb == 0 else "dve"))
                c_o = self.bank(1)
                for c in range(4):
                    po = self.PS[:, c_o + c * M: c_o + (c + 1) * M]
                    first = True
                    for hh in range(2):
                        h = 2 * c + hh
                        g = h // 4
                        for kb, (kf, nk) in enumerate(kviews):
                            lastmm = (hh == 1 and kb == nkb - 1)
                            P.mm(po, vviews[kb](g, hh, nk), pTs[0:nk, h * 2 + kb, 0:M], start=first, stop=lastmm)
                            first = False
                pov = self.PS[:, c_o:c_o + 4 * M]
                P.copy(self.mixT[:, 0:4, qc0:qc0 + M], pov.with_ap(pov.ap.rearrange("p (c m) -> p c m", c=4)), eng="act")
            return phase_b

        jobs = []
        for i in range(NB):
            mask = self.maskf if (s == 0 and i == 0) else self.maskb
            jobs.append((128, i * 128,
                         [(lambda g, hp, i=i: kdT[hp * 64:(hp + 1) * 64, g, i * 128:i * 128 + 128], 128),
                          (lambda g, hp, i=i: kdT[hp * 64:(hp + 1) * 64, g, (i + 1) * 128:(i + 1) * 128 + 128], 128)],
                         [lambda g, hh, nk, i=i: vpad[0:nk, i, g * 256 + hh * 128: g * 256 + hh * 128 + 128],
                          lambda g, hh, nk, i=i: vpad[0:nk, i + 1, g * 256 + hh * 128: g * 256 + hh * 128 + 128]],
                         mask, 256))
        for b in range(SB):
            jobs.append((4, TP + 4 * b,
                         [(lambda g, hp, b=b: kcT[hp * 64:(hp + 1) * 64, b, g, :], 128),
                          (lambda g, hp, b=b: kdT[hp * 64:(hp + 1) * 64, g, 128 + TP + 4 * b:128 + TP + 4 * b + 4], 4)],
                         [lambda g, hh, nk, b=b: vpad_c[0:nk, b, g * 256 + hh * 128: g * 256 + hh * 128 + 128],
                          lambda g, hh, nk, b=b: vpad_s[0:nk, b, g * 256 + hh * 128: g * 256 + hh * 128 + 128]],
                         self.maskb, 132))
        def side_steps(k):
            if side is not None:
                for _ in range(k):
                    if next(side, "done") == "done":
                        break
        pend = None
        for ji, job in enumerate(jobs):
            fin = attn_block(*job, ji % 2)
            side_steps(7)
            if pend is not None:
                pend()
                side_steps(7)
            pend = fin
        pend()
        side_steps(100000)
        P.copy(self.kcar[:, l, :, :], kdT[:, :, TP:TP + 128], eng="pool")
        P.copy(self.vcar[:, l, :], vpad[:, NB, :], eng="pool")

    def rwkv(self, l, s, ids):
        P, cfg = self.P, self.cfg
        self.wget(ids["lora"]); self.wget(ids["pair0"]); self.wget(ids["pair1"])
        P.memset(self.mixT[:, 4:6, :], 0.0, eng="pool")

    def lru_gen(self, l, s, ids):
        P, cfg, dr = self.P, self.cfg, self.dr
        T, TP, SB, NS = cfg.T, cfg.TP, cfg.SB, cfg.NS
        A = self.arena
        sl = self.wget(ids["lru"])
        xe_p = A.alloc("xe_p", [128, 2, 3 + TP], F32)
        xe_s = A.alloc("xe_s", [128, 2, SB, 7], F32)
        gb = A.alloc("gb", [128, 2, T], F32)
        xc = A.alloc("xc", [128, 2, T], F32)
        xcb = A.alloc("xcb", [128, 2, T], BF16)
        rg = A.alloc("rg", [128, 2, T], F32)
        ig = A.alloc("ig", [128, 2, T], F32)
        aa = A.alloc("aa", [128, 2, T], F32)
        uu = A.alloc("uu", [128, 2, T], F32)
        hh_ = A.alloc("hh", [128, 2, T], F32)
        gl = rg
        h0s = A.alloc("h0s", [128, 2, SB], F32)
        cvo = A.alloc("cvo", [128, 2, SB, 3], F32)
        for c in range(2):
            P.copy(xe_p[:, c, 0:3], self.cvcar[:, l, c, :], eng="pool")
        for b in range(SB):
            gbi = s * SB + b
            for c in range(2):
                P.dma("sp", xe_s[:, c, b, 0:3], dr["st_conv"][l, gbi, :, c * 128:(c + 1) * 128].rearrange("j p -> p j"),
                      chan="st", allow_slow_non_contiguous=True)
        for c in range(2):
            P.dma("sp", h0s[:, c, :], dr["st_lru"][l, s * SB:(s + 1) * SB, c * 128:(c + 1) * 128].rearrange("b p -> p b"), chan="st",
                  allow_slow_non_contiguous=True)
        for c in range(2):
            def ev_x(pv, c0, n, c=c):
                npp = max(0, min(c0 + n, TP) - c0)
                if npp > 0:
                    P.act(xe_p[:, c, 3 + c0:3 + c0 + npp], pv[:, 0:npp], AF.Identity, bias=self.prm(l, "blx", c))
                if npp < n:
                    lo = c0 + npp - TP
                    assert lo % 4 == 0 and (n - npp) % 4 == 0
                    ps_ = pv[:, npp:n]
                    P.act(xe_s[:, c, lo // 4:(lo + n - npp) // 4, 3:7], ps_.with_ap(ps_.ap.rearrange("p (b t) -> p b t", t=4)), AF.Identity,
                          bias=self.prm(l, "blx", c))
            self.dense(128, 8, lambda k: self.wview(sl, 8, 512, k, c * 128, 128), lambda k, c0, n: self.hT[:, k, c0:c0 + n], ev_x)
            yield
            self.dense(128, 8, lambda k: self.wview(sl, 8, 512, k, 256 + c * 128, 128), lambda k, c0, n: self.hT[:, k, c0:c0 + n],
                       lambda pv, c0, n, c=c: P.act(gb[:, c, c0:c0 + n], pv, AF.Identity, bias=self.prm(l, "blg", c)))
            yield
        def conv_chain(c, dst, ext):
            P.ts(dst, ext(0), self.prm(l, "cw", 0 + c), ALU.mult, self.prm(l, "cb", c), ALU.add)
            yield
            for j in range(1, 4):
                P.stt(dst, ext(j), self.prm(l, "cw", 2 * j + c), dst, ALU.mult, ALU.add)
                yield
        gens = []
        for c in range(2):
            gens.append(conv_chain(c, xc[:, c, 0:TP], lambda j, c=c: xe_p[:, c, j:j + TP]))
            xs_ = xc[:, c, TP:T]
            gens.append(conv_chain(c, xs_.with_ap(xs_.ap.rearrange("p (b t) -> p b t", t=4)), lambda j, c=c: xe_s[:, c, :, j:j + 4]))
        yield from rr_gen(gens)
        for c in range(2):
            P.copy(self.cvcar[:, l, c, :], xe_p[:, c, TP:TP + 3], eng="pool")
            P.copy(cvo[:, c, :, :], xe_s[:, c, :, 4:7], eng="pool")
            P.copy(xcb[:, c, :], xc[:, c, :], eng="act")
        for c in range(2):
            self.dense(128, 1, lambda k: self.wabd[:, l, c, :], lambda k, c0, n: xcb[:, c, c0:c0 + n],
                       lambda pv, c0, n, c=c: P.act(rg[:, c, c0:c0 + n], pv, AF.Sigmoid, bias=self.prm(l, "ba", c)))
            yield
            self.dense(128, 1, lambda k: self.wibd[:, l, c, :], lambda k, c0, n: xcb[:, c, c0:c0 + n],
                       lambda pv, c0, n, c=c: P.act(ig[:, c, c0:c0 + n], pv, AF.Sigmoid, bias=self.prm(l, "bi", c)))
            yield
        def main_chain(c):
            a_ = aa[:, c, :]
            u_ = uu[:, c, :]
            P.act(a_, rg[:, c, :], AF.Exp, scale=self.prm(l, "c8", c)); yield
            P.tt(u_, a_, a_, ALU.mult); yield
            P.ts(u_, u_, -1.0, ALU.mult, 1.0, ALU.add); yield
            P.act(u_, u_, AF.Sqrt); yield
            P.tt(ig[:, c, :], ig[:, c, :], xc[:, c, :], ALU.mult); yield
            P.tt(u_, u_, ig[:, c, :], ALU.mult); yield
            us = uu[:, c, TP:T]
            us3 = us.with_ap(us.ap.rearrange("p (b t) -> p b t", t=4)[:, :, 0])
            as_ = aa[:, c, TP:T]
            as3 = as_.with_ap(as_.ap.rearrange("p (b t) -> p b t", t=4)[:, :, 0])
            tmp = h0s[:, c, :]
            P.tt(tmp, tmp, as3, ALU.mult); yield
            P.tt(us3, us3, tmp, ALU.add); yield
            P.memset(as3, 0.0); yield
            hv = hh_[:, c, :]
            ini = self.hcar[:, l, c:c + 1]
            P.add("dve", lambda e, hv=hv, a_=a_, u_=u_, ini=ini: e.tensor_tensor_scan(hv.ap, a_.ap, u_.ap, ini.ap, ALU.mult, ALU.add),
                  [a_, u_, ini], [hv]); yield
            P.copy(self.hcar[:, l, c:c + 1], hh_[:, c, TP - 1:TP], eng="pool"); yield

        def gelu_chain(c):
            g_ = gb[:, c, :]
            t1 = gl[:, c, :]
            P.act(t1, g_, AF.Square); yield
            P.ts(t1, t1, 0.044715, ALU.mult, 1.0, ALU.add); yield
            P.tt(t1, t1, g_, ALU.mult); yield
            P.act(t1, t1, AF.Sigmoid, scale=1.5957691216057308); yield
            P.tt(t1, t1, g_, ALU.mult); yield
        yield from rr_gen([main_chain(0), gelu_chain(0), main_chain(1), gelu_chain(1)])
        for c in range(2):
            P.tt(self.mixT[:, 6 + c, :], gl[:, c, :], hh_[:, c, :], ALU.mult)
        for b in range(SB):
            gbi = s * SB + b
            for c in range(2):
                P.dma("sp", dr["convs"][l, gbi, :, c * 128:(c + 1) * 128].rearrange("j p -> p j"), cvo[:, c, b, :], chan="outs",
                      allow_slow_non_contiguous=True)
        hs = hh_[:, :, TP:T]
        hs_last = hs.with_ap(hs.ap.rearrange("p c (b t) -> p c b t", t=4)[:, :, :, 3])
        P.copy(h0s.full(), hs_last, eng="pool")
        for c in range(2):
            P.dma("sp", dr["lrus"][l, s * SB:(s + 1) * SB, c * 128:(c + 1) * 128].rearrange("b p -> p b"), h0s[:, c, :], chan="outs",
                  allow_slow_non_contiguous=True)
        if s == cfg.NSEG - 1:
            for c in range(2):
                P.dma("sp", dr["convp"][l, :, c * 128:(c + 1) * 128].rearrange("j p -> p j"), self.cvcar[:, l, c, :], chan="outs",
                      allow_slow_non_contiguous=True)
            P.dma("sp", dr["lrup"][l].rearrange("(c p) -> p c", p=128), self.hcar[:, l, :], chan="outs", allow_slow_non_contiguous=True)


NG = 5
CW = 64
DEC_C = -math.exp(-0.5)


class Kern(Kern):
    def setup(self):
        super().setup()
        P = self.P
        A = self.arena
        self.trimx = A.alloc("trimx", [64, 2, 2, 2, 64], F32)
        self.trilox = A.alloc("trilox", [64, NG, 64], F32)
        self.identx = A.alloc("identx", [64, NG, 2, 64], BF16)
        for a in range(2):
            for w in range(2):
                P.copy(self.trimx[:, a, w, :, :], self.trim.full(), eng="pool")
        for i in range(NG):
            P.copy(self.trilox[:, i, :], self.trilo.full(), eng="pool")
            for hh in range(2):
                P.copy(self.identx[:, i, hh, :], self.ident[0:64, 0:64], eng="pool")

    def rwkv(self, l, s, ids):
        P, cfg, dr = self.P, self.cfg, self.dr
        T, TP, SB, NS, NCH = cfg.T, cfg.TP, cfg.SB, cfg.NS, cfg.NCH
        assert NCH % 2 == 0
        NU = NCH + SB
        TW = NU * CW
        A = self.arena
        last_seg = (s == cfg.NSEG - 1)
        tblks = []
        c_ = 0
        while c_ < TW:
            n_ = min(512, TW - c_)
            tblks.append((c_, n_))
            c_ += n_

        def v3(view, inner):
            return view.with_ap(view.ap.rearrange("p (u c) -> p u c", c=inner))

        pE_ = [A.alloc(f"pE{i_}", [128, 1 + TP], F32) for i_ in range(2)]
        pEs_ = [A.alloc(f"pEs{i_}", [128, SB, 5], F32) for i_ in range(2)]
        pcnt = [0]
        shs = A.alloc("shs", [128, 8, SB], F32)
        sho = A.alloc("sho", [128, 8, SB], F32)
        dtmp_ = [A.alloc(f"dtmp{i_}", [128, TP], F32) for i_ in range(2)]
        dtmps_ = [A.alloc(f"dtmps{i_}", [128, SB, 4], F32) for i_ in range(2)]
        rmask = A.alloc("rmask", [128, TW], F32)
        tw = A.alloc("tw", [128, TW], BF16)
        sgb = A.alloc("sgb", [128, TW], BF16)
        P.memset(rmask.full(), 1.0, eng="pool")
        P.memset(v3(rmask.full(), CW)[:, :, 0:1], 0.0, eng="pool")
        for c in range(8):
            P.dma("sp", shs[:, c, :], dr["st_shift"][l, s * SB:(s + 1) * SB, c * 128:(c + 1) * 128].rearrange("b p -> p b"),
                  chan="st", allow_slow_non_contiguous=True)

        def proj_chunk(c, lhs_fn, xs):
            pb_ = pcnt[0] % 2
            pcnt[0] += 1
            pE, pEs, dtmp, dtmps = pE_[pb_], pEs_[pb_], dtmp_[pb_], dtmps_[pb_]
            P.memset(xs[:, TP:TW], 0.0, eng="pool")
            P.copy(pE[:, 0:1], self.shcar[:, l, c:c + 1], eng="pool")
            P.copy(pEs[:, :, 0], shs[:, c, :], eng="pool")

            def ev(pv, c0, n):
                npp = max(0, min(c0 + n, TP) - c0)
                if npp > 0:
                    P.act(pE[:, 1 + c0:1 + c0 + npp], pv[:, 0:npp], AF.Identity, bias=self.prm(l, "brw", c))
                if npp < n:
                    lo = c0 + npp - TP
                    assert lo % 4 == 0 and (n - npp) % 4 == 0
                    ps_ = pv[:, npp:n]
                    P.act(pEs[:, lo // 4:(lo + n - npp) // 4, 1:5], ps_.with_ap(ps_.ap.rearrange("p (b t) -> p b t", t=4)), AF.Identity,
                          bias=self.prm(l, "brw", c))
            self.dense(128, 8, lhs_fn, lambda k, c0, n: self.hT[:, k, c0:c0 + n], ev)
            P.copy(self.shcar[:, l, c:c + 1], pE[:, TP:TP + 1], eng="pool")
            P.copy(sho[:, c, :], pEs[:, :, 4], eng="pool")
            mu = self.prm(l, "mu", c)
            P.tt(dtmp.full(), pE[:, 0:TP], pE[:, 1:TP + 1], ALU.subtract, eng="pool")
            P.stt(xs[:, 0:TP], dtmp.full(), mu, pE[:, 1:TP + 1], ALU.mult, ALU.add)
            P.tt(dtmps.full(), pEs[:, :, 0:4], pEs[:, :, 1:5], ALU.subtract, eng="pool")
            xs_s = v3(xs[:, TP:TW], CW)[:, :, 0:4]
            P.stt(xs_s, dtmps.full(), mu, pEs[:, :, 1:5], ALU.mult, ALU.add)

        m1 = A.mark()
        xs6 = A.alloc("xs6", [128, TW], F32)
        xs7 = A.alloc("xs7", [128, TW], F32)
        sl = self.wget(ids["lora"])
        proj_chunk(6, lambda k: self.wview(sl, 8, 256, k, 0, 128), xs6)
        proj_chunk(7, lambda k: self.wview(sl, 8, 256, k, 128, 128), xs7)
        P.act(tw[0:64, :], xs6[0:64, :], AF.Tanh)
        P.copy(tw[64:128, :], xs6[64:128, :])
        P.act(sgb.full(), xs7.full(), AF.Sigmoid)
        A.release(m1)

        for p in range(2):
            m1 = A.mark()
            sl = self.wget(ids[f"pair{p}"])
            f32t = lambda nm: A.alloc(nm, [128, TW], F32)
            b16t = lambda nm: A.alloc(nm, [128, TW], BF16)
            xr, xk, xv = f32t("xr"), f32t("xk"), f32t("xv")
            lw, aa, gg = f32t("lw"), f32t("aa"), f32t("gg")
            kk, k2, bn, cs, S1, S2 = f32t("kk"), f32t("k2"), f32t("bn"), f32t("cs"), f32t("S1"), f32t("S2")
            KR = A.alloc("KR", [128, NU, 2, CW], BF16)
            kt, bt, kG, bG, vb, tb = b16t("kt"), b16t("bt"), b16t("kG"), b16t("bG"), b16t("vb"), b16t("tb")
            gc = A.alloc("gc", [128, NU], F32)
            TOK = A.alloc("TOK", [64, NG, 3, 2, 128], BF16)
            MT = A.alloc("MT", [64, NG, 2, 2, 2, CW], BF16)
            ML = A.alloc("ML", [64, NG, 2, CW], BF16)
            AV = A.alloc("AV", [64, NG, 128], F32)
            NM = [A.alloc(f"NM{j}", [64, NG, 2, 2, CW], BF16) for j in range(2)]
            XX = [A.alloc(f"XX{j}", [64, NG, 2, CW], BF16) for j in range(2)]
            Wsb = A.alloc("Wsb", [64, 2 + SB, 128], BF16)
            Upad = A.alloc("Upad", [64, 2 + SB, 2, 128], BF16)
            Hs = A.alloc("Hs", [128, SB, 2, 128], F32)
            HT = A.alloc("HT", [128, 128], F32)
            ST = A.alloc("ST", [64, SB, 2, 128], F32)
            P.memset(TOK.full(), 0.0, eng="pool")
            P.memset(Upad.full(), 0.0, eng="pool")
            P.memset(ST.full(), 0.0, eng="pool")
            for b in range(SB):
                gb_ = s * SB + b
                for hh in range(2):
                    P.dma("sp", ST[:, b, hh, hh * 64:(hh + 1) * 64], dr["st_wkv"][l, gb_, 2 * p + hh], chan="st")
            for b in range(SB):
                ph = self.ps(128, 128)
                for hh in range(2):
                    P.transpose(ph[:, hh * 64:(hh + 1) * 64], ST[:, b, hh, :], self.identf[0:64, 0:64])
                P.copy(Hs[:, b, 0, :], ph)

            proj_chunk(p, lambda k: self.wview(sl, 8, 384, k, 0, 128), xr)
            proj_chunk(2 + p, lambda k: self.wview(sl, 8, 384, k, 128, 128), xk)
            proj_chunk(4 + p, lambda k: self.wview(sl, 8, 384, k, 256, 128), xv)
            def prep(ua, ub):
                ca, cb = ua * CW, ub * CW
                C = slice(ca, cb)
                n = cb - ca
                assert n <= 512
                pv = self.ps(128, n)
                P.mm(pv, self.wup[0:64, l, p * 128:(p + 1) * 128], tw[0:64, C])
                P.act(lw[:, C], pv, AF.Sigmoid, bias=self.prm(l, "w0", p)); yield
                pv = self.ps(128, n)
                P.mm(pv, self.aup[64:128, l, p * 128:(p + 1) * 128], tw[64:128, C])
                P.act(aa[:, C], pv, AF.Sigmoid, bias=self.prm(l, "a0", p)); yield
                pv = self.ps(128, n)
                P.mm(pv, self.gup[:, l, p * 128:(p + 1) * 128], sgb[:, C])
                P.copy(gg[:, C], pv, eng="act"); yield
                P.ts(lw[:, C], lw[:, C], DEC_C, ALU.mult); yield
                if ub > NCH:
                    cs0 = max(ua, NCH) * CW
                    P.memset(v3(lw[:, cs0:cb], CW)[:, :, 4:CW], 0.0, eng="pool"); yield
                P.ts(kk[:, C], xk[:, C], self.prm(l, "kk", p), ALU.mult); yield
                P.act(tb[:, C], kk[:, C], AF.Square); yield
                pv = self.ps(128, n)
                P.mm(pv, self.onesbd.full(), tb[:, C])
                P.ts(S1[:, C], pv, 1e-18, ALU.max); yield
                P.act(S1[:, C], S1[:, C], AF.Ln); yield
                P.act(S1[:, C], S1[:, C], AF.Exp, scale=-0.5); yield
                P.tt(kk[:, C], kk[:, C], S1[:, C], ALU.mult); yield
                P.ts(S2[:, C], aa[:, C], self.prm(l, "ka", p), ALU.mult, self.prm(l, "ka1", p), ALU.add); yield
                P.tt(k2[:, C], xk[:, C], S2[:, C], ALU.mult); yield
                P.stt(bn[:, C], kk[:, C], -1.0, aa[:, C], ALU.mult, ALU.mult); yield
                P.stt(tb[:, C], xr[:, C], self.prm(l, "rk", p), k2[:, C], ALU.mult, ALU.mult); yield
                pv = self.ps(128, n)
                P.mm(pv, self.onesbd.full(), tb[:, C])
                P.tt(xk[:, C], pv, xv[:, C], ALU.mult); yield
                csv, lwv, rmv = cs[:, C], lw[:, C], rmask[:, C]
                P.add("dve", lambda e, csv=csv, lwv=lwv, rmv=rmv: e.tensor_tensor_scan(csv.ap, rmv.ap, lwv.ap, 0.0, ALU.mult, ALU.add),
                      [rmv, lwv], [csv]); yield
                P.act(S1[:, C], cs[:, C], AF.Exp); yield
                P.copy(gc[:, ua:ub], v3(S1[:, C], CW)[:, :, CW - 1], eng="pool"); yield
                P.tt(xr[:, C], xr[:, C], S1[:, C], ALU.mult); yield
                P.copy(KR[:, ua:ub, 1, :], v3(xr[:, C], CW), eng="act"); yield
                P.tt(S2[:, C], cs[:, C], lw[:, C], ALU.subtract); yield
                P.act(S2[:, C], S2[:, C], AF.Exp); yield
                P.tt(kk[:, C], kk[:, C], S2[:, C], ALU.mult); yield
                P.copy(KR[:, ua:ub, 0, :], v3(kk[:, C], CW), eng="act"); yield
                P.act(S1[:, C], cs[:, C], AF.Exp, scale=-1.0); yield
                P.tt(kt[:, C], k2[:, C], S1[:, C], ALU.mult); yield
                P.tt(bt[:, C], bn[:, C], S1[:, C], ALU.mult); yield
                for u in range(ua, ub):
                    P.act(S2[:, u * CW:(u + 1) * CW], cs[:, u * CW:(u + 1) * CW], AF.Exp, bias=cs[:, u * CW + CW - 1:u * CW + CW], scale=-1.0)
                yield
                P.tt(kG[:, C], k2[:, C], S2[:, C], ALU.mult); yield
                P.tt(bG[:, C], bn[:, C], S2[:, C], ALU.mult); yield
                P.copy(vb[:, C], xv[:, C], eng="act"); yield
            bonus = xk
            nstr = 2 if NU >= 4 else 1
            cuts = [round(i_ * NU / nstr) for i_ in range(nstr + 1)]
            run_rr([prep(cuts[i_], cuts[i_ + 1]) for i_ in range(nstr)])
            yT = lw

            for g0 in range(0, NU, NG):
                ng = min(NG, NU - g0)
                for i in range(ng):
                    u = g0 + i
                    cu = slice(u * CW, (u + 1) * CW)
                    pb = self.ps(64, 192).bitcast(BF16)
                    for w_, src in enumerate((vb, kG, bG)):
                        P.transpose(pb[:, w_ * 128:(w_ + 1) * 128], src[:, cu], self.ident.full())
                    tv = TOK[:, i, :, :, :]
                    flat = TOK.base[0:64, i, :, :, :]
                    dst_ap = flat.rearrange("p w h x -> p (w h x)").rearrange("p (w a b) -> p w a b", w=3, a=4, b=64)[:, :, 0:4:3, :]
                    P.copy(tv.with_ap(dst_ap), pb.with_ap(pb.ap.rearrange("p (w h b) -> p w h b", w=3, h=2)), eng="act")
                for i0 in range(0, ng, 2):
                    n2 = min(2, ng - i0)
                    pbank = [self.ps(64, 512), self.ps(64, 512)]
                    for ii in range(n2):
                        i = i0 + ii
                        u = g0 + i
                        cu = slice(u * CW, (u + 1) * CW)
                        for hh in range(2):
                            hs = slice(hh * 64, (hh + 1) * 64)
                            P.mm(pbank[hh][:, ii * 256:ii * 256 + 128], bt[hs, cu], KR[hs, u, :, :])
                            P.mm(pbank[hh][:, ii * 256 + 128:ii * 256 + 256], kt[hs, cu], KR[hs, u, :, :])
                    for hh in range(2):
                        dst = MT[:, i0:i0 + n2, hh, :, :, :]
                        P.tt(dst, pbank[hh][:, 0:n2 * 256].with_ap(pbank[hh][:, 0:n2 * 256].ap.rearrange("p (i w a t) -> p i w a t", i=n2, w=2, a=2)),
                             self.trimx[:, 0:n2, :, :, :], ALU.mult, eng=("dve" if hh == 0 else "dve"))
                pm = [self.ps(64, ng * 64), self.ps(64, ng * 64)]
                for i in range(ng):
                    u = g0 + i
                    cu = slice(u * CW, (u + 1) * CW)
                    for hh in range(2):
                        hs = slice(hh * 64, (hh + 1) * 64)
                        P.mm(pm[hh][:, i * 64:(i + 1) * 64], KR[hs, u, 0, :], bt[hs, cu])
                for hh in range(2):
                    P.tt(ML[:, 0:ng, hh, :], pm[hh].with_ap(pm[hh].ap.rearrange("p (i s) -> p i s", i=ng)), self.trilox[:, 0:ng, :], ALU.mult)
                for i0 in range(0, ng, 4):
                    n4 = min(4, ng - i0)
                    pv_ = self.ps(64, n4 * 128)
                    for ii in range(n4):
                        i = i0 + ii
                        for hh in range(2):
                            P.mm(pv_[:, ii * 128 + hh * 64:ii * 128 + hh * 64 + 64], MT[:, i, hh, 1, 0, :], TOK[:, i, 0, hh, hh * 64:(hh + 1) * 64])
                    P.copy(AV[:, i0:i0 + n4, :], pv_.with_ap(pv_.ap.rearrange("p (i x) -> p i x", i=n4)), eng="act")
                Nv = lambda j, i, hh: (MT[:, i, hh, 0, 0, :] if j == 0 else NM[j % 2][:, i, hh, 0, :])
                Mv = lambda j, i, hh: (ML[:, i, hh, :] if j == 0 else NM[j % 2][:, i, hh, 1, :])
                P.tt(XX[0][:, 0:ng, :, :], MT[:, 0:ng, :, 0, 0, :], self.identx[:, 0:ng, :, :], ALU.add)
                for j in range(1, 6):
                    for i0 in range(0, ng, 2):
                        n2 = min(2, ng - i0)
                        pn = self.ps(64, n2 * 256)
                        for ii in range(n2):
                            i = i0 + ii
                            for hh in range(2):
                                o = ii * 256 + hh * 128
                                if j <= 4:
                                    P.mm(pn[:, o:o + 64], Mv(j - 1, i, hh), Nv(j - 1, i, hh))
                                P.mm(pn[:, o + 64:o + 128], Nv(j - 1, i, hh), Mv(j - 1, i, hh))
                        src = pn.with_ap(pn.ap.rearrange("p (i h a t) -> p i h a t", i=n2, h=2, a=2))
                        if j <= 4:
                            P.copy(NM[j % 2][:, i0:i0 + n2, :, :, :], src, eng="act")
                        else:
                            P.copy(NM[j % 2][:, i0:i0 + n2, :, 1, :], src[:, :, :, 1, :], eng="act")
                    for i0 in range(0, ng, 4):
                        n4 = min(4, ng - i0)
                        px = self.ps(64, n4 * 128)
                        for ii in range(n4):
                            i = i0 + ii
                            for hh in range(2):
                                P.mm(px[:, ii * 128 + hh * 64:ii * 128 + hh * 64 + 64], Mv(j, i, hh), XX[(j - 1) % 2][:, i, hh, :])
                        P.tt(XX[j % 2][:, i0:i0 + n4, :, :], px.with_ap(px.ap.rearrange("p (i h t) -> p i h t", i=n4, h=2)),
                             XX[(j - 1) % 2][:, i0:i0 + n4, :, :], ALU.add)
                TT = XX[5 % 2]
                def serial(units):
                    for i in units:
                        u = g0 + i
                        cu = slice(u * CW, (u + 1) * CW)
                        is_s = u >= NCH
                        if not is_s:
                            Hcur = self.Hst[:, l, p, u % 2, :]
                            Hnext = self.Hst[:, l, p, (u + 1) % 2, :]
                        else:
                            b = u - NCH
                            gb_ = s * SB + b
                            Hcur = Hs[:, b, 0, :]
                            Hnext = Hs[:, b, 1, :]
                        pp_ = (u % 2) if not is_s else (2 + u - NCH)
                        pw = self.ps(64, 128)
                        P.mm(pw, kk[:, cu], Hcur)
                        P.tt(Wsb[:, pp_, :], pw, AV[:, i, :], ALU.add)
                        yield
                        pu = self.ps(64, 128)
                        for hh in range(2):
                            P.mm(pu[:, hh * 64:(hh + 1) * 64], TT[:, i, hh, :], Wsb[:, pp_, hh * 64:(hh + 1) * 64])
                        ud = Upad[:, pp_, :, :]
                        ud_ap = Upad.base[0:64, pp_, :, :].rearrange("p h x -> p (h x)").rearrange("p (a b) -> p a b", a=4, b=64)[:, 0:4:3, :]
                        P.copy(ud.with_ap(ud_ap), pu.with_ap(pu.ap.rearrange("p (h b) -> p h b", h=2)), eng="act")
                        yield
                        py = self.ps(128, 64)
                        P.mm(py, Hcur, xr[:, cu], start=True, stop=False)
                        for hh in range(2):
                            P.mm(py, TOK[:, i, 0, hh, :], MT[:, i, hh, 1, 1, :], start=False, stop=False)
                        for hh in range(2):
                            P.mm(py, Upad[:, pp_, hh, :], MT[:, i, hh, 0, 1, :], start=False, stop=(hh == 1))
                        P.copy(yT[:, cu], py, eng="act")
                        ph2 = self.ps(128, 128)
                        for hh in range(2):
                            P.mm(ph2, TOK[:, i, 1, hh, :], TOK[:, i, 0, hh, :], start=(hh == 0), stop=False)
                        for hh in range(2):
                            P.mm(ph2, TOK[:, i, 2, hh, :], Upad[:, pp_, hh, :], start=False, stop=(hh == 1))
                        P.stt(Hnext, Hcur, gc[:, u:u + 1], ph2, ALU.mult, ALU.add)
                        yield
                        if is_s:
                            self.wkv_out(Hnext, HT, dr["wkvs"][l, gb_], p)
                            yield
                pu_ = [i for i in range(ng) if g0 + i < NCH]
                su_ = [i for i in range(ng) if g0 + i >= NCH]
                gens_ = ([serial(pu_)] if pu_ else []) + [serial([i]) for i in su_]
                run_rr(gens_)
            if last_seg:
                self.wkv_out(self.Hst[:, l, p, 0, :], HT, dr["wkvp"][l], p)

            def gnorm(ua, ub):
                ca, cb = ua * CW, ub * CW
                C = slice(ca, cb)
                n = cb - ca
                P.copy(tb[:, C], yT[:, C], eng="act"); yield
                P.act(kt[:, C], yT[:, C], AF.Square); yield
                p1 = self.ps(128, n)
                P.mm(p1, self.onesbd.full(), tb[:, C])
                p2 = self.ps(128, n)
                P.mm(p2, self.onesbd.full(), kt[:, C])
                P.ts(S1[:, C], p1, 1.0 / 64, ALU.mult); yield
                P.tt(S2[:, C], S1[:, C], S1[:, C], ALU.mult); yield
                P.stt(S2[:, C], p2, 1.0 / 64, S2[:, C], ALU.mult, ALU.subtract); yield
                P.ts(S2[:, C], S2[:, C], 0.0, ALU.max); yield
                P.act(S2[:, C], S2[:, C], AF.Ln, bias=self.epsc[:, 1:2], scale=1.0); yield
                P.act(S2[:, C], S2[:, C], AF.Exp, scale=-0.5); yield
                P.tt(yT[:, C], yT[:, C], S1[:, C], ALU.subtract); yield
                P.tt(yT[:, C], yT[:, C], S2[:, C], ALU.mult); yield
                P.ts(yT[:, C], yT[:, C], self.prm(l, "lnw", p), ALU.mult, self.prm(l, "lnb", p), ALU.add); yield
                P.tt(yT[:, C], yT[:, C], bonus[:, C], ALU.add); yield
            run_rr([gnorm(cuts[i_], cuts[i_ + 1]) for i_ in range(nstr)])
            P.tt(self.mixT[:, 4 + p, 0:TP], yT[:, 0:TP], gg[:, 0:TP], ALU.mult)
            ms = self.mixT[:, 4 + p, TP:T]
            P.tt(ms.with_ap(ms.ap.rearrange("p (b t) -> p b t", t=4)), v3(yT[:, TP:TW], CW)[:, :, 0:4], v3(gg[:, TP:TW], CW)[:, :, 0:4], ALU.mult)
            A.release(m1)

        for c in range(8):
            P.dma("sp", dr["shs"][l, s * SB:(s + 1) * SB, c * 128:(c + 1) * 128].rearrange("b p -> p b"), sho[:, c, :], chan="outs",
                  allow_slow_non_contiguous=True)
        if last_seg:
            P.dma("sp", dr["shp"][l].rearrange("(c p) -> p c", p=128), self.shcar[:, l, :], chan="outs", allow_slow_non_contiguous=True)

    def wkv_out(self, H, HT, dst, p):
        P = self.P
        pt = self.ps(128, 128)
        P.transpose(pt, H, self.identf.full())
        P.copy(HT.full(), pt)
        for hh in range(2):
            P.dma("sp", dst[2 * p + hh], HT[hh * 64:(hh + 1) * 64, hh * 64:(hh + 1) * 64], chan="outs")

WNAMES = [("norm_mix_pre", [D]), ("norm_mix_post", [D]), ("norm_ffn_pre", [D]), ("norm_ffn_post", [D]),
          ("w_in", [D, NIN]), ("b_in", [NIN]), ("attn_sinks", [8]), ("rwkv_mu", [1024]), ("rwkv_w0", [256]),
          ("rwkv_w_up", [64, 256]), ("rwkv_a0", [256]), ("rwkv_a_up", [64, 256]), ("rwkv_g_up", [128, 256]),
          ("rwkv_k_k", [256]), ("rwkv_k_a", [256]), ("rwkv_r_k", [4, 64]), ("rwkv_ln_w", [256]), ("rwkv_ln_b", [256]),
          ("lru_conv_w", [4, 256]), ("lru_conv_b", [256]), ("lru_w_a", [4, 64, 64]), ("lru_b_a", [256]),
          ("lru_w_i", [4, 64, 64]), ("lru_b_i", [256]), ("lru_L", [256]), ("w_out", [D, D]), ("b_out", [D]),
          ("ffn_w_gate", [D, DFF]), ("ffn_w_up", [D, DFF]), ("ffn_w_down", [DFF, D]), ("ple_w", [PLE, D]),
          ("ple_gate_w", [D, D])]


def build(cfg):
    nc = bass.Bass("TRN2", target_bir_lowering=False)
    dr = {}
    L, SEQ, SBC = cfg.L, cfg.SEQ, cfg.SBC

    def inp(name, shape):
        dr[name] = nc.dram_tensor(name, list(shape), F32, kind="ExternalInput").ap()

    def outp(name, shape):
        dr[name] = nc.dram_tensor(name, list(shape), F32, kind="ExternalOutput").ap()
    inp("xp", [SEQ, D]); inp("xs", [SBC * 4, D]); inp("ck", [L, SBC, 128, 128]); inp("cv", [L, SBC, 128, 128])
    inp("st_shift", [L, SBC, 1024]); inp("st_wkv", [L, SBC, 4, 64, 64]); inp("st_conv", [L, SBC, 3, 256])
    inp("st_lru", [L, SBC, 256]); inp("pp", [L, SEQ, PLE]); inp("psm", [L, SBC * 4, PLE])
    for n, sh in WNAMES:
        inp(n, [L] + sh)
    outp("yp", [SEQ, D]); outp("ys", [SBC * 4, D]); outp("kp", [L, 128, 128]); outp("vp", [L, 128, 128])
    outp("shp", [L, 1024]); outp("wkvp", [L, 4, 64, 64]); outp("convp", [L, 3, 256]); outp("lrup", [L, 256])
    outp("ks", [L, SBC, 128, 128]); outp("vs", [L, SBC, 128, 128]); outp("shs", [L, SBC, 1024])
    outp("wkvs", [L, SBC, 4, 64, 64]); outp("convs", [L, SBC, 3, 256]); outp("lrus", [L, SBC, 256])
    P = Prog(nc)
    K = Kern(P, cfg, dr)
    K.setup()
    ids = [[K.register_layer(l) for l in range(L)] for s in range(cfg.NSEG)]
    for s in range(cfg.NSEG):
        K.load_segment(s)
        for l in range(L):
            K.layer(l, s, ids[s][l])
        K.store_segment(s)
    P.emit()
    return nc, P, K


_CACHE = {}


def run(cfg, inputs):
    key = (cfg.L, cfg.SEQ, cfg.NSEG, cfg.SBC)
    if key not in _CACHE:
        _CACHE[key] = build(cfg)[0]
    nc = _CACHE[key]
    L, SBC = cfg.L, cfg.SBC
    f = lambda a: np.ascontiguousarray(np.asarray(a, dtype=np.float32))
    in_maps = []
    for i in range(8):
        b = i % 4
        sl = slice(i * SBC, (i + 1) * SBC)
        m = {"xp": f(inputs["x_prompt"][b]), "xs": f(inputs["x_sample"][sl]).reshape(SBC * 4, D),
             "ck": f(inputs["cache_k"][:, sl]).reshape(L, SBC, 128, 128), "cv": f(inputs["cache_v"][:, sl]).reshape(L, SBC, 128, 128),
             "st_shift": f(inputs["state_shift"][:, sl]), "st_wkv": f(inputs["state_wkv"][:, sl]),
             "st_conv": f(inputs["state_conv"][:, sl]), "st_lru": f(inputs["state_lru"][:, sl]),
             "pp": f(inputs["p_prompt"][:, b]), "psm": f(inputs["p_sample"][:, sl]).reshape(L, SBC * 4, PLE)}
        for n, sh in WNAMES:
            m[n] = f(inputs[n])
        in_maps.append(m)
    res = run_bass_kernel_spmd(nc, in_maps, core_ids=list(range(8)))
    R = res.results
    DECB = 8 * SBC
    cat_p = lambda k: np.stack([R[i][k] for i in range(4)], axis=0)
    cat_pl = lambda k: np.stack([R[i][k] for i in range(4)], axis=1)
    cat_s = lambda k: np.concatenate([R[i][k] for i in range(8)], axis=1)
    yp = cat_p("yp")
    ys = np.concatenate([R[i]["ys"] for i in range(8)], axis=0).reshape(DECB, 4, D)
    outs = (yp, ys, cat_pl("kp").reshape(L, 4, 128, 2, 64), cat_pl("vp").reshape(L, 4, 128, 2, 64), cat_pl("shp"), cat_pl("wkvp"),
            cat_pl("convp"), cat_pl("lrup"), cat_s("ks").reshape(L, DECB, 128, 2, 64), cat_s("vs").reshape(L, DECB, 128, 2, 64),
            cat_s("shs"), cat_s("wkvs"), cat_s("convs"), cat_s("lrus"))
    return tuple(np.ascontiguousarray(o, dtype=np.float32) for o in outs)


def kernel(**inputs):
    cfg = Cfg(L=4, SEQ=4096, NSEG=8, DECB=128)
    return run(cfg, inputs)
```

```python
import math
import numpy as np
from contextlib import ExitStack
import concourse.bass as bass
import concourse.mybir as mybir
from concourse.bass_utils import run_bass_kernel_spmd


F32 = mybir.dt.float32
BF16 = mybir.dt.bfloat16
AF = mybir.ActivationFunctionType
ALU = mybir.AluOpType
AX = mybir.AxisListType

_DTSIZE = {F32: 4, BF16: 2, mybir.dt.int32: 4, mybir.dt.uint32: 4}

SAME_ENGINE_SYNC = True
SEM_ROLL = 30000


class View:
    __slots__ = ("tile", "ap", "p0", "p1", "b0", "b1")

    def __init__(self, tile, ap, p0, p1, b0, b1):
        self.tile = tile
        self.ap = ap
        self.p0, self.p1, self.b0, self.b1 = p0, p1, b0, b1

    def with_ap(self, ap):
        return View(self.tile, ap, self.p0, self.p1, self.b0, self.b1)

    def bitcast(self, dt):
        return View(self.tile, self.ap.bitcast(dt), self.p0, self.p1, self.b0, self.b1)

    def __getitem__(self, idx):
        return View(self.tile, self.ap[idx], self.p0, self.p1, self.b0, self.b1)


class Tile:
    _n = 0

    def __init__(self, handle, name, shape, dtype, space):
        self.h = handle
        self.name = name
        self.shape = list(shape)
        self.dtype = dtype
        self.space = space
        self.esz = _DTSIZE[dtype]
        self.id = Tile._n
        Tile._n += 1
        st = [1] * len(shape)
        for i in range(len(shape) - 2, 0, -1):
            st[i] = st[i + 1] * shape[i + 1]
        self.strides = st
        self.w = []
        self.r = []

    def __getitem__(self, idx):
        if not isinstance(idx, tuple):
            idx = (idx,)
        idx = list(idx) + [slice(None)] * (len(self.shape) - len(idx))
        lo = 0
        hi = 0
        p0, p1 = 0, self.shape[0]
        for d, (ix, n) in enumerate(zip(idx, self.shape)):
            if isinstance(ix, int):
                if ix < 0:
                    ix += n
                s, e, stp = ix, ix + 1, 1
            else:
                s, e, stp = ix.indices(n)
            assert 0 <= s < e <= n, (self.name, idx, self.shape)
            last = s + ((e - 1 - s) // stp) * stp
            if d == 0:
                p0, p1 = s, last + 1
            else:
                lo += s * self.strides[d]
                hi += last * self.strides[d]
        b0, b1 = lo * self.esz, (hi + 1) * self.esz
        if self.space == "psum":
            p0, p1 = 0, 128
            b0 = b0 // 2048 * 2048
            b1 = (b1 + 2047) // 2048 * 2048
        return View(self, self.h[tuple(idx)], p0, p1, b0, b1)

    def full(self):
        return self[tuple(slice(None) for _ in self.shape)]


class VTile:
    def __init__(self, arena, off_bytes, shape, dtype, name=""):
        self.arena = arena
        self.name = name
        self.shape = list(shape)
        self.dtype = dtype
        self.esz = _DTSIZE[dtype]
        self.off = off_bytes
        n = 1
        for d in shape[1:]:
            n *= d
        self.nbytes = n * self.esz
        assert off_bytes % 4 == 0
        n4 = (self.nbytes + 3) // 4
        base = arena.h[0:shape[0], off_bytes // 4: off_bytes // 4 + n4]
        if dtype != arena.dtype:
            base = base.bitcast(dtype)
        if self.nbytes % 4 != 0:
            base = base[:, 0:n]
        if len(shape) > 2:
            names = [f"d{i}" for i in range(1, len(shape))]
            kw = {nm: shape[i + 1] for i, nm in enumerate(names)}
            base = base.rearrange("p (" + " ".join(names) + ") -> p " + " ".join(names), **kw)
        self.base = base
        st = [1] * len(shape)
        for i in range(len(shape) - 2, 0, -1):
            st[i] = st[i + 1] * shape[i + 1]
        self.strides = st

    def __getitem__(self, idx):
        if not isinstance(idx, tuple):
            idx = (idx,)
        idx = list(idx) + [slice(None)] * (len(self.shape) - len(idx))
        lo = 0
        hi = 0
        p0, p1 = 0, self.shape[0]
        for d, (ix, n) in enumerate(zip(idx, self.shape)):
            if isinstance(ix, int):
                if ix < 0:
                    ix += n
                s, e, stp = ix, ix + 1, 1
            else:
                s, e, stp = ix.indices(n)
            assert 0 <= s < e <= n, (self.name, idx, self.shape)
            last = s + ((e - 1 - s) // stp) * stp
            if d == 0:
                p0, p1 = s, last + 1
            else:
                lo += s * self.strides[d]
                hi += last * self.strides[d]
        return View(self.arena, self.base[tuple(idx)], p0, p1, self.off + lo * self.esz, self.off + (hi + 1) * self.esz)

    def full(self):
        return self[tuple(slice(None) for _ in self.shape)]


class Arena:
    def __init__(self, tile):
        self.tile = tile
        self.top = 0
        self.cap = tile.shape[1] * tile.esz
        self.peak = 0

    def alloc(self, name, shape, dtype):
        n = 1
        for d in shape[1:]:
            n *= d
        nb = (n * _DTSIZE[dtype] + 31) // 32 * 32
        assert self.top + nb <= self.cap, f"arena overflow {name} {self.top}+{nb}>{self.cap}"
        v = VTile(self.tile, self.top, shape, dtype, name)
        self.top += nb
        self.peak = max(self.peak, self.top)
        return v

    def mark(self):
        return self.top

    def release(self, m):
        self.top = m


def _ov(a, b):
    return a[0] < b[1] and b[0] < a[1] and a[2] < b[3] and b[2] < a[3]


def _cov(a, b):
    return a[0] <= b[0] and a[1] >= b[1] and a[2] <= b[2] and a[3] >= b[3]


class Op:
    __slots__ = ("eng", "fn", "deps", "signal", "chan", "idx", "cnt", "semk", "name", "isdma")

    def __init__(self, eng, fn, chan, name):
        self.eng = eng
        self.fn = fn
        self.deps = set()
        self.signal = False
        self.chan = chan
        self.cnt = None
        self.semk = None
        self.name = name
        self.isdma = chan is not None


class Prog:
    ENGS = ("pe", "act", "dve", "pool", "sp")

    def __init__(self, nc):
        self.nc = nc
        self.ops = []
        self.stack = ExitStack()
        self.tiles = []
        self.chan_count = {}

    def sbuf(self, name, shape, dtype):
        h = self.stack.enter_context(self.nc.sbuf_tensor(name, list(shape), dtype))
        t = Tile(h, name, shape, dtype, "sbuf")
        self.tiles.append(t)
        return t

    def psum(self, name, shape, dtype):
        h = self.stack.enter_context(self.nc.psum_tensor(name, list(shape), dtype))
        t = Tile(h, name, shape, dtype, "psum")
        self.tiles.append(t)
        return t

    def add(self, eng, fn, reads=(), writes=(), chan=None, name=""):
        op = Op(eng, fn, chan, name)
        op.idx = len(self.ops)
        self.ops.append(op)
        isdma = chan is not None
        for v in reads:
            if v is None:
                continue
            t = v.tile
            reg = (v.p0, v.p1, v.b0, v.b1)
            for w in t.w:
                if _ov(w, reg):
                    op.deps.add(w[4])
            if not isdma:
                t.r = [r for r in t.r if not (r[5] == eng and _cov(reg, r))]
            t.r.append((v.p0, v.p1, v.b0, v.b1, op.idx, None if isdma else eng))
        for v in writes:
            if v is None:
                continue
            t = v.tile
            reg = (v.p0, v.p1, v.b0, v.b1)
            for w in t.w:
                if _ov(w, reg):
                    op.deps.add(w[4])
            for r in t.r:
                if _ov(r, reg) and r[4] != op.idx:
                    op.deps.add(r[4])
            t.w = [w for w in t.w if not _cov(reg, w)]
            t.r = [r for r in t.r if not _cov(reg, r) or r[4] == op.idx]
            t.w.append((v.p0, v.p1, v.b0, v.b1, op.idx))
        op.deps.discard(op.idx)
        return op

    def dma(self, q, out, in_, chan, reads=(), writes=(), **kw):
        oa = out.ap if isinstance(out, View) else out
        ia = in_.ap if isinstance(in_, View) else in_
        rd = list(reads) + ([in_] if isinstance(in_, View) else [])
        wr = list(writes) + ([out] if isinstance(out, View) else [])
        return self.add(q, lambda e: e.dma_start(out=oa, in_=ia, **kw), rd, wr, chan=chan, name="dma")

    def mm(self, out, lhsT, rhs, start=True, stop=True, **kw):
        return self.add(
            "pe",
            lambda e: e.matmul(out.ap, lhsT.ap, rhs.ap, start=start, stop=stop, **kw),
            [lhsT, rhs] + ([] if start else [out]),
            [out],
            name="mm",
        )

    def transpose(self, out, in_, ident):
        return self.add("pe", lambda e: e.transpose(out.ap, in_.ap, ident.ap), [in_, ident], [out], name="tr")

    def act(self, out, in_, func, bias=None, scale=None, accum=None, eng="act"):
        kw = {}
        rd = [in_]
        if bias is not None:
            if isinstance(bias, View):
                kw["bias"] = bias.ap
                rd.append(bias)
            else:
                kw["bias"] = bias
        if scale is not None:
            if isinstance(scale, View):
                kw["scale"] = scale.ap
                rd.append(scale)
            else:
                kw["scale"] = scale
        wr = [out]
        if accum is not None:
            kw["accum_out"] = accum.ap
            wr.append(accum)
        return self.add(eng, lambda e: e.activation(out.ap, in_.ap, func, **kw), rd, wr, name="act")

    def tt(self, out, in0, in1, op, eng="dve"):
        return self.add(eng, lambda e: e.tensor_tensor(out.ap, in0.ap, in1.ap, op), [in0, in1], [out], name="tt")

    def ts(self, out, in0, s1, op0, s2=None, op1=None, eng="dve", accum=None):
        rd = [in0]
        a1 = s1
        if isinstance(s1, View):
            a1 = s1.ap
            rd.append(s1)
        a2 = s2
        if isinstance(s2, View):
            a2 = s2.ap
            rd.append(s2)
        kw = {}
        wr = [out]
        if accum is not None:
            kw["accum_out"] = accum.ap
            wr.append(accum)
        if op1 is None:
            return self.add(eng, lambda e: e.tensor_scalar(out.ap, in0.ap, a1, None, op0, **kw), rd, wr, name="ts")
        return self.add(eng, lambda e: e.tensor_scalar(out.ap, in0.ap, a1, a2, op0, op1, **kw), rd, wr, name="ts")

    def stt(self, out, in0, scalar, in1, op0, op1, eng="dve"):
        rd = [in0, in1]
        a = scalar
        if isinstance(scalar, View):
            a = scalar.ap
            rd.append(scalar)
        return self.add(eng, lambda e: e.scalar_tensor_tensor(out.ap, in0.ap, a, in1.ap, op0, op1), rd, [out], name="stt")

    def copy(self, out, in_, eng="dve"):
        if eng == "act":
            return self.act(out, in_, AF.Copy)
        return self.add(eng, lambda e: e.tensor_copy(out.ap, in_.ap), [in_], [out], name="copy")

    def memset(self, out, val, eng="dve"):
        return self.add(eng, lambda e: e.memset(out.ap, val), [], [out], name="memset")

    DMA_POOL = {"sp": 24, "pool": 12, "act": 4}

    def emit(self, final_chans=()):
        nc = self.nc
        ops = self.ops
        for op in ops:
            for d in op.deps:
                ops[d].signal = True
        per = {e: [] for e in self.ENGS}
        for op in ops:
            per[op.eng].append(op)
        nsem = {}
        ndma = {}
        for e in self.ENGS:
            c = 0
            nd = 0
            K = self.DMA_POOL.get(e, 4)
            for op in per[e]:
                if op.isdma:
                    op.semk = ("dma", e, nd % K)
                    op.cnt = 16 * (nd // K + 1)
                    nd += 1
                elif op.signal:
                    k = c // SEM_ROLL
                    op.semk = (e, k)
                    op.cnt = c % SEM_ROLL + 1
                    c += 1
            nsem[e] = (c + SEM_ROLL - 1) // SEM_ROLL
            ndma[e] = nd
        sems = {}
        for e in self.ENGS:
            for k in range(max(nsem[e], 1)):
                sems[(e, k)] = self.stack.enter_context(nc.semaphore(f"s_{e}_{k}"))
            K = self.DMA_POOL.get(e, 4)
            for k in range(min(K, ndma[e])):
                sems[("dma", e, k)] = self.stack.enter_context(nc.semaphore(f"d_{e}_{k}"))
        self.nwaits = 0
        block = self.stack.enter_context(nc.Block())

        def section(ename):
            def body(eng):
                waited = {}
                K = self.DMA_POOL.get(ename, 4)
                nd = 0
                for op in per[ename]:
                    need = {}
                    for d in op.deps:
                        y = ops[d]
                        if (not y.isdma) and y.eng == ename and (ename == "pe" or not SAME_ENGINE_SYNC):
                            continue
                        k = y.semk
                        if y.cnt > need.get(k, 0):
                            need[k] = y.cnt
                    if op.isdma and nd >= K:
                        k = ("dma", ename, nd % K)
                        c = 16 * (nd // K)
                        if c > need.get(k, 0):
                            need[k] = c
                    for k, c in need.items():
                        if waited.get(k, 0) >= c:
                            continue
                        if k[0] != "dma":
                            later = any(kk[0] == k[0] and kk[1] > k[1] for kk in waited if kk[0] != "dma")
                            if later:
                                continue
                        eng.wait_ge(sems[k], c)
                        self.nwaits += 1
                        waited[k] = c
                    ins = op.fn(eng)
                    if op.isdma:
                        ins.then_inc(sems[op.semk], 16)
                        nd += 1
                    elif op.signal:
                        ins.then_inc(sems[op.semk], 1)
                for k in range(min(K, nd)):
                    cnt = 16 * ((nd - 1 - k) // K + 1)
                    if waited.get(("dma", ename, k), 0) < cnt:
                        eng.wait_ge(sems[("dma", ename, k)], cnt)
            return body

        block.tensor(section("pe"))
        block.scalar(section("act"))
        block.vector(section("dve"))
        block.gpsimd(section("pool"))
        block.sync(section("sp"))
        self.stack.close()


D = 1024
NIN = 2304
DFF = 2816
PLE = 256
RMS_EPS = 1e-6
GN_EPS = 64e-5
NEG = -30000.0


class Cfg:
    def __init__(self, L=4, SEQ=4096, NSEG=4, DECB=128):
        self.L = L
        self.SEQ = SEQ
        self.NSEG = NSEG
        self.TP = SEQ // NSEG
        self.SBC = DECB // 8
        self.SB = self.SBC // NSEG
        self.NS = self.SB * 4
        self.T = self.TP + self.NS
        self.NB = self.TP // 128
        self.NCH = self.TP // 64
        assert self.TP % 128 == 0 and self.SBC % NSEG == 0
        nblk = -(-self.T // 512)
        w = -(-self.T // nblk)
        w = -(-w // 4) * 4
        blks = []
        c = 0
        while c < self.T:
            n = min(w, self.T - c)
            blks.append((c, n))
            c += n
        self.blks = blks


PC = {}
_c = 0
for _n, _w in [("gpre", 8), ("gpost", 8), ("gfpre", 8), ("gfpost", 8), ("bq", 4), ("bkd", 2), ("brw", 8), ("blx", 2),
               ("blg", 2), ("mu", 8), ("w0", 2), ("a0", 2), ("kk", 2), ("ka", 2), ("rk", 2), ("lnw", 2), ("lnb", 2),
               ("cw", 8), ("cb", 2), ("ba", 2), ("bi", 2), ("Lp", 2), ("bout", 8), ("c8", 2), ("ka1", 2)]:
    PC[_n] = (_c, _w)
    _c += _w
NPRM = _c


class KernBase:
    def __init__(self, P, cfg, dr):
        self.P = P
        self.cfg = cfg
        self.dr = dr
        self.nc = P.nc
        self._bank = 0
        self.wq = []
        self.wq_issued = 0
        self.wq_slots = {}

    def bank(self, n=1):
        if self._bank + n > 8:
            self._bank = 0
        b = self._bank
        self._bank = (self._bank + n) % 8
        return b * 512

    def ps(self, p, ncols, nb=1, c0=None):
        if c0 is None:
            c0 = self.bank(nb)
        return self.PS[0:p, c0:c0 + ncols]

    def prm(self, l, name, i=0, n=1, p0=0, p1=128):
        c, w = PC[name]
        return self.PRM[p0:p1, l, c + i:c + i + n]

    def setup(self):
        P, cfg, dr = self.P, self.cfg, self.dr
        L, T = cfg.L, cfg.T
        self.PS = P.psum("PS", [128, 4096], F32)
        self.ident = P.sbuf("ident", [128, 128], BF16)
        self.identf = P.sbuf("identf", [128, 128], F32)
        self.ones = P.sbuf("ones", [128, 128], BF16)
        self.onesbd = P.sbuf("onesbd", [128, 128], BF16)
        self.maskb = P.sbuf("maskb", [128, 256], F32)
        self.maskf = P.sbuf("maskf", [128, 256], F32)
        self.trim = P.sbuf("trim", [64, 2, 64], F32)
        self.trilo = P.sbuf("trilo", [64, 64], F32)
        self.PRM = P.sbuf("PRM", [128, L, NPRM], F32)
        self.bkv = P.sbuf("bkv", [128, L, 256], F32)
        self.sinkb = P.sbuf("sinkb", [128, L, 8], F32)
        self.wup = P.sbuf("wup", [128, L, 256], BF16)
        self.aup = P.sbuf("aup", [128, L, 256], BF16)
        self.gup = P.sbuf("gup", [128, L, 256], BF16)
        self.wabd = P.sbuf("wabd", [128, L, 2, 128], BF16)
        self.wibd = P.sbuf("wibd", [128, L, 2, 128], BF16)
        self.epsc = P.sbuf("epsc", [128, 2], F32)
        self.kcar = P.sbuf("kcar", [128, L, 2, 128], BF16)
        self.vcar = P.sbuf("vcar", [128, L, 512], BF16)
        self.shcar = P.sbuf("shcar", [128, L, 8], F32)
        self.Hst = P.sbuf("Hst", [128, L, 2, 2, 128], F32)
        self.cvcar = P.sbuf("cvcar", [128, L, 2, 3], F32)
        self.hcar = P.sbuf("hcar", [128, L, 2], F32)
        self.xT = P.sbuf("xT", [128, 8, T], F32)
        self.hT = P.sbuf("hT", [128, 8, T], BF16)
        self.mixT = P.sbuf("mixT", [128, 8, T], BF16)
        self.rstd = P.sbuf("rstd", [128, T], F32)
        self.lntmp = P.sbuf("lntmp", [128, 512], F32)
        self.stage = [P.sbuf(f"stage{i}", [128, 1024], F32) for i in range(2)]
        self.NSLOT = 4
        self.wslot = [P.sbuf(f"wslot{i}", [128, 4096], BF16) for i in range(self.NSLOT)]
        self.AR = P.sbuf("arena", [128, self.ARENA_BYTES // 4], F32)
        self.arena = Arena(self.AR)

        P.memset(self.identf.full(), 1.0)
        idf = self.identf.full()
        P.add("pool", lambda e: e.affine_select(idf.ap, idf.ap, [[-1, 128]], ALU.is_equal, 0.0, base=0, channel_multiplier=1),
              [idf], [idf], name="ident")
        P.copy(self.ident.full(), idf)
        P.memset(self.ones.full(), 1.0)
        P.memset(self.onesbd.full(), 0.0)
        P.memset(self.onesbd[0:64, 0:64], 1.0)
        P.memset(self.onesbd[64:128, 64:128], 1.0)
        P.memset(self.epsc[:, 0:1], RMS_EPS)
        P.memset(self.epsc[:, 1:2], GN_EPS)

        def asel(view, pattern, op, fill, base, cm):
            P.add("pool", lambda e: e.affine_select(view.ap, view.ap, pattern, op, fill, base=base, channel_multiplier=cm),
                  [view], [view], name="asel")
        for m in (self.maskb, self.maskf):
            P.memset(m.full(), 0.0)
            asel(m.full(), [[1, 256]], ALU.is_ge, NEG, 0, -1)
            asel(m.full(), [[-1, 256]], ALU.is_ge, NEG, 128, 1)
        asel(self.maskf.full(), [[1, 256]], ALU.is_ge, NEG, -128, 0)
        P.memset(self.trim.full(), 1.0)
        asel(self.trim[:, 0, :], [[1, 64]], ALU.is_gt, 0.0, 0, -1)
        asel(self.trim[:, 1, :], [[1, 64]], ALU.is_ge, 0.0, 0, -1)
        P.memset(self.trilo.full(), 1.0)
        asel(self.trilo.full(), [[-1, 64]], ALU.is_gt, 0.0, 0, 1)

        for t_ in (self.kcar, self.vcar, self.shcar, self.Hst, self.cvcar, self.hcar, self.wabd, self.wibd):
            P.memset(t_.full(), 0.0, eng="pool")

        def ld(name, src, i=0):
            c, w = PC[name]
            n = src.shape[1] // 128
            for l_ in range(L):
                P.dma("sp", self.PRM[:, l_, c + i:c + i + n], src[l_].rearrange("(c p) -> p c", p=128), chan="setup",
                      allow_slow_non_contiguous=True)
        ld("gpre", dr["norm_mix_pre"]); ld("gpost", dr["norm_mix_post"])
        ld("gfpre", dr["norm_ffn_pre"]); ld("gfpost", dr["norm_ffn_post"])
        ld("bq", dr["b_in"][:, 0:512])
        ld("brw", dr["b_in"][:, 768:1792]); ld("blx", dr["b_in"][:, 1792:2048]); ld("blg", dr["b_in"][:, 2048:2304])
        ld("mu", dr["rwkv_mu"]); ld("w0", dr["rwkv_w0"]); ld("a0", dr["rwkv_a0"]); ld("kk", dr["rwkv_k_k"])
        ld("ka", dr["rwkv_k_a"]); ld("rk", dr["rwkv_r_k"].rearrange("l h d -> l (h d)"))
        ld("lnw", dr["rwkv_ln_w"]); ld("lnb", dr["rwkv_ln_b"])
        for j in range(4):
            ld("cw", dr["lru_conv_w"][:, j, :], i=2 * j)
        ld("cb", dr["lru_conv_b"]); ld("ba", dr["lru_b_a"]); ld("bi", dr["lru_b_i"]); ld("Lp", dr["lru_L"])
        ld("bout", dr["b_out"])
        c, w = PC["bkd"]
        for g in range(2):
            for hp in range(2):
                for l_ in range(L):
                    P.dma("sp", self.PRM[hp * 64:(hp + 1) * 64, l_, c + g:c + g + 1],
                          dr["b_in"][l_, 512 + 64 * g:512 + 64 * g + 64].rearrange("(c p) -> p c", p=64), chan="setup",
                          allow_slow_non_contiguous=True)
        P.dma("sp", self.bkv.full(), dr["b_in"][:, 512:768].partition_broadcast(128), chan="setup")
        P.dma("sp", self.sinkb.full(), dr["attn_sinks"].partition_broadcast(128), chan="setup")
        P.dma("pool", self.wup[0:64, :, :], dr["rwkv_w_up"].rearrange("l i j -> i l j"), chan="setup2")
        P.dma("pool", self.aup[64:128, :, :], dr["rwkv_a_up"].rearrange("l i j -> i l j"), chan="setup2")
        P.dma("pool", self.gup.full(), dr["rwkv_g_up"].rearrange("l i j -> i l j"), chan="setup2")
        for par in range(2):
            for (dst, src) in ((self.wabd, dr["lru_w_a"]), (self.wibd, dr["lru_w_i"])):
                for cc in range(2):
                    P.dma("pool", dst[par * 64:(par + 1) * 64, :, cc, par * 64:(par + 1) * 64],
                          src[:, 2 * cc + par, :, :].rearrange("l i j -> i l j"), chan="setup2")
        for l in range(L):
            P.act(self.prm(l, "c8", 0, 2), self.prm(l, "Lp", 0, 2), AF.Sigmoid)
            P.act(self.prm(l, "c8", 0, 2), self.prm(l, "c8", 0, 2), AF.Ln)
            P.ts(self.prm(l, "c8", 0, 2), self.prm(l, "c8", 0, 2), 8.0, ALU.mult)
            P.ts(self.prm(l, "ka1", 0, 2), self.prm(l, "ka", 0, 2), -1.0, ALU.mult, 1.0, ALU.add)

    def wq_add(self, loader):
        self.wq.append(loader)
        return len(self.wq) - 1

    def wget(self, bid):
        lim = min(len(self.wq), bid + self.NSLOT)
        while self.wq_issued < lim:
            i = self.wq_issued
            self.wq[i](self.wslot[i % self.NSLOT], f"w{i % self.NSLOT}")
            self.wq_issued += 1
        return self.wslot[bid % self.NSLOT]

    def wload(self, slot, chan, src, nk, W, pieces):
        for (s0, n, d0) in pieces:
            dst_ap = slot.h[:, 0:nk * W].rearrange("p (k w) -> p k w", w=W)[:, :, d0:d0 + n]
            v = View(slot, dst_ap, 0, 128, 0, nk * W * 2)
            self.P.dma("pool", v, src[:, s0:s0 + n].rearrange("(k p) n -> p k n", p=128), chan=chan)

    def wview(self, slot, nk, W, k, c0, n, p0=0, p1=128):
        ap = slot.h[p0:p1, 0:nk * W].rearrange("p (k w) -> p k w", w=W)[:, k, c0:c0 + n]
        return View(slot, ap, p0, p1, (k * W + c0) * 2, (k * W + c0 + n) * 2)

    def load_segment(self, s):
        P, cfg, dr = self.P, self.cfg, self.dr
        TP, NB, NS = cfg.TP, cfg.NB, cfg.NS
        k = 0
        for b in range(NB + 1):
            st = self.stage[k % 2]
            k += 1
            if b < NB:
                n = 128
                src = dr["xp"][s * TP + b * 128: s * TP + (b + 1) * 128, :]
                c0 = b * 128
            else:
                n = NS
                src = dr["xs"][s * NS:(s + 1) * NS, :]
                c0 = TP
            P.dma("sp", st[0:n, :], src, chan=f"xin{(k - 1) % 2}")
            for half in range(2):
                pv = self.ps(128, 4 * n)
                for j in range(4):
                    c = half * 4 + j
                    P.transpose(pv[:, j * n:(j + 1) * n], st[0:n, c * 128:(c + 1) * 128], self.identf[0:n, 0:n])
                dst = self.xT[:, half * 4:half * 4 + 4, c0:c0 + n]
                src_v = pv.with_ap(pv.ap.rearrange("p (j n) -> p j n", j=4))
                if half == 0:
                    P.copy(dst, src_v, eng="act")
                else:
                    P.copy(dst, src_v, eng="dve")

    def store_segment(self, s):
        P, cfg, dr = self.P, self.cfg, self.dr
        TP, NB, NS = cfg.TP, cfg.NB, cfg.NS
        k = 0
        for b in range(NB + 1):
            st = self.stage[k % 2]
            k += 1
            if b < NB:
                n = 128
                dst = dr["yp"][s * TP + b * 128: s * TP + (b + 1) * 128, :]
                c0 = b * 128
            else:
                n = NS
                dst = dr["ys"][s * NS:(s + 1) * NS, :]
                c0 = TP
            for half in range(2):
                pv = self.ps(n, 512)
                for j in range(4):
                    c = half * 4 + j
                    P.transpose(pv[:, j * 128:(j + 1) * 128], self.xT[:, c, c0:c0 + n], self.identf.full())
                if half == 0:
                    P.copy(st[0:n, 0:512], pv, eng="act")
                else:
                    P.copy(st[0:n, 512:1024], pv, eng="dve")
            P.dma("sp", dst, st[0:n, :], chan=f"out{(k - 1) % 2}")

    def sumsq_rstd(self, src_fn, sq):
        P, cfg = self.P, self.cfg
        for c in range(8):
            P.act(sq[:, c, :], src_fn(c), AF.Square)
        for (c0, n) in cfg.blks:
            pv = self.ps(128, n)
            for c in range(8):
                P.mm(pv, self.ones.full(), sq[:, c, c0:c0 + n], start=(c == 0), stop=(c == 7))
            P.act(self.lntmp[:, 0:n], pv, AF.Ln, bias=self.epsc[:, 0:1], scale=1.0 / D)
            P.act(self.rstd[:, c0:c0 + n], self.lntmp[:, 0:n], AF.Exp, scale=-0.5)

    def dense(self, M, nk, lhsT_fn, rhs_fn, evac_fn, blks=None):
        P = self.P
        for (c0, n) in (blks or self.cfg.blks):
            pv = self.ps(M, n)
            for k in range(nk):
                P.mm(pv, lhsT_fn(k), rhs_fn(k, c0, n), start=(k == 0), stop=(k == nk - 1))
            evac_fn(pv, c0, n)


NBLK_PER_LAYER = 30


def rr_gen(gens):
    gens = list(gens)
    while gens:
        for g_ in list(gens):
            try:
                next(g_)
                yield
            except StopIteration:
                gens.remove(g_)


def run_rr(gens):
    gens = list(gens)
    while gens:
        for g_ in list(gens):
            try:
                next(g_)
            except StopIteration:
                gens.remove(g_)


class Kern(KernBase):
    ARENA_BYTES = 92 * 1024

    def register_layer(self, l):
        dr = self.dr
        w_in = dr["w_in"][l]
        ids = {}

        def reg(name, fn):
            ids[name] = self.wq_add(fn)
        reg("q", lambda sl, ch: self.wload(sl, ch, w_in, 8, 512, [(0, 512, 0)]))
        reg("kv", lambda sl, ch: self.wload(sl, ch, w_in, 8, 512, [(512, 64, 0), (512, 64, 64), (576, 64, 128), (576, 64, 192), (512, 256, 256)]))
        reg("lru", lambda sl, ch: self.wload(sl, ch, w_in, 8, 512, [(1792, 512, 0)]))
        reg("lora", lambda sl, ch: self.wload(sl, ch, w_in, 8, 256, [(768 + 768, 256, 0)]))
        for p in range(2):
            reg(f"pair{p}", lambda sl, ch, p=p: self.wload(sl, ch, w_in, 8, 384, [(768 + 128 * p, 128, 0), (768 + 256 + 128 * p, 128, 128), (768 + 512 + 128 * p, 128, 256)]))
        for j in range(2):
            reg(f"wout{j}", lambda sl, ch, j=j: self.wload(sl, ch, dr["w_out"][l], 8, 512, [(512 * j, 512, 0)]))
        for j in range(11):
            def f(sl, ch, j=j):
                self.wload(sl, ch, dr["ffn_w_gate"][l], 8, 512, [(256 * j, 256, 0)])
                self.wload(sl, ch, dr["ffn_w_up"][l], 8, 512, [(256 * j, 256, 256)])
            reg(f"gu{j}", f)
        for m in range(8):
            reg(f"down{m}", lambda sl, ch, m=m: self.wload(sl, ch, dr["ffn_w_down"][l], 22, 128, [(128 * m, 128, 0)]))
        reg("pw", lambda sl, ch: self.wload(sl, ch, dr["ple_w"][l], 2, 1024, [(0, 1024, 0)]))
        for j in range(2):
            reg(f"pg{j}", lambda sl, ch, j=j: self.wload(sl, ch, dr["ple_gate_w"][l], 8, 512, [(512 * j, 512, 0)]))
        return ids

    def layer(self, l, s, ids):
        P, cfg = self.P, self.cfg
        T, TP = cfg.T, cfg.TP
        A = self.arena
        m0 = A.mark()
        import os as _os
        STOP = int(_os.environ.get("KS_STOP", "99"))
        if STOP < 1:
            return
        sq = self.mixT
        self.sumsq_rstd(lambda c: self.xT[:, c, :], sq)
        for c in range(8):
            P.stt(self.hT[:, c, :], self.xT[:, c, :], self.prm(l, "gpre", c), self.rstd.full(), ALU.mult, ALU.mult)
        if STOP < 2:
            return
        self.attention(l, s, ids, side=self.lru_gen(l, s, ids))
        A.release(m0)
        self.rwkv(l, s, ids)
        A.release(m0)
        ybuf = A.alloc("ybuf", [128, 8, T], F32)
        for j in range(2):
            sl = self.wget(ids[f"wout{j}"])
            for mm_ in range(4):
                m = j * 4 + mm_
                self.dense(128, 8, lambda k: self.wview(sl, 8, 512, k, mm_ * 128, 128),
                           lambda k, c0, n: self.mixT[:, k, c0:c0 + n],
                           lambda pv, c0, n, m=m: P.act(ybuf[:, m, c0:c0 + n], pv, AF.Identity, bias=self.prm(l, "bout", m)))
        self.post_norm_add(l, ybuf, "gpost")
        if STOP < 6:
            A.release(m0)
            return
        self.sumsq_rstd(lambda c: self.xT[:, c, :], self.mixT)
        for c in range(8):
            P.stt(self.hT[:, c, :], self.xT[:, c, :], self.prm(l, "gfpre", c), self.rstd.full(), ALU.mult, ALU.mult)
        act = A.alloc("act", [128, 22, T], BF16)
        sg = A.alloc("sg", [128, 2, 512], F32)
        ei = 0
        for j in range(11):
            sl = self.wget(ids[f"gu{j}"])
            for jj in range(2):
                fc = 2 * j + jj
                for (c0, n) in cfg.blks:
                    pg = self.ps(128, n)
                    for k in range(8):
                        P.mm(pg, self.wview(sl, 8, 512, k, jj * 128, 128), self.hT[:, k, c0:c0 + n], start=(k == 0), stop=(k == 7))
                    pu = self.ps(128, n)
                    for k in range(8):
                        P.mm(pu, self.wview(sl, 8, 512, k, 256 + jj * 128, 128), self.hT[:, k, c0:c0 + n], start=(k == 0), stop=(k == 7))
                    sgt = sg[:, ei % 2, 0:n]
                    ei += 1
                    P.act(sgt, pg, AF.Silu)
                    P.tt(act[:, fc, c0:c0 + n], sgt, pu, ALU.mult)
        for m in range(8):
            sl = self.wget(ids[f"down{m}"])
            self.dense(128, 22, lambda k: self.wview(sl, 22, 128, k, 0, 128),
                       lambda k, c0, n: act[:, k, c0:c0 + n],
                       lambda pv, c0, n, m=m: P.copy(ybuf[:, m, c0:c0 + n], pv, eng="act"))
        self.post_norm_add(l, ybuf, "gfpost")
        A.release(m0)
        if STOP < 7:
            return
        for c in range(8):
            P.copy(self.hT[:, c, :], self.xT[:, c, :], eng=("act" if c % 2 else "dve"))
        pT = A.alloc("pT", [128, 2, T], BF16)
        self.load_ple(l, s, pT)
        pwb = A.alloc("pwb", [128, 8, T], F32)
        sgp = A.alloc("sgp", [128, 2, 512], F32)
        slw = self.wget(ids["pw"])
        for m in range(8):
            self.dense(128, 2, lambda k: self.wview(slw, 2, 1024, k, m * 128, 128), lambda k, c0, n: pT[:, k, c0:c0 + n],
                       lambda pv, c0, n, m=m: P.copy(pwb[:, m, c0:c0 + n], pv, eng="act"))
        ei = 0
        for j in range(2):
            sl = self.wget(ids[f"pg{j}"])
            for mm_ in range(4):
                m = j * 4 + mm_
                for (c0, n) in cfg.blks:
                    pg = self.ps(128, n)
                    for k in range(8):
                        P.mm(pg, self.wview(sl, 8, 512, k, mm_ * 128, 128), self.hT[:, k, c0:c0 + n], start=(k == 0), stop=(k == 7))
                    sgt = sgp[:, ei % 2, 0:n]
                    ei += 1
                    P.act(sgt, pg, AF.Sigmoid)
                    P.tt(sgt, sgt, pwb[:, m, c0:c0 + n], ALU.mult)
                    P.tt(self.xT[:, m, c0:c0 + n], self.xT[:, m, c0:c0 + n], sgt, ALU.add)
        A.release(m0)

    def post_norm_add(self, l, ybuf, gname):
        P, cfg = self.P, self.cfg
        self.sumsq_rstd(lambda c: ybuf[:, c, :], self.mixT)
        for c in range(8):
            P.stt(ybuf[:, c, :], ybuf[:, c, :], self.prm(l, gname, c), self.rstd.full(), ALU.mult, ALU.mult)
            P.tt(self.xT[:, c, :], self.xT[:, c, :], ybuf[:, c, :], ALU.add)

    def load_ple(self, l, s, pT):
        P, cfg, dr = self.P, self.cfg, self.dr
        TP, NB, NS = cfg.TP, cfg.NB, cfg.NS
        for b in range(NB + 1):
            st = self.stage[b % 2]
            if b < NB:
                n = 128
                src = dr["pp"][l, s * TP + b * 128: s * TP + (b + 1) * 128, :]
                c0 = b * 128
            else:
                n = NS
                src = dr["psm"][l, s * NS:(s + 1) * NS, :]
                c0 = TP
            P.dma("sp", st[0:n, 0:256], src, chan=f"xin{b % 2}")
            pv = self.ps(128, 2 * n)
            for j in range(2):
                P.transpose(pv[:, j * n:(j + 1) * n], st[0:n, j * 128:(j + 1) * 128], self.identf[0:n, 0:n])
            P.copy(pT[:, :, c0:c0 + n], pv.with_ap(pv.ap.rearrange("p (j n) -> p j n", j=2)), eng="act")

    def attention(self, l, s, ids, side=None):
        P, cfg, dr = self.P, self.cfg, self.dr
        T, TP, NB, SB, NS = cfg.T, cfg.TP, cfg.NB, cfg.SB, cfg.NS
        A = self.arena
        qT = A.alloc("qT", [128, 4, T], BF16)
        kdT = A.alloc("kdT", [128, 2, 128 + T], BF16)
        vpad = A.alloc("vpad", [128, NB + 1, 512], BF16)
        vpad_c = A.alloc("vpad_c", [128, SB, 512], BF16)
        vpad_s = A.alloc("vpad_s", [4, SB, 512], BF16)
        kcT = A.alloc("kcT", [128, SB, 2, 128], BF16)
        kvtok = A.alloc("kvtok", [128, 2, 256], F32)
        cst = A.alloc("cst", [128, SB, 2, 128], F32)
        cdup = A.alloc("cdup", [128, 2, 128], BF16)
        sc = A.alloc("sc", [128, 8, 256], F32)
        ee = A.alloc("ee", [128, 8, 256], F32)
        pps = [A.alloc(f"pp{i_}", [128, 8, 256], BF16) for i_ in range(2)]
        pTs = A.alloc("pTs", [128, 16, 128], BF16)
        sm = A.alloc("sm", [128, 6, 8], F32)
        P.memset(vpad.full(), 0.0, eng="pool")
        P.memset(vpad_c.full(), 0.0, eng="pool")
        P.memset(vpad_s.full(), 0.0, eng="pool")
        P.copy(kdT[:, :, 0:128], self.kcar[:, l, :, :], eng="pool")
        P.copy(vpad[:, 0, :], self.vcar[:, l, :], eng="pool")
        for b in range(SB):
            gb = s * SB + b
            P.dma("sp", cst[:, b, 0, :], dr["ck"][l, gb], chan="cache0")
            P.dma("sp", cst[:, b, 1, :], dr["cv"][l, gb], chan="cache1")
        for b in range(SB):
            for g in range(2):
                for var in range(2):
                    P.copy(vpad_c[:, b, g * 256 + var * 192: g * 256 + var * 192 + 64], cst[:, b, 1, g * 64:(g + 1) * 64], eng=("dve" if var else "pool"))
            for g in range(2):
                for hp in range(2):
                    P.copy(cdup[:, g, hp * 64:(hp + 1) * 64], cst[:, b, 0, g * 64:(g + 1) * 64], eng=("dve" if hp else "pool"))
            pb = self.ps(128, 128).bitcast(BF16)
            for g in range(2):
                P.transpose(pb[:, g * 128:(g + 1) * 128], cdup[:, g, :], self.ident.full())
            P.copy(kcT[:, b, :, :], pb.with_ap(pb.ap.rearrange("p (g n) -> p g n", g=2)), eng="act")
        sl = self.wget(ids["q"])
        for m in range(4):
            self.dense(128, 8, lambda k: self.wview(sl, 8, 512, k, m * 128, 128),
                       lambda k, c0, n: self.hT[:, k, c0:c0 + n],
                       lambda pv, c0, n, m=m: P.act(qT[:, m, c0:c0 + n], pv, AF.Identity, bias=self.prm(l, "bq", m)))
        sl = self.wget(ids["kv"])
        for g in range(2):
            self.dense(128, 8, lambda k: self.wview(sl, 8, 512, k, g * 128, 128),
                       lambda k, c0, n: self.hT[:, k, c0:c0 + n],
                       lambda pv, c0, n, g=g: P.act(kdT[:, g, 128 + c0:128 + c0 + n], pv, AF.Identity, bias=self.prm(l, "bkd", g)))

        def vscatter(dst_fn, src, n):
            for g in range(2):
                for var in range(2):
                    P.copy(dst_fn(g, var), src[0:n, 128 + g * 64:128 + (g + 1) * 64], eng=("dve" if var else "pool"))

        last_seg = (s == cfg.NSEG - 1)
        for b in range(NB):
            pv = self.ps(128, 256)
            for k in range(8):
                P.mm(pv, self.hT[:, k, b * 128:(b + 1) * 128], self.wview(sl, 8, 512, k, 256, 256), start=(k == 0), stop=(k == 7))
            kt = kvtok[:, b % 2, :]
            P.tt(kt, pv, self.bkv[:, l, :], ALU.add)
            vscatter(lambda g, var: vpad[:, b + 1, g * 256 + var * 192: g * 256 + var * 192 + 64], kt, 128)
            if last_seg and b == NB - 1:
                P.dma("sp", dr["kp"][l], kt[:, 0:128], chan="outs")
                P.dma("sp", dr["vp"][l], kt[:, 128:256], chan="outs")
        for b in range(SB):
            gb = s * SB + b
            pv = self.ps(4, 256)
            for k in range(8):
                P.mm(pv, self.hT[:, k, TP + 4 * b:TP + 4 * b + 4], self.wview(sl, 8, 512, k, 256, 256), start=(k == 0), stop=(k == 7))
            kt = kvtok[0:4, b % 2, :]
            P.tt(kt, pv, self.bkv[0:4, l, :], ALU.add)
            vscatter(lambda g, var: vpad_s[0:4, b, g * 256 + var * 192: g * 256 + var * 192 + 64], kt, 4)
            P.dma("sp", dr["ks"][l, gb, 124:128, :], kt[:, 0:128], chan="outs")
            P.dma("sp", dr["vs"][l, gb, 124:128, :], kt[:, 128:256], chan="outs")
        if s == 0:
            P.dma("sp", dr["ks"][l, :, 0:124, :], dr["ck"][l, :, 4:128, :], chan="outs")
            P.dma("sp", dr["vs"][l, :, 0:124, :], dr["cv"][l, :, 4:128, :], chan="outs")

        def attn_block(M, qc0, kviews, vviews, mask, wk, par):
            c_s = self.bank(4)
            pp = pps[par]

            def hcol(h):
                hp_, cc = h % 2, h // 2
                return (hp_ * 2 + cc // 2) * 512 + (cc % 2) * 256
            for h in range(8):
                c, hp, g = h // 2, h % 2, h // 4
                off = 0
                for (kf, nk) in kviews:
                    P.mm(self.PS[0:M, c_s + hcol(h) + off: c_s + hcol(h) + off + nk],
                         qT[hp * 64:(hp + 1) * 64, c, qc0:qc0 + M], kf(g, hp))
                    off += nk
            scv = sc[0:M, :, 0:wk]
            mk = mask[0:M, 0:wk]
            for h in range(8):
                P.stt(sc[0:M, h, 0:wk], self.PS[0:M, c_s + hcol(h): c_s + hcol(h) + wk], 0.125, mk, ALU.mult, ALU.add)
            mx = sm[0:M, 0, :]
            P.add("dve", lambda e: e.tensor_reduce(mx.ap, scv.ap, AX.X, ALU.max), [scv], [mx])
            P.tt(mx, mx, self.sinkb[0:M, l, :], ALU.max)
            nm = sm[0:M, 1, :]
            P.ts(nm, mx, -1.0, ALU.mult)
            rs = sm[0:M, 2, :]
            for h in range(8):
                P.act(ee[0:M, h, 0:wk], sc[0:M, h, 0:wk], AF.Exp, bias=sm[0:M, 1, h:h + 1], accum=sm[0:M, 2, h:h + 1])
            es = sm[0:M, 3, :]
            P.tt(es, self.sinkb[0:M, l, :], nm, ALU.add)
            P.act(es, es, AF.Exp)
            P.tt(es, es, rs, ALU.add)
            rd = sm[0:M, 4, :]
            P.add("dve", lambda e: e.reciprocal(rd.ap, es.ap), [es], [rd])
            for h in range(8):
                if h % 2:
                    P.ts(pp[0:M, h, 0:wk], ee[0:M, h, 0:wk], sm[0:M, 4, h:h + 1], ALU.mult)
                else:
                    P.act(pp[0:M, h, 0:wk], ee[0:M, h, 0:wk], AF.Copy, scale=sm[0:M, 4, h:h + 1])

            def phase_b():
                c_t = self.bank(2)
                pTp = self.PS[:, c_t:c_t + 1024].bitcast(BF16)
                nkb = len(kviews)
                for h in range(8):
                    off = 0
                    for kb, (kf, nk) in enumerate(kviews):
                        P.transpose(pTp[0:nk, (h * 2 + kb) * 128:(h * 2 + kb) * 128 + M], pp[0:M, h, off:off + nk], self.ident[0:M, 0:M])
                        off += nk
                for kb, (kf, nk) in enumerate(kviews):
                    for hb in range(2):
                        src = pTp[0:nk, hb * 1024:(hb + 1) * 1024]
                        src = src.with_ap(src.ap.rearrange("p (h kb m) -> p h kb m", h=4, kb=2)[:, :, kb, 0:M])
                        d0 = pTs[0:nk, hb * 8:(hb + 1) * 8, :]
                        dst = d0.with_ap(d0.ap.rearrange("p (h kb) m -> p h kb m", kb=2)[:, :, kb, 0:M])
                        P.copy(dst, src, eng=("act" if hb == 0 else "dve"))
                c_o = self.bank(1)
                for c in range(4):
                    po = self.PS[:, c_o + c * M: c_o + (c + 1) * M]
                    first = True
                    for hh in range(2):
                        h = 2 * c + hh
                        g = h // 4
                        for kb, (kf, nk) in enumerate(kviews):
                            lastmm = (hh == 1 and kb == nkb - 1)
                            P.mm(po, vviews[kb](g, hh, nk), pTs[0:nk, h * 2 + kb, 0:M], start=first, stop=lastmm)
                            first = False
                pov = self.PS[:, c_o:c_o + 4 * M]
                P.copy(self.mixT[:, 0:4, qc0:qc0 + M], pov.with_ap(pov.ap.rearrange("p (c m) -> p c m", c=4)), eng="act")
            return phase_b

        jobs = []
        for i in range(NB):
            mask = self.maskf if (s == 0 and i == 0) else self.maskb
            jobs.append((128, i * 128,
                         [(lambda g, hp, i=i: kdT[hp * 64:(hp + 1) * 64, g, i * 128:i * 128 + 128], 128),
                          (lambda g, hp, i=i: kdT[hp * 64:(hp + 1) * 64, g, (i + 1) * 128:(i + 1) * 128 + 128], 128)],
                         [lambda g, hh, nk, i=i: vpad[0:nk, i, g * 256 + hh * 128: g * 256 + hh * 128 + 128],
                          lambda g, hh, nk, i=i: vpad[0:nk, i + 1, g * 256 + hh * 128: g * 256 + hh * 128 + 128]],
                         mask, 256))
        for b in range(SB):
            jobs.append((4, TP + 4 * b,
                         [(lambda g, hp, b=b: kcT[hp * 64:(hp + 1) * 64, b, g, :], 128),
                          (lambda g, hp, b=b: kdT[hp * 64:(hp + 1) * 64, g, 128 + TP + 4 * b:128 + TP + 4 * b + 4], 4)],
                         [lambda g, hh, nk, b=b: vpad_c[0:nk, b, g * 256 + hh * 128: g * 256 + hh * 128 + 128],
                          lambda g, hh, nk, b=b: vpad_s[0:nk, b, g * 256 + hh * 128: g * 256 + hh * 128 + 128]],
                         self.maskb, 132))
        def side_steps(k):
            if side is not None:
                for _ in range(k):
                    if next(side, "done") == "done":
                        break
        pend = None
        for ji, job in enumerate(jobs):
            fin = attn_block(*job, ji % 2)
            side_steps(7)
            if pend is not None:
                pend()
                side_steps(7)
            pend = fin
        pend()
        side_steps(100000)
        P.copy(self.kcar[:, l, :, :], kdT[:, :, TP:TP + 128], eng="pool")
        P.copy(self.vcar[:, l, :], vpad[:, NB, :], eng="pool")

    def rwkv(self, l, s, ids):
        P, cfg = self.P, self.cfg
        self.wget(ids["lora"]); self.wget(ids["pair0"]); self.wget(ids["pair1"])
        P.memset(self.mixT[:, 4:6, :], 0.0, eng="pool")

    def lru_gen(self, l, s, ids):
        P, cfg, dr = self.P, self.cfg, self.dr
        T, TP, SB, NS = cfg.T, cfg.TP, cfg.SB, cfg.NS
        A = self.arena
        sl = self.wget(ids["lru"])
        xe_p = A.alloc("xe_p", [128, 2, 3 + TP], F32)
        xe_s = A.alloc("xe_s", [128, 2, SB, 7], F32)
        gb = A.alloc("gb", [128, 2, T], F32)
        xc = A.alloc("xc", [128, 2, T], F32)
        xcb = A.alloc("xcb", [128, 2, T], BF16)
        rg = A.alloc("rg", [128, 2, T], F32)
        ig = A.alloc("ig", [128, 2, T], F32)
        aa = A.alloc("aa", [128, 2, T], F32)
        uu = A.alloc("uu", [128, 2, T], F32)
        hh_ = A.alloc("hh", [128, 2, T], F32)
        gl = rg
        h0s = A.alloc("h0s", [128, 2, SB], F32)
        cvo = A.alloc("cvo", [128, 2, SB, 3], F32)
        for c in range(2):
            P.copy(xe_p[:, c, 0:3], self.cvcar[:, l, c, :], eng="pool")
        for b in range(SB):
            gbi = s * SB + b
            for c in range(2):
                P.dma("sp", xe_s[:, c, b, 0:3], dr["st_conv"][l, gbi, :, c * 128:(c + 1) * 128].rearrange("j p -> p j"),
                      chan="st", allow_slow_non_contiguous=True)
        for c in range(2):
            P.dma("sp", h0s[:, c, :], dr["st_lru"][l, s * SB:(s + 1) * SB, c * 128:(c + 1) * 128].rearrange("b p -> p b"), chan="st",
                  allow_slow_non_contiguous=True)
        for c in range(2):
            def ev_x(pv, c0, n, c=c):
                npp = max(0, min(c0 + n, TP) - c0)
                if npp > 0:
                    P.act(xe_p[:, c, 3 + c0:3 + c0 + npp], pv[:, 0:npp], AF.Identity, bias=self.prm(l, "blx", c))
                if npp < n:
                    lo = c0 + npp - TP
                    assert lo % 4 == 0 and (n - npp) % 4 == 0
                    ps_ = pv[:, npp:n]
                    P.act(xe_s[:, c, lo // 4:(lo + n - npp) // 4, 3:7], ps_.with_ap(ps_.ap.rearrange("p (b t) -> p b t", t=4)), AF.Identity,
                          bias=self.prm(l, "blx", c))
            self.dense(128, 8, lambda k: self.wview(sl, 8, 512, k, c * 128, 128), lambda k, c0, n: self.hT[:, k, c0:c0 + n], ev_x)
            yield
            self.dense(128, 8, lambda k: self.wview(sl, 8, 512, k, 256 + c * 128, 128), lambda k, c0, n: self.hT[:, k, c0:c0 + n],
                       lambda pv, c0, n, c=c: P.act(gb[:, c, c0:c0 + n], pv, AF.Identity, bias=self.prm(l, "blg", c)))
            yield
        def conv_chain(c, dst, ext):
            P.ts(dst, ext(0), self.prm(l, "cw", 0 + c), ALU.mult, self.prm(l, "cb", c), ALU.add)
            yield
            for j in range(1, 4):
                P.stt(dst, ext(j), self.prm(l, "cw", 2 * j + c), dst, ALU.mult, ALU.add)
                yield
        gens = []
        for c in range(2):
            gens.append(conv_chain(c, xc[:, c, 0:TP], lambda j, c=c: xe_p[:, c, j:j + TP]))
            xs_ = xc[:, c, TP:T]
            gens.append(conv_chain(c, xs_.with_ap(xs_.ap.rearrange("p (b t) -> p b t", t=4)), lambda j, c=c: xe_s[:, c, :, j:j + 4]))
        yield from rr_gen(gens)
        for c in range(2):
            P.copy(self.cvcar[:, l, c, :], xe_p[:, c, TP:TP + 3], eng="pool")
            P.copy(cvo[:, c, :, :], xe_s[:, c, :, 4:7], eng="pool")
            P.copy(xcb[:, c, :], xc[:, c, :], eng="act")
        for c in range(2):
            self.dense(128, 1, lambda k: self.wabd[:, l, c, :], lambda k, c0, n: xcb[:, c, c0:c0 + n],
                       lambda pv, c0, n, c=c: P.act(rg[:, c, c0:c0 + n], pv, AF.Sigmoid, bias=self.prm(l, "ba", c)))
            yield
            self.dense(128, 1, lambda k: self.wibd[:, l, c, :], lambda k, c0, n: xcb[:, c, c0:c0 + n],
                       lambda pv, c0, n, c=c: P.act(ig[:, c, c0:c0 + n], pv, AF.Sigmoid, bias=self.prm(l, "bi", c)))
            yield
        def main_chain(c):
            a_ = aa[:, c, :]
            u_ = uu[:, c, :]
            P.act(a_, rg[:, c, :], AF.Exp, scale=self.prm(l, "c8", c)); yield
            P.tt(u_, a_, a_, ALU.mult); yield
            P.ts(u_, u_, -1.0, ALU.mult, 1.0, ALU.add); yield
            P.act(u_, u_, AF.Sqrt); yield
            P.tt(ig[:, c, :], ig[:, c, :], xc[:, c, :], ALU.mult); yield
            P.tt(u_, u_, ig[:, c, :], ALU.mult); yield
            us = uu[:, c, TP:T]
            us3 = us.with_ap(us.ap.rearrange("p (b t) -> p b t", t=4)[:, :, 0])
            as_ = aa[:, c, TP:T]
            as3 = as_.with_ap(as_.ap.rearrange("p (b t) -> p b t", t=4)[:, :, 0])
            tmp = h0s[:, c, :]
            P.tt(tmp, tmp, as3, ALU.mult); yield
            P.tt(us3, us3, tmp, ALU.add); yield
            P.memset(as3, 0.0); yield
            hv = hh_[:, c, :]
            ini = self.hcar[:, l, c:c + 1]
            P.add("dve", lambda e, hv=hv, a_=a_, u_=u_, ini=ini: e.tensor_tensor_scan(hv.ap, a_.ap, u_.ap, ini.ap, ALU.mult, ALU.add),
                  [a_, u_, ini], [hv]); yield
            P.copy(self.hcar[:, l, c:c + 1], hh_[:, c, TP - 1:TP], eng="pool"); yield

        def gelu_chain(c):
            g_ = gb[:, c, :]
            t1 = gl[:, c, :]
            P.act(t1, g_, AF.Square); yield
            P.ts(t1, t1, 0.044715, ALU.mult, 1.0, ALU.add); yield
            P.tt(t1, t1, g_, ALU.mult); yield
            P.act(t1, t1, AF.Sigmoid, scale=1.5957691216057308); yield
            P.tt(t1, t1, g_, ALU.mult); yield
        yield from rr_gen([main_chain(0), gelu_chain(0), main_chain(1), gelu_chain(1)])
        for c in range(2):
            P.tt(self.mixT[:, 6 + c, :], gl[:, c, :], hh_[:, c, :], ALU.mult)
        for b in range(SB):
            gbi = s * SB + b
            for c in range(2):
                P.dma("sp", dr["convs"][l, gbi, :, c * 128:(c + 1) * 128].rearrange("j p -> p j"), cvo[:, c, b, :], chan="outs",
                      allow_slow_non_contiguous=True)
        hs = hh_[:, :, TP:T]
        hs_last = hs.with_ap(hs.ap.rearrange("p c (b t) -> p c b t", t=4)[:, :, :, 3])
        P.copy(h0s.full(), hs_last, eng="pool")
        for c in range(2):
            P.dma("sp", dr["lrus"][l, s * SB:(s + 1) * SB, c * 128:(c + 1) * 128].rearrange("b p -> p b"), h0s[:, c, :], chan="outs",
                  allow_slow_non_contiguous=True)
        if s == cfg.NSEG - 1:
            for c in range(2):
                P.dma("sp", dr["convp"][l, :, c * 128:(c + 1) * 128].rearrange("j p -> p j"), self.cvcar[:, l, c, :], chan="outs",
                      allow_slow_non_contiguous=True)
            P.dma("sp", dr["lrup"][l].rearrange("(c p) -> p c", p=128), self.hcar[:, l, :], chan="outs", allow_slow_non_contiguous=True)


NG = 5
CW = 64
DEC_C = -math.exp(-0.5)


class Kern(Kern):
    def setup(self):
        super().setup()
        P = self.P
        A = self.arena
        self.trimx = A.alloc("trimx", [64, 2, 2, 2, 64], F32)
        self.trilox = A.alloc("trilox", [64, NG, 64], F32)
        self.identx = A.alloc("identx", [64, NG, 2, 64], BF16)
        for a in range(2):
            for w in range(2):
                P.copy(self.trimx[:, a, w, :, :], self.trim.full(), eng="pool")
        for i in range(NG):
            P.copy(self.trilox[:, i, :], self.trilo.full(), eng="pool")
            for hh in range(2):
                P.copy(self.identx[:, i, hh, :], self.ident[0:64, 0:64], eng="pool")

    def rwkv(self, l, s, ids):
        P, cfg, dr = self.P, self.cfg, self.dr
        T, TP, SB, NS, NCH = cfg.T, cfg.TP, cfg.SB, cfg.NS, cfg.NCH
        assert NCH % 2 == 0
        NU = NCH + SB
        TW = NU * CW
        A = self.arena
        last_seg = (s == cfg.NSEG - 1)
        tblks = []
        c_ = 0
        while c_ < TW:
            n_ = min(512, TW - c_)
            tblks.append((c_, n_))
            c_ += n_

        def v3(view, inner):
            return view.with_ap(view.ap.rearrange("p (u c) -> p u c", c=inner))

        pE_ = [A.alloc(f"pE{i_}", [128, 1 + TP], F32) for i_ in range(2)]
        pEs_ = [A.alloc(f"pEs{i_}", [128, SB, 5], F32) for i_ in range(2)]
        pcnt = [0]
        shs = A.alloc("shs", [128, 8, SB], F32)
        sho = A.alloc("sho", [128, 8, SB], F32)
        dtmp_ = [A.alloc(f"dtmp{i_}", [128, TP], F32) for i_ in range(2)]
        dtmps_ = [A.alloc(f"dtmps{i_}", [128, SB, 4], F32) for i_ in range(2)]
        rmask = A.alloc("rmask", [128, TW], F32)
        tw = A.alloc("tw", [128, TW], BF16)
        sgb = A.alloc("sgb", [128, TW], BF16)
        P.memset(rmask.full(), 1.0, eng="pool")
        P.memset(v3(rmask.full(), CW)[:, :, 0:1], 0.0, eng="pool")
        for c in range(8):
            P.dma("sp", shs[:, c, :], dr["st_shift"][l, s * SB:(s + 1) * SB, c * 128:(c + 1) * 128].rearrange("b p -> p b"),
                  chan="st", allow_slow_non_contiguous=True)

        def proj_chunk(c, lhs_fn, xs):
            pb_ = pcnt[0] % 2
            pcnt[0] += 1
            pE, pEs, dtmp, dtmps = pE_[pb_], pEs_[pb_], dtmp_[pb_], dtmps_[pb_]
            P.memset(xs[:, TP:TW], 0.0, eng="pool")
            P.copy(pE[:, 0:1], self.shcar[:, l, c:c + 1], eng="pool")
            P.copy(pEs[:, :, 0], shs[:, c, :], eng="pool")

            def ev(pv, c0, n):
                npp = max(0, min(c0 + n, TP) - c0)
                if npp > 0:
                    P.act(pE[:, 1 + c0:1 + c0 + npp], pv[:, 0:npp], AF.Identity, bias=self.prm(l, "brw", c))
                if npp < n:
                    lo = c0 + npp - TP
                    assert lo % 4 == 0 and (n - npp) % 4 == 0
                    ps_ = pv[:, npp:n]
                    P.act(pEs[:, lo // 4:(lo + n - npp) // 4, 1:5], ps_.with_ap(ps_.ap.rearrange("p (b t) -> p b t", t=4)), AF.Identity,
                          bias=self.prm(l, "brw", c))
            self.dense(128, 8, lhs_fn, lambda k, c0, n: self.hT[:, k, c0:c0 + n], ev)
            P.copy(self.shcar[:, l, c:c + 1], pE[:, TP:TP + 1], eng="pool")
            P.copy(sho[:, c, :], pEs[:, :, 4], eng="pool")
            mu = self.prm(l, "mu", c)
            P.tt(dtmp.full(), pE[:, 0:TP], pE[:, 1:TP + 1], ALU.subtract, eng="pool")
            P.stt(xs[:, 0:TP], dtmp.full(), mu, pE[:, 1:TP + 1], ALU.mult, ALU.add)
            P.tt(dtmps.full(), pEs[:, :, 0:4], pEs[:, :, 1:5], ALU.subtract, eng="pool")
            xs_s = v3(xs[:, TP:TW], CW)[:, :, 0:4]
            P.stt(xs_s, dtmps.full(), mu, pEs[:, :, 1:5], ALU.mult, ALU.add)

        m1 = A.mark()
        xs6 = A.alloc("xs6", [128, TW], F32)
        xs7 = A.alloc("xs7", [128, TW], F32)
        sl = self.wget(ids["lora"])
        proj_chunk(6, lambda k: self.wview(sl, 8, 256, k, 0, 128), xs6)
        proj_chunk(7, lambda k: self.wview(sl, 8, 256, k, 128, 128), xs7)
        P.act(tw[0:64, :], xs6[0:64, :], AF.Tanh)
        P.copy(tw[64:128, :], xs6[64:128, :])
        P.act(sgb.full(), xs7.full(), AF.Sigmoid)
        A.release(m1)

        for p in range(2):
            m1 = A.mark()
            sl = self.wget(ids[f"pair{p}"])
            f32t = lambda nm: A.alloc(nm, [128, TW], F32)
            b16t = lambda nm: A.alloc(nm, [128, TW], BF16)
            xr, xk, xv = f32t("xr"), f32t("xk"), f32t("xv")
            lw, aa, gg = f32t("lw"), f32t("aa"), f32t("gg")
            kk, k2, bn, cs, S1, S2 = f32t("kk"), f32t("k2"), f32t("bn"), f32t("cs"), f32t("S1"), f32t("S2")
            KR = A.alloc("KR", [128, NU, 2, CW], BF16)
            kt, bt, kG, bG, vb, tb = b16t("kt"), b16t("bt"), b16t("kG"), b16t("bG"), b16t("vb"), b16t("tb")
            gc = A.alloc("gc", [128, NU], F32)
            TOK = A.alloc("TOK", [64, NG, 3, 2, 128], BF16)
            MT = A.alloc("MT", [64, NG, 2, 2, 2, CW], BF16)
            ML = A.alloc("ML", [64, NG, 2, CW], BF16)
            AV = A.alloc("AV", [64, NG, 128], F32)
            NM = [A.alloc(f"NM{j}", [64, NG, 2, 2, CW], BF16) for j in range(2)]
            XX = [A.alloc(f"XX{j}", [64, NG, 2, CW], BF16) for j in range(2)]
            Wsb = A.alloc("Wsb", [64, 2 + SB, 128], BF16)
            Upad = A.alloc("Upad", [64, 2 + SB, 2, 128], BF16)
            Hs = A.alloc("Hs", [128, SB, 2, 128], F32)
            HT = A.alloc("HT", [128, 128], F32)
            ST = A.alloc("ST", [64, SB, 2, 128], F32)
            P.memset(TOK.full(), 0.0, eng="pool")
            P.memset(Upad.full(), 0.0, eng="pool")
            P.memset(ST.full(), 0.0, eng="pool")
            for b in range(SB):
                gb_ = s * SB + b
                for hh in range(2):
                    P.dma("sp", ST[:, b, hh, hh * 64:(hh + 1) * 64], dr["st_wkv"][l, gb_, 2 * p + hh], chan="st")
            for b in range(SB):
                ph = self.ps(128, 128)
                for hh in range(2):
                    P.transpose(ph[:, hh * 64:(hh + 1) * 64], ST[:, b, hh, :], self.identf[0:64, 0:64])
                P.copy(Hs[:, b, 0, :], ph)

            proj_chunk(p, lambda k: self.wview(sl, 8, 384, k, 0, 128), xr)
            proj_chunk(2 + p, lambda k: self.wview(sl, 8, 384, k, 128, 128), xk)
            proj_chunk(4 + p, lambda k: self.wview(sl, 8, 384, k, 256, 128), xv)
            def prep(ua, ub):
                ca, cb = ua * CW, ub * CW
                C = slice(ca, cb)
                n = cb - ca
                assert n <= 512
                pv = self.ps(128, n)
                P.mm(pv, self.wup[0:64, l, p * 128:(p + 1) * 128], tw[0:64, C])
                P.act(lw[:, C], pv, AF.Sigmoid, bias=self.prm(l, "w0", p)); yield
                pv = self.ps(128, n)
                P.mm(pv, self.aup[64:128, l, p * 128:(p + 1) * 128], tw[64:128, C])
                P.act(aa[:, C], pv, AF.Sigmoid, bias=self.prm(l, "a0", p)); yield
                pv = self.ps(128, n)
                P.mm(pv, self.gup[:, l, p * 128:(p + 1) * 128], sgb[:, C])
                P.copy(gg[:, C], pv, eng="act"); yield
                P.ts(lw[:, C], lw[:, C], DEC_C, ALU.mult); yield
                if ub > NCH:
                    cs0 = max(ua, NCH) * CW
                    P.memset(v3(lw[:, cs0:cb], CW)[:, :, 4:CW], 0.0, eng="pool"); yield
                P.ts(kk[:, C], xk[:, C], self.prm(l, "kk", p), ALU.mult); yield
                P.act(tb[:, C], kk[:, C], AF.Square); yield
                pv = self.ps(128, n)
                P.mm(pv, self.onesbd.full(), tb[:, C])
                P.ts(S1[:, C], pv, 1e-18, ALU.max); yield
                P.act(S1[:, C], S1[:, C], AF.Ln); yield
                P.act(S1[:, C], S1[:, C], AF.Exp, scale=-0.5); yield
                P.tt(kk[:, C], kk[:, C], S1[:, C], ALU.mult); yield
                P.ts(S2[:, C], aa[:, C], self.prm(l, "ka", p), ALU.mult, self.prm(l, "ka1", p), ALU.add); yield
                P.tt(k2[:, C], xk[:, C], S2[:, C], ALU.mult); yield
                P.stt(bn[:, C], kk[:, C], -1.0, aa[:, C], ALU.mult, ALU.mult); yield
                P.stt(tb[:, C], xr[:, C], self.prm(l, "rk", p), k2[:, C], ALU.mult, ALU.mult); yield
                pv = self.ps(128, n)
                P.mm(pv, self.onesbd.full(), tb[:, C])
                P.tt(xk[:, C], pv, xv[:, C], ALU.mult); yield
                csv, lwv, rmv = cs[:, C], lw[:, C], rmask[:, C]
                P.add("dve", lambda e, csv=csv, lwv=lwv, rmv=rmv: e.tensor_tensor_scan(csv.ap, rmv.ap, lwv.ap, 0.0, ALU.mult, ALU.add),
                      [rmv, lwv], [csv]); yield
                P.act(S1[:, C], cs[:, C], AF.Exp); yield
                P.copy(gc[:, ua:ub], v3(S1[:, C], CW)[:, :, CW - 1], eng="pool"); yield
                P.tt(xr[:, C], xr[:, C], S1[:, C], ALU.mult); yield
                P.copy(KR[:, ua:ub, 1, :], v3(xr[:, C], CW), eng="act"); yield
                P.tt(S2[:, C], cs[:, C], lw[:, C], ALU.subtract); yield
                P.act(S2[:, C], S2[:, C], AF.Exp); yield
                P.tt(kk[:, C], kk[:, C], S2[:, C], ALU.mult); yield
                P.copy(KR[:, ua:ub, 0, :], v3(kk[:, C], CW), eng="act"); yield
                P.act(S1[:, C], cs[:, C], AF.Exp, scale=-1.0); yield
                P.tt(kt[:, C], k2[:, C], S1[:, C], ALU.mult); yield
                P.tt(bt[:, C], bn[:, C], S1[:, C], ALU.mult); yield
                for u in range(ua, ub):
                    P.act(S2[:, u * CW:(u + 1) * CW], cs[:, u * CW:(u + 1) * CW], AF.Exp, bias=cs[:, u * CW + CW - 1:u * CW + CW], scale=-1.0)
                yield
                P.tt(kG[:, C], k2[:, C], S2[:, C], ALU.mult); yield
                P.tt(bG[:, C], bn[:, C], S2[:, C], ALU.mult); yield
                P.copy(vb[:, C], xv[:, C], eng="act"); yield
            bonus = xk
            nstr = 2 if NU >= 4 else 1
            cuts = [round(i_ * NU / nstr) for i_ in range(nstr + 1)]
            run_rr([prep(cuts[i_], cuts[i_ + 1]) for i_ in range(nstr)])
            yT = lw

            for g0 in range(0, NU, NG):
                ng = min(NG, NU - g0)
                for i in range(ng):
                    u = g0 + i
                    cu = slice(u * CW, (u + 1) * CW)
                    pb = self.ps(64, 192).bitcast(BF16)
                    for w_, src in enumerate((vb, kG, bG)):
                        P.transpose(pb[:, w_ * 128:(w_ + 1) * 128], src[:, cu], self.ident.full())
                    tv = TOK[:, i, :, :, :]
                    flat = TOK.base[0:64, i, :, :, :]
                    dst_ap = flat.rearrange("p w h x -> p (w h x)").rearrange("p (w a b) -> p w a b", w=3, a=4, b=64)[:, :, 0:4:3, :]
                    P.copy(tv.with_ap(dst_ap), pb.with_ap(pb.ap.rearrange("p (w h b) -> p w h b", w=3, h=2)), eng="act")
                for i0 in range(0, ng, 2):
                    n2 = min(2, ng - i0)
                    pbank = [self.ps(64, 512), self.ps(64, 512)]
                    for ii in range(n2):
                        i = i0 + ii
                        u = g0 + i
                        cu = slice(u * CW, (u + 1) * CW)
                        for hh in range(2):
                            hs = slice(hh * 64, (hh + 1) * 64)
                            P.mm(pbank[hh][:, ii * 256:ii * 256 + 128], bt[hs, cu], KR[hs, u, :, :])
                            P.mm(pbank[hh][:, ii * 256 + 128:ii * 256 + 256], kt[hs, cu], KR[hs, u, :, :])
                    for hh in range(2):
                        dst = MT[:, i0:i0 + n2, hh, :, :, :]
                        srcv = pbank[hh][:, 0:n2 * 256].with_ap(pbank[hh][:, 0:n2 * 256].ap.rearrange("p (i w a t) -> p i w a t", i=n2, w=2, a=2))
                        if hh == 0:
                            P.tt(dst, srcv, self.trimx[:, 0:n2, :, :, :], ALU.mult)
                        else:
                            P.copy(dst, srcv, eng="act")
                            pat = [[0, n2], [0, 2], [1, 2], [1, 64]]
                            P.add("pool", lambda e, dst=dst, pat=pat: e.affine_select(dst.ap, dst.ap, pat, ALU.is_ge, 0.0, base=-1, channel_multiplier=-1),
                                  [dst], [dst], name="asel")
                pm = [self.ps(64, ng * 64), self.ps(64, ng * 64)]
                for i in range(ng):
                    u = g0 + i
                    cu = slice(u * CW, (u + 1) * CW)
                    for hh in range(2):
                        hs = slice(hh * 64, (hh + 1) * 64)
                        P.mm(pm[hh][:, i * 64:(i + 1) * 64], KR[hs, u, 0, :], bt[hs, cu])
                for hh in range(2):
                    P.tt(ML[:, 0:ng, hh, :], pm[hh].with_ap(pm[hh].ap.rearrange("p (i s) -> p i s", i=ng)), self.trilox[:, 0:ng, :], ALU.mult)
                for i0 in range(0, ng, 4):
                    n4 = min(4, ng - i0)
                    pv_ = self.ps(64, n4 * 128)
                    for ii in range(n4):
                        i = i0 + ii
                        for hh in range(2):
                            P.mm(pv_[:, ii * 128 + hh * 64:ii * 128 + hh * 64 + 64], MT[:, i, hh, 1, 0, :], TOK[:, i, 0, hh, hh * 64:(hh + 1) * 64])
                    P.copy(AV[:, i0:i0 + n4, :], pv_.with_ap(pv_.ap.rearrange("p (i x) -> p i x", i=n4)), eng="act")
                Nv = lambda j, i, hh: (MT[:, i, hh, 0, 0, :] if j == 0 else NM[j % 2][:, i, hh, 0, :])
                Mv = lambda j, i, hh: (ML[:, i, hh, :] if j == 0 else NM[j % 2][:, i, hh, 1, :])
                P.tt(XX[0][:, 0:ng, :, :], MT[:, 0:ng, :, 0, 0, :], self.identx[:, 0:ng, :, :], ALU.add)
                for j in range(1, 6):
                    for i0 in range(0, ng, 2):
                        n2 = min(2, ng - i0)
                        pn = self.ps(64, n2 * 256)
                        for ii in range(n2):
                            i = i0 + ii
                            for hh in range(2):
                                o = ii * 256 + hh * 128
                                if j <= 4:
                                    P.mm(pn[:, o:o + 64], Mv(j - 1, i, hh), Nv(j - 1, i, hh))
                                P.mm(pn[:, o + 64:o + 128], Nv(j - 1, i, hh), Mv(j - 1, i, hh))
                        src = pn.with_ap(pn.ap.rearrange("p (i h a t) -> p i h a t", i=n2, h=2, a=2))
                        if j <= 4:
                            P.copy(NM[j % 2][:, i0:i0 + n2, :, :, :], src, eng="act")
                        else:
                            P.copy(NM[j % 2][:, i0:i0 + n2, :, 1, :], src[:, :, :, 1, :], eng="act")
                    for i0 in range(0, ng, 4):
                        n4 = min(4, ng - i0)
                        px = self.ps(64, n4 * 128)
                        for ii in range(n4):
                            i = i0 + ii
                            for hh in range(2):
                                P.mm(px[:, ii * 128 + hh * 64:ii * 128 + hh * 64 + 64], Mv(j, i, hh), XX[(j - 1) % 2][:, i, hh, :])
                        P.tt(XX[j % 2][:, i0:i0 + n4, :, :], px.with_ap(px.ap.rearrange("p (i h t) -> p i h t", i=n4, h=2)),
                             XX[(j - 1) % 2][:, i0:i0 + n4, :, :], ALU.add)
                TT = XX[5 % 2]
                def serial(units):
                    for i in units:
                        u = g0 + i
                        cu = slice(u * CW, (u + 1) * CW)
                        is_s = u >= NCH
                        if not is_s:
                            Hcur = self.Hst[:, l, p, u % 2, :]
                            Hnext = self.Hst[:, l, p, (u + 1) % 2, :]
                        else:
                            b = u - NCH
                            gb_ = s * SB + b
                            Hcur = Hs[:, b, 0, :]
                            Hnext = Hs[:, b, 1, :]
                        pp_ = (u % 2) if not is_s else (2 + u - NCH)
                        pw = self.ps(64, 128)
                        P.mm(pw, kk[:, cu], Hcur)
                        P.tt(Wsb[:, pp_, :], pw, AV[:, i, :], ALU.add)
                        yield
                        pu = self.ps(64, 128)
                        for hh in range(2):
                            P.mm(pu[:, hh * 64:(hh + 1) * 64], TT[:, i, hh, :], Wsb[:, pp_, hh * 64:(hh + 1) * 64])
                        ud = Upad[:, pp_, :, :]
                        ud_ap = Upad.base[0:64, pp_, :, :].rearrange("p h x -> p (h x)").rearrange("p (a b) -> p a b", a=4, b=64)[:, 0:4:3, :]
                        P.copy(ud.with_ap(ud_ap), pu.with_ap(pu.ap.rearrange("p (h b) -> p h b", h=2)), eng="act")
                        yield
                        py = self.ps(128, 64)
                        P.mm(py, Hcur, xr[:, cu], start=True, stop=False)
                        for hh in range(2):
                            P.mm(py, TOK[:, i, 0, hh, :], MT[:, i, hh, 1, 1, :], start=False, stop=False)
                        for hh in range(2):
                            P.mm(py, Upad[:, pp_, hh, :], MT[:, i, hh, 0, 1, :], start=False, stop=(hh == 1))
                        P.copy(yT[:, cu], py, eng="act")
                        ph2 = self.ps(128, 128)
                        for hh in range(2):
                            P.mm(ph2, TOK[:, i, 1, hh, :], TOK[:, i, 0, hh, :], start=(hh == 0), stop=False)
                        for hh in range(2):
                            P.mm(ph2, TOK[:, i, 2, hh, :], Upad[:, pp_, hh, :], start=False, stop=(hh == 1))
                        P.stt(Hnext, Hcur, gc[:, u:u + 1], ph2, ALU.mult, ALU.add)
                        yield
                        if is_s:
                            self.wkv_out(Hnext, HT, dr["wkvs"][l, gb_], p)
                            yield
                pu_ = [i for i in range(ng) if g0 + i < NCH]
                su_ = [i for i in range(ng) if g0 + i >= NCH]
                gens_ = ([serial(pu_)] if pu_ else []) + [serial([i]) for i in su_]
                run_rr(gens_)
            if last_seg:
                self.wkv_out(self.Hst[:, l, p, 0, :], HT, dr["wkvp"][l], p)

            def gnorm(ua, ub):
                ca, cb = ua * CW, ub * CW
                C = slice(ca, cb)
                n = cb - ca
                P.copy(tb[:, C], yT[:, C], eng="act"); yield
                P.act(kt[:, C], yT[:, C], AF.Square); yield
                p1 = self.ps(128, n)
                P.mm(p1, self.onesbd.full(), tb[:, C])
                p2 = self.ps(128, n)
                P.mm(p2, self.onesbd.full(), kt[:, C])
                P.ts(S1[:, C], p1, 1.0 / 64, ALU.mult); yield
                P.tt(S2[:, C], S1[:, C], S1[:, C], ALU.mult); yield
                P.stt(S2[:, C], p2, 1.0 / 64, S2[:, C], ALU.mult, ALU.subtract); yield
                P.ts(S2[:, C], S2[:, C], 0.0, ALU.max); yield
                P.act(S2[:, C], S2[:, C], AF.Ln, bias=self.epsc[:, 1:2], scale=1.0); yield
                P.act(S2[:, C], S2[:, C], AF.Exp, scale=-0.5); yield
                P.tt(yT[:, C], yT[:, C], S1[:, C], ALU.subtract); yield
                P.tt(yT[:, C], yT[:, C], S2[:, C], ALU.mult); yield
                P.ts(yT[:, C], yT[:, C], self.prm(l, "lnw", p), ALU.mult, self.prm(l, "lnb", p), ALU.add); yield
                P.tt(yT[:, C], yT[:, C], bonus[:, C], ALU.add); yield
            run_rr([gnorm(cuts[i_], cuts[i_ + 1]) for i_ in range(nstr)])
            P.tt(self.mixT[:, 4 + p, 0:TP], yT[:, 0:TP], gg[:, 0:TP], ALU.mult)
            ms = self.mixT[:, 4 + p, TP:T]
            P.tt(ms.with_ap(ms.ap.rearrange("p (b t) -> p b t", t=4)), v3(yT[:, TP:TW], CW)[:, :, 0:4], v3(gg[:, TP:TW], CW)[:, :, 0:4], ALU.mult)
            A.release(m1)

        for c in range(8):
            P.dma("sp", dr["shs"][l, s * SB:(s + 1) * SB, c * 128:(c + 1) * 128].rearrange("b p -> p b"), sho[:, c, :], chan="outs",
                  allow_slow_non_contiguous=True)
        if last_seg:
            P.dma("sp", dr["shp"][l].rearrange("(c p) -> p c", p=128), self.shcar[:, l, :], chan="outs", allow_slow_non_contiguous=True)

    def wkv_out(self, H, HT, dst, p):
        P = self.P
        pt = self.ps(128, 128)
        P.transpose(pt, H, self.identf.full())
        P.copy(HT.full(), pt)
        for hh in range(2):
            P.dma("sp", dst[2 * p + hh], HT[hh * 64:(hh + 1) * 64, hh * 64:(hh + 1) * 64], chan="outs")

WNAMES = [("norm_mix_pre", [D]), ("norm_mix_post", [D]), ("norm_ffn_pre", [D]), ("norm_ffn_post", [D]),
          ("w_in", [D, NIN]), ("b_in", [NIN]), ("attn_sinks", [8]), ("rwkv_mu", [1024]), ("rwkv_w0", [256]),
          ("rwkv_w_up", [64, 256]), ("rwkv_a0", [256]), ("rwkv_a_up", [64, 256]), ("rwkv_g_up", [128, 256]),
          ("rwkv_k_k", [256]), ("rwkv_k_a", [256]), ("rwkv_r_k", [4, 64]), ("rwkv_ln_w", [256]), ("rwkv_ln_b", [256]),
          ("lru_conv_w", [4, 256]), ("lru_conv_b", [256]), ("lru_w_a", [4, 64, 64]), ("lru_b_a", [256]),
          ("lru_w_i", [4, 64, 64]), ("lru_b_i", [256]), ("lru_L", [256]), ("w_out", [D, D]), ("b_out", [D]),
          ("ffn_w_gate", [D, DFF]), ("ffn_w_up", [D, DFF]), ("ffn_w_down", [DFF, D]), ("ple_w", [PLE, D]),
          ("ple_gate_w", [D, D])]


def build(cfg):
    nc = bass.Bass("TRN2", target_bir_lowering=False)
    dr = {}
    L, SEQ, SBC = cfg.L, cfg.SEQ, cfg.SBC

    def inp(name, shape):
        dr[name] = nc.dram_tensor(name, list(shape), F32, kind="ExternalInput").ap()

    def outp(name, shape):
        dr[name] = nc.dram_tensor(name, list(shape), F32, kind="ExternalOutput").ap()
    inp("xp", [SEQ, D]); inp("xs", [SBC * 4, D]); inp("ck", [L, SBC, 128, 128]); inp("cv", [L, SBC, 128, 128])
    inp("st_shift", [L, SBC, 1024]); inp("st_wkv", [L, SBC, 4, 64, 64]); inp("st_conv", [L, SBC, 3, 256])
    inp("st_lru", [L, SBC, 256]); inp("pp", [L, SEQ, PLE]); inp("psm", [L, SBC * 4, PLE])
    for n, sh in WNAMES:
        inp(n, [L] + sh)
    outp("yp", [SEQ, D]); outp("ys", [SBC * 4, D]); outp("kp", [L, 128, 128]); outp("vp", [L, 128, 128])
    outp("shp", [L, 1024]); outp("wkvp", [L, 4, 64, 64]); outp("convp", [L, 3, 256]); outp("lrup", [L, 256])
    outp("ks", [L, SBC, 128, 128]); outp("vs", [L, SBC, 128, 128]); outp("shs", [L, SBC, 1024])
    outp("wkvs", [L, SBC, 4, 64, 64]); outp("convs", [L, SBC, 3, 256]); outp("lrus", [L, SBC, 256])
    P = Prog(nc)
    K = Kern(P, cfg, dr)
    K.setup()
    ids = [[K.register_layer(l) for l in range(L)] for s in range(cfg.NSEG)]
    for s in range(cfg.NSEG):
        K.load_segment(s)
        for l in range(L):
            K.layer(l, s, ids[s][l])
        K.store_segment(s)
    P.emit()
    return nc, P, K


_CACHE = {}


def run(cfg, inputs):
    key = (cfg.L, cfg.SEQ, cfg.NSEG, cfg.SBC)
    if key not in _CACHE:
        _CACHE[key] = build(cfg)[0]
    nc = _CACHE[key]
    L, SBC = cfg.L, cfg.SBC
    f = lambda a: np.ascontiguousarray(np.asarray(a, dtype=np.float32))
    in_maps = []
    for i in range(8):
        b = i % 4
        sl = slice(i * SBC, (i + 1) * SBC)
        m = {"xp": f(inputs["x_prompt"][b]), "xs": f(inputs["x_sample"][sl]).reshape(SBC * 4, D),
             "ck": f(inputs["cache_k"][:, sl]).reshape(L, SBC, 128, 128), "cv": f(inputs["cache_v"][:, sl]).reshape(L, SBC, 128, 128),
             "st_shift": f(inputs["state_shift"][:, sl]), "st_wkv": f(inputs["state_wkv"][:, sl]),
             "st_conv": f(inputs["state_conv"][:, sl]), "st_lru": f(inputs["state_lru"][:, sl]),
             "pp": f(inputs["p_prompt"][:, b]), "psm": f(inputs["p_sample"][:, sl]).reshape(L, SBC * 4, PLE)}
        for n, sh in WNAMES:
            m[n] = f(inputs[n])
        in_maps.append(m)
    res = run_bass_kernel_spmd(nc, in_maps, core_ids=list(range(8)))
    R = res.results
    DECB = 8 * SBC
    cat_p = lambda k: np.stack([R[i][k] for i in range(4)], axis=0)
    cat_pl = lambda k: np.stack([R[i][k] for i in range(4)], axis=1)
    cat_s = lambda k: np.concatenate([R[i][k] for i in range(8)], axis=1)
    yp = cat_p("yp")
    ys = np.concatenate([R[i]["ys"] for i in range(8)], axis=0).reshape(DECB, 4, D)
    outs = (yp, ys, cat_pl("kp").reshape(L, 4, 128, 2, 64), cat_pl("vp").reshape(L, 4, 128, 2, 64), cat_pl("shp"), cat_pl("wkvp"),
            cat_pl("convp"), cat_pl("lrup"), cat_s("ks").reshape(L, DECB, 128, 2, 64), cat_s("vs").reshape(L, DECB, 128, 2, 64),
            cat_s("shs"), cat_s("wkvs"), cat_s("convs"), cat_s("lrus"))
    return tuple(np.ascontiguousarray(o, dtype=np.float32) for o in outs)


def kernel(**inputs):
    cfg = Cfg(L=4, SEQ=4096, NSEG=8, DECB=128)
    return run(cfg, inputs)
```
